# Optimizing a Trainium2 kernel written in Bass

```python
import math
import jax, jax.numpy as jnp
from jax import lax
import numpy as np

D_MODEL = 1024
BATCH = 8
SEQ = 4096
DEPTH = 1

DA_HEADS = 4
DA_V_DIM = D_MODEL // (2 * DA_HEADS)
DA_HEAD_DIM = DA_V_DIM // 2
A_QK = DA_HEADS * 2 * DA_HEAD_DIM
A_V = DA_HEADS * DA_V_DIM
Q_BLOCK = 128

GLA_HEADS = 4
GLA_DV = D_MODEL // (2 * GLA_HEADS)
GLA_DK = GLA_DV // 2
B_QK = GLA_HEADS * GLA_DK
B_V = GLA_HEADS * GLA_DV
GLA_GATE_RANK = 16
GLA_TAU = 16.0
GLA_CHUNK = 64

RMS_EPS = 1e-6

SPLITS = (A_QK, A_QK, A_V, A_V,
          B_QK, B_QK, B_V, B_V, GLA_GATE_RANK,
          D_MODEL, D_MODEL)
D_IN = sum(SPLITS)

kernel_name = 'hybrid_diffattn_gla_gated_merge'


def rms_norm(x, g):
    xf = x.astype(jnp.float32)
    y = xf * lax.rsqrt(jnp.mean(xf * xf, axis=-1, keepdims=True) + RMS_EPS)
    return (y * g.astype(jnp.float32)).astype(x.dtype)


def diff_attention(q, k, v, lam, slopes):
    B, H, _, S, DH = q.shape
    nb = S // Q_BLOCK
    scale = DH ** -0.5
    qb = q.reshape(B, H, 2, nb, Q_BLOCK, DH).transpose(3, 0, 1, 2, 4, 5)
    kpos = jnp.arange(S)

    def block(args):
        qi, i = args
        s = jnp.einsum('bhmqd,bhmkd->bhmqk', qi, k).astype(jnp.float32) * scale
        qpos = i * Q_BLOCK + jnp.arange(Q_BLOCK)
        dist = (qpos[:, None] - kpos[None, :]).astype(jnp.float32)
        bias = -slopes[:, None, None, None] * dist
        s = jnp.where(dist >= 0, s + bias, -jnp.inf)
        p = jax.nn.softmax(s, axis=-1)
        pd = p[:, :, 0] - lam * p[:, :, 1]
        return jnp.einsum('bhqk,bhkd->bhqd', pd.astype(v.dtype), v)

    o = lax.map(block, (qb, jnp.arange(nb)))
    return o.transpose(1, 0, 3, 2, 4).reshape(B, S, H, v.shape[-1])


def gla_chunked(q, k, v, glog):
    B, H, S, DK = q.shape
    DV = v.shape[-1]
    n = S // GLA_CHUNK

    def to_chunks(t):
        return t.reshape(B, H, n, GLA_CHUNK, t.shape[-1]).transpose(2, 0, 1, 3, 4)

    tri = jnp.tril(jnp.ones((GLA_CHUNK, GLA_CHUNK), dtype=bool))

    def step(state, inp):
        qc, kc, vc, gc = inp
        b = jnp.cumsum(gc, axis=-2)
        o_inter = jnp.einsum('bhtk,bhkv->bhtv', qc * jnp.exp(b), state)
        rel = jnp.where(tri[:, :, None], b[:, :, :, None, :] - b[:, :, None, :, :], -jnp.inf)
        a = jnp.einsum('bhtk,bhsk,bhtsk->bhts', qc, kc, jnp.exp(rel))
        o_intra = jnp.einsum('bhts,bhsv->bhtv', a, vc)
        b_last = b[:, :, -1:, :]
        state = (jnp.exp(b_last[:, :, 0, :, None]) * state
                 + jnp.einsum('bhsk,bhsv->bhkv', kc * jnp.exp(b_last - b), vc))
        return state, o_inter + o_intra

    s0 = jnp.zeros((B, H, DK, DV), jnp.float32)
    _, o = lax.scan(step, s0, (to_chunks(q), to_chunks(k), to_chunks(v), to_chunks(glog)))
    return o.transpose(1, 0, 3, 2, 4).reshape(B, S, H, DV)


def hybrid_layer(x, g_pre, w_in, lam_q1, lam_k1, lam_q2, lam_k2, g_sub_a,
                 w_alpha, b_alpha, g_sub_b, w_up_a, w_up_b, w_out, g_post, layer_idx):
    B, S, _ = x.shape
    f32 = jnp.float32
    h = rms_norm(x, g_pre)
    proj = h @ w_in
    idx = np.cumsum(np.array(SPLITS))[:-1].tolist()
    qa, ka, va, za, qb, kb, vb, zb, lr, gate_a, gate_b = jnp.split(proj, idx, axis=-1)

    lam_init = 0.8 - 0.6 * math.exp(-0.3 * layer_idx)
    lam = (jnp.exp(jnp.sum(lam_q1.astype(f32) * lam_k1.astype(f32)))
           - jnp.exp(jnp.sum(lam_q2.astype(f32) * lam_k2.astype(f32))) + lam_init)
    slopes = jnp.exp2(-8.0 * jnp.arange(1, DA_HEADS + 1, dtype=f32) / DA_HEADS)
    qa = qa.reshape(B, S, DA_HEADS, 2, DA_HEAD_DIM).transpose(0, 2, 3, 1, 4)
    ka = ka.reshape(B, S, DA_HEADS, 2, DA_HEAD_DIM).transpose(0, 2, 3, 1, 4)
    va = va.reshape(B, S, DA_HEADS, DA_V_DIM).transpose(0, 2, 1, 3)
    oa = diff_attention(qa, ka, va, lam, slopes)
    oa = rms_norm(oa, g_sub_a) * (1.0 - lam_init)
    ya = (oa.reshape(B, S, A_V) * jax.nn.silu(za)) @ w_up_a

    glog = jax.nn.log_sigmoid((lr @ w_alpha + b_alpha).astype(f32)) / GLA_TAU
    qb = qb.reshape(B, S, GLA_HEADS, GLA_DK).transpose(0, 2, 1, 3).astype(f32) * (GLA_DK ** -0.5)
    kb = kb.reshape(B, S, GLA_HEADS, GLA_DK).transpose(0, 2, 1, 3).astype(f32)
    vb = vb.reshape(B, S, GLA_HEADS, GLA_DV).transpose(0, 2, 1, 3).astype(f32)
    glog = glog.reshape(B, S, GLA_HEADS, GLA_DK).transpose(0, 2, 1, 3)
    ob = gla_chunked(qb, kb, vb, glog).astype(x.dtype)
    ob = rms_norm(ob, g_sub_b)
    yb = (ob.reshape(B, S, B_V) * jax.nn.silu(zb)) @ w_up_b

    y = jax.nn.sigmoid(gate_a) * ya + jax.nn.sigmoid(gate_b) * yb
    return x + rms_norm(y @ w_out, g_post)


def setup_inputs(seed: int = 0) -> dict:
    key = jax.random.key(seed)
    ks = jax.random.split(key, 16)
    L = DEPTH
    nrm = jax.random.normal
    f32 = jnp.float32
    return {
        'x': nrm(ks[0], (BATCH, SEQ, D_MODEL), f32),
        'g_pre': 1.0 + 0.05 * nrm(ks[1], (L, D_MODEL), f32),
        'w_in': nrm(ks[2], (L, D_MODEL, D_IN), f32) * D_MODEL ** -0.5,
        'lam_q1': 0.1 * nrm(ks[3], (L, DA_HEAD_DIM), f32),
        'lam_k1': 0.1 * nrm(ks[4], (L, DA_HEAD_DIM), f32),
        'lam_q2': 0.1 * nrm(ks[5], (L, DA_HEAD_DIM), f32),
        'lam_k2': 0.1 * nrm(ks[6], (L, DA_HEAD_DIM), f32),
        'g_sub_a': 1.0 + 0.05 * nrm(ks[7], (L, DA_V_DIM), f32),
        'w_alpha': nrm(ks[8], (L, GLA_GATE_RANK, B_QK), f32) * GLA_GATE_RANK ** -0.5,
        'b_alpha': 0.1 * nrm(ks[9], (L, B_QK), f32),
        'g_sub_b': 1.0 + 0.05 * nrm(ks[10], (L, GLA_DV), f32),
        'w_up_a': nrm(ks[11], (L, A_V, D_MODEL), f32) * A_V ** -0.5,
        'w_up_b': nrm(ks[12], (L, B_V, D_MODEL), f32) * B_V ** -0.5,
        'w_out': nrm(ks[13], (L, D_MODEL, D_MODEL), f32) * D_MODEL ** -0.5,
        'g_post': 1.0 + 0.05 * nrm(ks[14], (L, D_MODEL), f32),
    }


def reference(x, g_pre, w_in, lam_q1, lam_k1, lam_q2, lam_k2, g_sub_a, w_alpha, b_alpha,
              g_sub_b, w_up_a, w_up_b, w_out, g_post):
    for i in range(DEPTH):
        x = hybrid_layer(x, g_pre[i], w_in[i], lam_q1[i], lam_k1[i], lam_q2[i], lam_k2[i],
                         g_sub_a[i], w_alpha[i], b_alpha[i], g_sub_b[i], w_up_a[i], w_up_b[i],
                         w_out[i], g_post[i], i)
    return x
```

```python
import math
from contextlib import ExitStack

import numpy as np
import ml_dtypes

import concourse.bass as bass
import concourse.mybir as mybir
from concourse.bass_utils import run_bass_kernel_spmd

F32 = mybir.dt.float32
BF16 = mybir.dt.bfloat16
AF = mybir.ActivationFunctionType
ALU = mybir.AluOpType
AX = mybir.AxisListType

S = 4096
D = 1024
T = 512
NT = S // T
DIN = 5648
EPS = 1e-6
LAM_INIT = 0.8 - 0.6 * math.exp(-0.3 * 0)
SLOPES = [2.0 ** (-8.0 * (h + 1) / 4) for h in range(4)]
C_QA, C_KA, C_VA, C_ZA, C_QB, C_KB, C_VB, C_ZB, C_LR, C_GA, C_GB = 0, 512, 1024, 1536, 2048, 2304, 2560, 3072, 3584, 3600, 4624
CHUNK_COL = [C_QA, C_KA, C_VA, C_ZA, C_QB, C_VB, C_ZB, C_GA, C_GA + 512, C_GB, C_GB + 512]
CH_QA, CH_KA, CH_VA, CH_ZA, CH_QK, CH_VB, CH_ZB, CH_GA0, CH_GA1, CH_GB0, CH_GB1, CH_WO0, CH_WO1 = range(13)
NEG = -30000.0
NSLOT = 4

ENGS = ("pe", "act", "dve", "pool", "sp")


class _Op:
    __slots__ = ("eng", "fn", "deps", "is_dma", "sem", "val", "needs_inc")

    def __init__(self, eng, fn, is_dma):
        self.eng = eng
        self.fn = fn
        self.deps = []
        self.is_dma = is_dma
        self.sem = None
        self.val = 0
        self.needs_inc = is_dma


class _Prog:
    def __init__(self, same_eng_sync=("act", "dve", "pool")):
        self.ops = []
        self.last_writer = {}
        self.readers = {}
        self.same_eng_sync = set(same_eng_sync)
        self.dma_slots = []

    def op(self, eng, fn, reads=(), writes=(), dma=None):
        o = _Op(eng, fn, dma is not None)
        deps = {}
        for k in reads:
            w = self.last_writer.get(k)
            if w is not None:
                deps[id(w)] = w
        for k in writes:
            w = self.last_writer.get(k)
            if w is not None:
                deps[id(w)] = w
            for r in self.readers.get(k, ()):
                deps[id(r)] = r
        o.deps = list(deps.values())
        for k in reads:
            self.readers.setdefault(k, []).append(o)
        for k in writes:
            self.last_writer[k] = o
            self.readers[k] = []
        if dma is not None:
            o.sem = dma
            if dma not in self.dma_slots:
                self.dma_slots.append(dma)
        self.ops.append(o)
        return o

    def emit(self, block_engines, sems):
        ops = self.ops
        for o in ops:
            for d in o.deps:
                if d.is_dma:
                    continue
                if d.eng != o.eng or (d.eng in self.same_eng_sync):
                    d.needs_inc = True
        cnt = {e: 0 for e in ENGS}
        dcnt = {}
        for o in ops:
            if o.is_dma:
                slot = o.sem
                dcnt[slot] = dcnt.get(slot, 0) + 16
                o.sem = sems[slot]
                o.val = dcnt[slot]
            elif o.needs_inc:
                cnt[o.eng] += 1
                o.sem = sems[o.eng]
                o.val = cnt[o.eng]
        same = self.same_eng_sync

        def make(eng_name):
            my_ops = [o for o in ops if o.eng == eng_name]

            def body(e):
                waited = {}
                for o in my_ops:
                    need = {}
                    for d in o.deps:
                        if (not d.is_dma) and d.eng == eng_name and eng_name not in same:
                            continue
                        key = id(d.sem)
                        if d.val > need.get(key, (None, 0))[1]:
                            need[key] = (d.sem, d.val)
                    for key, (s, v) in need.items():
                        if waited.get(key, 0) >= v:
                            continue
                        e.wait_ge(s, v)
                        waited[key] = v
                    ins = o.fn(e)
                    if o.needs_inc and ins is not None:
                        ins.then_inc(o.sem, 16 if o.is_dma else 1)

            return body

        for eng_name, deco in block_engines.items():
            deco(make(eng_name))


def _build(ntiles=NT, debug=False, stop_after=None):
    nc = bass.Bass("TRN2", target_bir_lowering=False)

    def din(name, shape, dt=F32):
        return nc.dram_tensor(name, shape, dt, kind="ExternalInput").ap()

    x = din("x", [S, D])
    w_in = din("w_in", [D, DIN])
    w_up_a = din("w_up_a", [512, D])
    w_up_b = din("w_up_b", [512, D])
    w_out = din("w_out", [D, D])
    w_alpha = din("w_alpha", [16, 256])
    smalls_d = din("smalls", [128, 12])
    lamv_d = din("lamv", [128, 256])
    gpost_d = din("gpost", [128, D])
    ident_d = din("ident", [128, 128], BF16)
    tri_d = din("tri4", [128, 512], BF16)
    negm_d = din("negmask", [128, 128], BF16)
    btab_d = din("biastab", [128, 128])
    ef_d = din("eftab", [128, 4])
    rmask_d = din("resetmask", [128, 512], BF16)
    out = nc.dram_tensor("out", [S, D], F32, kind="ExternalOutput").ap()
    wsc = nc.dram_tensor("wsc", [13, 128, 4096], BF16).ap()
    dbg = {}
    if debug:
        for nm, shp in (("d_hT", [128, 8, 512]), ("d_QT", [128, 4, 512]), ("d_oacc", [128, 4, 4, 128]),
                        ("d_ob", [128, 4, 512]), ("d_yT", [128, 8, 512]), ("d_gaT", [128, 4, 512]),
                        ("d_gbT", [128, 4, 512]),
                        ("d_eb", [128, 2, 512]), ("d_state", [128, 2, 128])):
            dbg[nm] = nc.dram_tensor(nm, shp, F32, kind="ExternalOutput").ap()

    P = _Prog()
    with ExitStack() as es:
        def SB(name, shape, dt):
            return es.enter_context(nc.sbuf_tensor("sb_" + name, shape, dt))

        def PSM(name, shape, dt):
            return es.enter_context(nc.psum_tensor("ps_" + name, shape, dt))

        KT = SB("KT", [128, 4, S], BF16)
        V = SB("V", [128, 4 * ntiles, 4, 130], BF16)
        wupA = SB("wupA", [128, 4, 1024], BF16)
        wupB = SB("wupB", [128, 4, 1024], BF16)
        wbuf = [SB(f"wbuf{i}", [128, 4096], BF16) for i in range(NSLOT)]
        wlr = SB("wlr", [128, 8, 16], BF16)
        walpha = SB("walpha", [16, 256], BF16)
        xn = [SB(f"xn{i}", [128, 1024], F32) for i in range(2)]
        xr = xn
        xa = [SB(f"xa{i}", [128, 1024], F32) for i in range(2)]
        hb = [SB(f"hb{i}", [128, 1024], BF16) for i in range(2)]
        hT = SB("hT", [128, 8, T], BF16)
        gated = [hb[i][:, 0:512] for i in range(2)]
        obf = [hb[i][:, 512:1024] for i in range(2)]
        QT = SB("QT", [128, 4, T], BF16)
        yT = SB("yT", [128, 8, T], BF16)
        szA = yT[:, 0:4, :]
        szB = yT[:, 4:8, :]
        gbT = SB("gbT", [128, 4, T], BF16)
        gl = SB("gl", [128, 2, T], F32)
        sq = gl[:, 0, :]
        lamv = gl[:, 0, 0:256]
        lamt = gl[:, 0, 256:384]
        wlr_st = gl[:, 1, 0:128].rearrange("p (kc c) -> p kc c", kc=8)
        cum = SB("cum", [128, 2, T], F32)
        walpha_st = cum[0:16, 0, 0:256]
        enb = cum
        qtT = SB("qtT", [128, 2, 2, T], BF16)
        ktT = SB("ktT", [128, 2, T], BF16)
        khat = SB("khat", [128, 4, 2, 128], BF16)
        vb = SB("vb", [128, 4, 512], BF16)
        lrT = SB("lrT", [16, T], BF16)
        state = SB("state", [128, 2, 128], F32)
        stbf = SB("stbf", [128, 2, 128], BF16)
        PT2 = [SB(f"PT{i}", [128, 2, 512], BF16) for i in range(2)]
        gatedA = SB("gatedA", [128, 4, 512], BF16)
        khT = gatedA[:, 0:2, :]
        oaf = [SB(f"oaf{i}", [128, 128], F32) for i in range(4)]
        junkb = SB("junkb", [128, 128], BF16)
        otmp = [SB(f"otmp{i}", [128, 128], F32) for i in range(2)]
        gaT = QT
        eb = SB("eb", [128, 2, T], F32)
        tga = [eb[:, 0, :], eb[:, 1, :]]
        ttmp = tga
        ztmp = [PT2[i][:].rearrange("p a b -> p (a b)").bitcast(F32) for i in range(2)]
        eblast = SB("eblast", [128, 2, 4], F32)
        tgb = [SB("tgb0", [128, 512], F32)] * 2
        ATs = [SB("ATs0", [128, 512], BF16)] * 2
        ident = SB("ident", [128, 128], BF16)
        tri4 = SB("tri4", [128, 512], BF16)
        negm = SB("negm", [128, 128], BF16)
        btab = SB("btab", [128, 128], F32)
        eft = SB("eft", [128, 4], F32)
        rmask = SB("rmask", [128, 512], BF16)
        gpost = SB("gpost", [128, D], F32)
        smalls = SB("smalls", [128, 12], F32)
        lams = SB("lams", [128, 8], F32)
        nbal = SB("nbal", [128, 2], F32)
        ss = SB("ss", [128, 32], F32)
        rs = SB("rs", [128, 32], F32)
        rstd = SB("rstd", [128, 32], F32)
        ssa = SB("ssa", [128, 16], F32)
        rsa = SB("rsa", [128, 16], F32)
        rstda = SB("rstda", [128, 16], F32)
        ssb = SB("ssb", [128, 16], F32)
        rsb = SB("rsb", [128, 16], F32)
        rstdb = SB("rstdb", [128, 16], F32)
        ssz = SB("ssz", [128, 64], F32)
        rsz = SB("rsz", [128, 32], F32)
        rstdz = SB("rstdz", [128, 32], F32)
        rz = SB("rz", [128, 8], F32)
        dbgbuf = SB("dbgbuf", [128, 8, 512], F32) if debug else None
        dbgoa = SB("dbgoa", [128, 4, 4, 128], F32) if debug else None
        pg4 = PSM("pg4", [128, 4, 512], F32)
        pg = [pg4[:, i, :] for i in range(4)]
        ptr = PSM("ptr", [128, 8, 128], BF16)
        poa = [PSM(f"poa{i}", [128, 512], F32) for i in range(3)]

        dma_names = (["stg%d" % i for i in range(8)] + ["wst%d" % i for i in range(NSLOT)] + ["wld%d" % i for i in range(NSLOT)]
                     + ["cst", "xn0", "xn1", "xr0", "xr1", "xa0", "xa1", "st0", "st1", "dbg"])
        sems = {}
        for nm in list(ENGS) + dma_names:
            sems[nm] = es.enter_context(nc.semaphore("s_" + nm))
        _build.sbuf_left = nc.sbuf_bytes_remaining
        block = es.enter_context(nc.Block())

        gctr = [0]

        def alloc_pg():
            b = gctr[0] % 4
            gctr[0] += 1
            return b

        def wkeys(slot):
            return [f"w{slot}_{i}" for i in range(8)]

        def ev_engine(i):
            return "dve"

        for dst, src, key in ((ident, ident_d, "ident"), (tri4, tri_d, "tri4"), (negm, negm_d, "negm"),
                              (btab, btab_d, "btab"), (eft, ef_d, "eft"), (rmask, rmask_d, "rmask"), (gpost, gpost_d, "gpost"),
                              (smalls, smalls_d, "smalls"), (lamv, lamv_d, "gl0"), (walpha_st, w_alpha, "cum0")):
            P.op("sp", lambda e, dst=dst, src=src: e.dma_start(out=dst[:], in_=src), writes=[key], dma="cst")
        P.op("sp", lambda e: e.dma_start(out=wlr_st[:], in_=w_in[:, C_LR:C_LR + 16].rearrange("(kc p) c -> p kc c", p=128)),
             writes=["gl1"], dma="cst")
        _last_c = P.ops[-1]
        for _k in ("ident", "tri4", "negm", "btab", "eft", "rmask", "gpost", "smalls", "gl0", "cum0", "gl1"):
            P.last_writer[_k] = _last_c
        P.op("pool", lambda e: e.memset(lams[:, 5:6], -0.5), writes=["nhalf"])
        P.op("pool", lambda e: e.memset(V[:, :, :, 128:130], 1.0), writes=["Vones"])
        P.op("pool", lambda e: e.memset(state[:], 0.0), writes=["state0", "state1"])
        P.op("pool", lambda e: e.memset(qtT[:], 0.0), writes=["qtT0", "qtT1"])
        P.op("pool", lambda e: e.memset(stbf[:], 0.0), writes=["stbf0", "stbf1"])
        P.op("dve", lambda e: e.tensor_tensor(out=lamt[:, 0:64], in0=lamv[:, 0:64], in1=lamv[:, 64:128], op=ALU.mult),
             reads=[], writes=["gl0"])
        P.op("dve", lambda e: e.tensor_tensor(out=lamt[:, 64:128], in0=lamv[:, 128:192], in1=lamv[:, 192:256], op=ALU.mult),
             reads=[], writes=["gl0"])
        P.op("dve", lambda e: e.reduce_sum(out=lams[:, 0:2], in_=lamt[:].rearrange("p (a b) -> p a b", a=2), axis=AX.X),
             reads=["gl0"], writes=["lams01"])
        P.op("act", lambda e: e.activation(out=lams[:, 2:4], in_=lams[:, 0:2], func=AF.Exp), reads=["lams01"], writes=["lams23"])
        P.op("dve", lambda e: e.tensor_tensor(out=lams[:, 4:5], in0=lams[:, 3:4], in1=lams[:, 2:3], op=ALU.subtract),
             reads=["lams23"], writes=["nlam"])
        P.op("dve", lambda e: e.tensor_scalar(out=lams[:, 4:5], in0=lams[:, 4:5], scalar1=-LAM_INIT, scalar2=None, op0=ALU.add),
             reads=["nlam"], writes=["nlam"])
        P.op("dve", lambda e: e.tensor_scalar(out=nbal[:], in0=smalls[:, 10:12], scalar1=-1.0, scalar2=None, op0=ALU.mult),
             reads=["smalls"], writes=["nbal"])
        P.op("dve", lambda e: e.tensor_copy(out=walpha[:], in_=walpha_st[:]), reads=["cum0"], writes=["walpha"])
        for kc in range(8):
            P.op("dve", lambda e, kc=kc: e.tensor_scalar(out=wlr[:, kc, :], in0=wlr_st[:, kc, :], scalar1=smalls[:, kc:kc + 1],
                                                        scalar2=None, op0=ALU.mult),
                 reads=["gl1", "smalls"], writes=["wlr"])

        stgK = [KT[:, i // 2, 2048 + (i % 2) * 1024:2048 + (i % 2 + 1) * 1024].bitcast(F32) for i in range(8)]
        stg_keys = [f"stgK{i}" for i in range(8)]
        pctr = [0]

        def conv_piece(src_ap, dst_ap, dst_keys, scale_ap, scale_c):
            i = pctr[0] % 8
            pctr[0] += 1
            sap = stgK[i]
            P.op("sp", lambda e: e.dma_start(out=sap, in_=src_ap), writes=[stg_keys[i]], dma=f"stg{i}")
            if scale_ap is None:
                if i % 2 == 0:
                    P.op("dve", lambda e: e.tensor_scalar(out=dst_ap, in0=sap, scalar1=scale_c, scalar2=None, op0=ALU.mult),
                         reads=[stg_keys[i]], writes=dst_keys)
                else:
                    P.op("act", lambda e: e.activation(out=dst_ap, in_=sap, func=AF.Copy, scale=scale_c),
                         reads=[stg_keys[i]], writes=dst_keys)
            elif i % 2 == 0 or scale_c != 1.0:
                P.op("dve", lambda e: e.tensor_scalar(out=dst_ap, in0=sap, scalar1=scale_ap, scalar2=scale_c,
                                                      op0=ALU.mult, op1=ALU.mult),
                     reads=[stg_keys[i], "smalls"], writes=dst_keys)
            else:
                P.op("act", lambda e: e.activation(out=dst_ap, in_=sap, func=AF.Copy, scale=scale_ap),
                     reads=[stg_keys[i], "smalls"], writes=dst_keys)

        def conv_chunk(c, slot):
            if c < 11:
                for kc in range(8):
                    conv_piece(w_in[kc * 128:(kc + 1) * 128, CHUNK_COL[c]:CHUNK_COL[c] + 512],
                               wbuf[slot][:, kc * 512:(kc + 1) * 512], [f"w{slot}_{kc}"], smalls[:, kc:kc + 1], 1.0)
            else:
                half = c - 11
                for kk in range(4):
                    kc = half * 4 + kk
                    for hh in range(2):
                        conv_piece(w_out[kc * 128:(kc + 1) * 128, hh * 512:(hh + 1) * 512],
                                   wbuf[slot][:, kk * 1024 + hh * 512: kk * 1024 + (hh + 1) * 512], [f"w{slot}_{kk * 2 + hh}"],
                                   None, 0.5)
            P.op("pool", lambda e: e.dma_start(out=wsc[c], in_=wbuf[slot][:]), reads=wkeys(slot),
                 writes=[f"wsc{c}"], dma=f"wst{slot}")

        def conv_up():
            for kc in range(4):
                for hh in range(2):
                    conv_piece(w_up_a[kc * 128:(kc + 1) * 128, hh * 512:(hh + 1) * 512], wupA[:, kc, hh * 512:(hh + 1) * 512],
                               ["wupA"], smalls[:, 8:9], (1.0 - LAM_INIT) * 0.5)
                    conv_piece(w_up_b[kc * 128:(kc + 1) * 128, hh * 512:(hh + 1) * 512], wupB[:, kc, hh * 512:(hh + 1) * 512],
                               ["wupB"], smalls[:, 9:10], 0.5)

        TILE_SEQ = [CH_VB, CH_ZB, CH_QK, CH_QA, CH_KA, CH_VA, CH_ZA, CH_GA0, CH_GB0, CH_GA1, CH_GB1, CH_WO0, CH_WO1]
        wseq = TILE_SEQ * ntiles
        wstate = {"loaded": 0, "acq": 0}
        slot_base = 0

        def emit_load():
            i = wstate["loaded"]
            if i >= len(wseq):
                return
            c = wseq[i]
            slot = (slot_base + i) % NSLOT
            if i < len(TILE_SEQ):
                conv_chunk(c, slot)
            else:
                P.op("sp", lambda e: e.dma_start(out=wbuf[slot][:], in_=wsc[c]), reads=[f"wsc{c}"], writes=wkeys(slot),
                     dma=f"wld{slot}")
            wstate["loaded"] += 1

        def acquire(c):
            i = wstate["acq"]
            assert wseq[i] == c, (wseq[i], c)
            assert i < wstate["loaded"]
            wstate["acq"] += 1
            return (slot_base + i) % NSLOT

        def release():
            emit_load()

        def xa_buf(s_):
            return xa[s_ % 2], [f"xa{s_ % 2}"]

        def load_xa(t_, s_):
            buf, keys = xa_buf(s_)
            g = 4 * t_ + s_
            P.op("sp", lambda e: e.dma_start(out=buf[:], in_=x[g * 128:(g + 1) * 128, :]), writes=keys, dma=f"xa{s_ % 2}")

        def load_xr(g):
            i = g % 2
            P.op("sp", lambda e: e.dma_start(out=xr[i][:], in_=x[g * 128:(g + 1) * 128, :]),
                 writes=[f"xn{i}", f"xn{i}b"], dma=f"xr{i}")

        HT_KEYS = ["hT0", "hT1", "hT2", "hT3"]

        def fm_matmuls(slot, j, rows=128, lhs_from=None):
            b = alloc_pg()
            for kc in range(8):
                if lhs_from is None:
                    lhsT = wbuf[slot][:, kc * 512 + j * 128: kc * 512 + j * 128 + rows]
                    rk = [f"w{slot}_{kc}"]
                else:
                    lhsT = lhs_from[:, kc, :]
                    rk = ["wlr"]
                P.op("pe", lambda e, lhsT=lhsT, kc=kc: e.matmul(out=pg[b][0:rows, :], lhsT=lhsT, rhs=hT[:, kc, :],
                                                                start=(kc == 0), stop=(kc == 7)),
                     reads=rk + HT_KEYS, writes=[f"pg{b}"])
            return b

        def tm_matmuls(slot, s):
            b = alloc_pg()
            for kc in range(8):
                P.op("pe", lambda e, kc=kc: e.matmul(out=pg[b][:, :], lhsT=hT[:, kc, s * 128:(s + 1) * 128],
                                                     rhs=wbuf[slot][:, kc * 512:(kc + 1) * 512],
                                                     start=(kc == 0), stop=(kc == 7)),
                     reads=[f"w{slot}_{kc}", f"hT{s}"], writes=[f"pg{b}"])
            return b

        def dump(name, src_ap, rkeys, shape):
            if not debug:
                return
            n = 1
            for d_ in shape[1:]:
                n *= d_
            view = dbgbuf[:].rearrange("p a b -> p (a b)")[:, 0:n]
            if len(shape) == 3:
                view = view.rearrange("p (a b) -> p a b", a=shape[1])
            elif len(shape) == 4:
                view = view.rearrange("p (a b c) -> p a b c", a=shape[1], b=shape[2])
            P.op("dve", lambda e: e.tensor_copy(out=view, in_=src_ap), reads=rkeys, writes=["dbgbuf"])
            P.op("sp", lambda e: e.dma_start(out=dbg[name], in_=view), reads=["dbgbuf"], dma="dbg")

        def pow_cols(dst, src, col0, n, rkeys, wkey):
            for q_ in range(n):
                P.op("pool", lambda e, q_=q_: e.tensor_tensor(out=dst[:, col0 + q_:col0 + q_ + 1], in0=src[:, col0 + q_:col0 + q_ + 1],
                                                             in1=lams[:, 5:6], op=ALU.pow),
                     reads=rkeys + ["nhalf"], writes=[f"{wkey}_{q_}"])

        def a_stats(t, s):
            junk = PT2[0][:].rearrange("p a b -> p (a b)")
            g = 4 * t + s
            xbuf, xkeys = xa_buf(s)
            P.op("act", lambda e: e.activation(out=junk, in_=xbuf[:], func=AF.Square, accum_out=ss[:, g:g + 1]),
                 reads=xkeys, writes=["PT0", f"ss{g}"])
            P.op("dve", lambda e: e.tensor_scalar(out=rs[:, g:g + 1], in0=ss[:, g:g + 1], scalar1=1.0 / D, scalar2=EPS,
                                                  op0=ALU.mult, op1=ALU.add), reads=[f"ss{g}"], writes=[f"rs{g}"])
            P.op("pool", lambda e: e.tensor_tensor(out=rstd[:, g:g + 1], in0=rs[:, g:g + 1], in1=lams[:, 5:6], op=ALU.pow),
                 reads=[f"rs{g}", "nhalf"], writes=[f"rstd{g}"])

        def a_norm_tr(t, s):
            g = 4 * t + s
            i = g % 2
            xbuf, xkeys = xa_buf(s)
            P.op("dve", lambda e: e.tensor_scalar(out=hb[i][:], in0=xbuf[:], scalar1=rstd[:, g:g + 1], scalar2=None, op0=ALU.mult),
                 reads=xkeys + [f"rstd{g}"], writes=[f"hbL{i}", f"hbR{i}"])
            if s + 2 < 4:
                load_xa(t, s + 2)
            for kc in range(8):
                P.op("pe", lambda e, kc=kc: e.transpose(out=ptr[:, kc, :], in_=hb[i][:, kc * 128:(kc + 1) * 128], identity=ident[:]),
                     reads=[f"hbL{i}", f"hbR{i}", "ident"], writes=["ptr"])
            if s % 2 == 0:
                P.op("dve", lambda e: e.tensor_copy(out=hT[:, :, s * 128:(s + 1) * 128], in_=ptr[:, :, :]),
                     reads=["ptr"], writes=[f"hT{s}"])
            else:
                P.op("act", lambda e: e.activation(out=hT[:, :, s * 128:(s + 1) * 128], in_=ptr[:, :, :], func=AF.Copy),
                     reads=["ptr"], writes=[f"hT{s}"])

        def phase_a(t, subs):
            for s in subs:
                a_stats(t, s)
            for s in subs:
                a_norm_tr(t, s)
            if debug and t == 0 and 3 in subs:
                dump("d_hT", hT[:], HT_KEYS, [128, 8, 512])

        def phase_b1(t):
            b = fm_matmuls(None, 0, rows=16, lhs_from=wlr)
            P.op("dve", lambda e, b=b: e.tensor_copy(out=lrT[:, :], in_=pg[b][0:16, :]), reads=[f"pg{b}"], writes=["lrT"])
            for c in range(2):
                b = alloc_pg()
                P.op("pe", lambda e, c=c, b=b: e.matmul(out=pg[b][:, :], lhsT=walpha[:, c * 128:(c + 1) * 128], rhs=lrT[:, :],
                                                        start=True, stop=True), reads=["walpha", "lrT"], writes=[f"pg{b}"])
                P.op("act", lambda e, c=c, b=b: e.activation(out=gl[:, c, :], in_=pg[b][:, :], func=AF.Exp, scale=-1.0,
                                                             bias=nbal[:, c:c + 1]), reads=[f"pg{b}", "nbal"], writes=[f"gl{c}"])
            for c in range(2):
                P.op("act", lambda e, c=c: e.activation(out=gl[:, c, :], in_=gl[:, c, :], func=AF.Ln, bias=1.0, scale=1.0),
                     reads=[f"gl{c}"], writes=[f"gl{c}"])
                P.op("dve", lambda e, c=c: e.tensor_tensor_scan(out=cum[:, c, :], data0=rmask[:, :], data1=gl[:, c, :], initial=0.0,
                                                                op0=ALU.mult, op1=ALU.add),
                     reads=[f"gl{c}", "rmask"], writes=[f"cum{c}"])
            for c in range(2):
                P.op("act", lambda e, c=c: e.activation(out=eb[:, c, :], in_=cum[:, c, :], func=AF.Exp, scale=-1.0 / 16.0),
                     reads=[f"cum{c}"], writes=[f"tga{c}"])
                P.op("act", lambda e, c=c: e.activation(out=enb[:, c, :], in_=cum[:, c, :], func=AF.Exp, scale=1.0 / 16.0),
                     reads=[f"cum{c}"], writes=[f"cum{c}"])
            if debug and t == 0:
                dump("d_eb", eb[:], ["tga0", "tga1"], [128, 2, 512])
            sl_vb = acquire(CH_VB)
            for s in range(4):
                b = tm_matmuls(sl_vb, s)
                P.op("dve", lambda e, s=s, b=b: e.tensor_copy(out=vb[:, s, :], in_=pg[b][:, :]), reads=[f"pg{b}"], writes=[f"vb{s}"])
            release()
            sl_zb = acquire(CH_ZB)
            for s in range(4):
                b = tm_matmuls(sl_zb, s)
                i = s % 2
                P.op("act", lambda e, i=i, b=b: e.activation(out=ztmp[i], in_=pg[b][:, :], func=AF.Tanh, scale=0.5),
                     reads=[f"pg{b}"], writes=[f"PT{i}"])
                P.op("dve", lambda e, i=i, b=b, s=s: e.scalar_tensor_tensor(out=szB[:, s, :], in0=ztmp[i], scalar=1.0,
                                                                            in1=pg[b][:, :], op0=ALU.add, op1=ALU.mult),
                     reads=[f"pg{b}", f"PT{i}"], writes=[f"szB{s}", f"yT{4 + s}"])
            release()
            sl_qk = acquire(CH_QK)
            for c in range(2):
                b = fm_matmuls(sl_qk, c)
                for hh in range(2):
                    P.op("dve", lambda e, c=c, hh=hh, b=b: e.scalar_tensor_tensor(
                        out=qtT[64 * hh:64 * hh + 64, c, hh, :], in0=pg[b][64 * hh:64 * hh + 64, :], scalar=0.125,
                        in1=eb[64 * hh:64 * hh + 64, c, :], op0=ALU.mult, op1=ALU.mult),
                        reads=[f"pg{b}", f"tga{c}"], writes=[f"qtT{c}"])
            for c in range(2):
                b = fm_matmuls(sl_qk, 2 + c)
                P.op("dve", lambda e, c=c, b=b: e.tensor_tensor(out=ktT[:, c, :], in0=pg[b][:, :], in1=enb[:, c, :], op=ALU.mult),
                     reads=[f"pg{b}", f"cum{c}"], writes=[f"ktT{c}"])
                for s in range(4):
                    P.op("dve", lambda e, c=c, s=s, b=b: e.scalar_tensor_tensor(
                        out=khT[:, c, s * 128:(s + 1) * 128], in0=pg[b][:, s * 128:(s + 1) * 128],
                        scalar=eb[:, c, s * 128 + 127:s * 128 + 128], in1=enb[:, c, s * 128:(s + 1) * 128],
                        op0=ALU.mult, op1=ALU.mult),
                        reads=[f"pg{b}", f"cum{c}", f"tga{c}"], writes=[f"gatedA{c}"])
                P.op("dve", lambda e, c=c: e.tensor_copy(out=eblast[:, c, :],
                                                         in_=eb[:, c, :].rearrange("p (s j) -> p s j", j=128)[:, :, 127]),
                     reads=[f"tga{c}"], writes=[f"eblast{c}"])
            release()
            for s in range(4):
                for c in range(2):
                    P.op("pe", lambda e, s=s, c=c: e.transpose(out=ptr[:, c * 4 + s, :], in_=khT[:, c, s * 128:(s + 1) * 128],
                                                               identity=ident[:]),
                         reads=[f"gatedA{c}", "ident"], writes=["ptr"])
            for c in range(2):
                P.op("dve", lambda e, c=c: e.tensor_copy(out=khat[:, :, c, :], in_=ptr[:, c * 4:(c + 1) * 4, :]),
                     reads=["ptr"], writes=[f"khat{c}"])

        def phase_b2(t):
            sl_qa = acquire(CH_QA)
            for j in range(4):
                b = fm_matmuls(sl_qa, j)
                P.op("dve", lambda e, j=j, b=b: e.tensor_scalar(out=QT[:, j, :], in0=pg[b][:, :], scalar1=0.125, scalar2=None,
                                                                op0=ALU.mult), reads=[f"pg{b}"], writes=[f"QT{j}", "gaT0", "gaT1", "gaT2", "gaT3"])
            release()
            sl_ka = acquire(CH_KA)
            for j in range(4):
                b = fm_matmuls(sl_ka, j)
                P.op("dve", lambda e, j=j, b=b: e.tensor_copy(out=KT[:, j, t * T:(t + 1) * T], in_=pg[b][:, :]),
                     reads=[f"pg{b}"], writes=[f"KT{j}"] + (stg_keys if t == 4 else []))
            release()
            sl_va = acquire(CH_VA)
            for s in range(4):
                b = tm_matmuls(sl_va, s)
                g = 4 * t + s
                for hd in range(4):
                    P.op("dve", lambda e, g=g, b=b, hd=hd: e.tensor_scalar(out=V[:, g, hd, 0:128], in0=pg[b][:, hd * 128:(hd + 1) * 128],
                                                                          scalar1=eft[:, hd:hd + 1], scalar2=None, op0=ALU.mult),
                         reads=[f"pg{b}", "eft"], writes=["V"])
                P.op("dve", lambda e, g=g: e.tensor_copy(out=V[:, g, :, 128:129], in_=eft[:, :].unsqueeze(2)),
                     reads=["eft"], writes=["V"])
            release()
            sl_za = acquire(CH_ZA)
            for s in range(4):
                b = tm_matmuls(sl_za, s)
                i = s % 2
                P.op("act", lambda e, i=i, b=b: e.activation(out=ztmp[i], in_=pg[b][:, :], func=AF.Tanh, scale=0.5),
                     reads=[f"pg{b}"], writes=[f"PT{i}"])
                P.op("dve", lambda e, i=i, b=b, s=s: e.scalar_tensor_tensor(out=szA[:, s, :], in0=ztmp[i], scalar=1.0,
                                                                            in1=pg[b][:, :], op0=ALU.add, op1=ALU.mult),
                     reads=[f"pg{b}", f"PT{i}"], writes=[f"szA{s}", f"yT{s}"])
            release()
            if debug and t == 0:
                dump("d_QT", QT[:], ["QT0", "QT1", "QT2", "QT3"], [128, 4, 512])

        def gla_at(t, s):
            bA = alloc_pg()
            for c in range(2):
                P.op("pe", lambda e, c=c: e.matmul(
                    out=pg[bA][:, c * 256:(c + 1) * 256], lhsT=ktT[:, c, s * 128:(s + 1) * 128],
                    rhs=qtT[:, c, :, s * 128:(s + 1) * 128], start=True, stop=True),
                    reads=[f"ktT{c}", f"qtT{c}"], writes=[f"pg{bA}"])
            P.op("dve", lambda e: e.tensor_tensor(out=ATs[0][:, :], in0=pg[bA][:, :], in1=tri4[:, :], op=ALU.mult),
                 reads=[f"pg{bA}", "tri4"], writes=["ATs0"])

        def gla_rest(t, s):
            bS = alloc_pg()
            for c in range(2):
                P.op("pe", lambda e, c=c: e.matmul(out=pg[bS][:, c * 256:(c + 1) * 256], lhsT=khat[:, s, c, :],
                                                   rhs=vb[:, s, c * 256:(c + 1) * 256], start=True, stop=True),
                     reads=[f"khat{c}", f"vb{s}"], writes=[f"pg{bS}"])
            bO = alloc_pg()
            for hd in range(4):
                c, hh = hd // 2, hd % 2
                P.op("pe", lambda e, c=c, hh=hh, hd=hd: e.matmul(
                    out=pg[bO][:, hd * 128:(hd + 1) * 128], lhsT=qtT[:, c, hh, s * 128:(s + 1) * 128],
                    rhs=stbf[:, c, :], start=True, stop=False),
                    reads=[f"qtT{c}", f"stbf{c}"], writes=[f"pg{bO}"])
                P.op("pe", lambda e, hd=hd: e.matmul(
                    out=pg[bO][:, hd * 128:(hd + 1) * 128], lhsT=ATs[0][:, hd * 128:(hd + 1) * 128],
                    rhs=vb[:, s, hd * 128:(hd + 1) * 128], start=False, stop=True),
                    reads=["ATs0", f"vb{s}"], writes=[f"pg{bO}"])
            for c in range(2):
                for hh in range(2):
                    P.op("dve", lambda e, c=c, hh=hh: e.scalar_tensor_tensor(
                        out=state[64 * hh:64 * hh + 64, c, :], in0=state[64 * hh:64 * hh + 64, c, :],
                        scalar=eblast[64 * hh:64 * hh + 64, c, s:s + 1],
                        in1=pg[bS][64 * hh:64 * hh + 64, c * 256 + hh * 128:c * 256 + (hh + 1) * 128],
                        op0=ALU.mult, op1=ALU.add),
                        reads=[f"pg{bS}", f"eblast{c}", f"state{c}"], writes=[f"state{c}"])
                P.op("pool", lambda e, c=c: e.tensor_copy(out=stbf[:, c, :], in_=state[:, c, :]),
                     reads=[f"state{c}"], writes=[f"stbf{c}"])
            oi = s % 2
            P.op("dve", lambda e: e.tensor_copy(out=gl[:, oi, :], in_=pg[bO][:, :]), reads=[f"pg{bO}"], writes=[f"gl{oi}"])
            if debug and t == 0:
                P.op("dve", lambda e: e.tensor_copy(out=dbgbuf[:, s, :], in_=gl[:, oi, :]), reads=[f"gl{oi}"], writes=["dbgbuf"])

        def gla_out_ew(t, s):
            oi = s % 2
            for hd in range(4):
                P.op("dve", lambda e, hd=hd: e.scalar_tensor_tensor(out=junkb[:, :], in0=gl[:, oi, hd * 128:(hd + 1) * 128], scalar=1.0,
                                                                    in1=gl[:, oi, hd * 128:(hd + 1) * 128], op0=ALU.mult, op1=ALU.mult,
                                                                    accum_out=ssb[:, s * 4 + hd:s * 4 + hd + 1]),
                     reads=[f"gl{oi}"], writes=["junkb", f"ssb{s}_{hd}"])
            P.op("dve", lambda e: e.tensor_scalar(out=rsb[:, s * 4:(s + 1) * 4], in0=ssb[:, s * 4:(s + 1) * 4],
                                                  scalar1=1.0 / 128, scalar2=EPS, op0=ALU.mult, op1=ALU.add),
                 reads=[f"ssb{s}_{hd}" for hd in range(4)], writes=[f"rsb{s}"])
            pow_cols(rstdb, rsb, s * 4, 4, [f"rsb{s}"], f"rstdb{s}")

        def gla_out_ew2(t, s):
            oi = s % 2
            gi = s % 2
            for hd in range(4):
                P.op("dve", lambda e, hd=hd: e.scalar_tensor_tensor(
                    out=gated[gi][:, hd * 128:(hd + 1) * 128], in0=gl[:, oi, hd * 128:(hd + 1) * 128],
                    scalar=rstdb[:, s * 4 + hd:s * 4 + hd + 1], in1=szB[:, s, hd * 128:(hd + 1) * 128],
                    op0=ALU.mult, op1=ALU.mult),
                    reads=[f"gl{oi}", f"rstdb{s}_{hd}", f"szB{s}"], writes=[f"hbL{gi}"])

        def gla_out_pe(t, s):
            gi = s % 2
            for hd in range(4):
                P.op("pe", lambda e, hd=hd: e.transpose(out=ptr[:, hd, :], in_=gated[gi][:, hd * 128:(hd + 1) * 128],
                                                        identity=ident[:]),
                     reads=[f"hbL{gi}", "ident"], writes=["ptr"])
            P.op("dve", lambda e: e.tensor_copy(out=gbT[:, :, s * 128:(s + 1) * 128], in_=ptr[:, 0:4, :]),
                 reads=["ptr"], writes=[f"gbT{s}"])

        def acc_idx(m, a):
            return 2 * a + m

        def acc_ap(m, a, lo, hi):
            i_ = acc_idx(m, a)
            return poa[i_ // 3][:, (i_ % 3) * 130 + lo:(i_ % 3) * 130 + hi]

        pair_ctr = [0]

        def alloc_pair():
            bp = 2 * (pair_ctr[0] % 2)
            pair_ctr[0] += 1
            return bp

        def attention_jobs(t, h, deferred):
            nkb = 4 * t + 4
            qk_info = {}

            def emit_qk_pair(kb):
                r = kb - 4 * t
                bp = alloc_pair()
                c0 = 0 if r < 0 else 128 * r
                for m in range(2):
                    b = bp + m
                    lhsT = KT[64 * m:64 * m + 64, h, kb * 128:(kb + 1) * 128]
                    if r < 0:
                        P.op("pe", lambda e, b=b, lhsT=lhsT, m=m: e.matmul(out=pg[b][:, :], lhsT=lhsT, rhs=QT[64 * m:64 * m + 64, h, :],
                                                                           start=True, stop=True),
                             reads=[f"KT{h}", f"QT{h}"], writes=[f"pg{b}"])
                    else:
                        P.op("pe", lambda e, b=b, lhsT=lhsT, m=m: e.matmul(out=pg[b][:, c0:c0 + 128], lhsT=lhsT,
                                                                           rhs=QT[64 * m:64 * m + 64, h, c0:c0 + 128],
                                                                           start=True, stop=False),
                             reads=[f"KT{h}", f"QT{h}"], writes=[f"pg{b}"])
                        P.op("pe", lambda e, b=b: e.matmul(out=pg[b][:, c0:c0 + 128], lhsT=ident[:, :], rhs=negm[:, :],
                                                           start=False, stop=True),
                             reads=["ident", "negm"], writes=[f"pg{b}"])
                        if c0 + 128 < 512:
                            P.op("pe", lambda e, b=b, lhsT=lhsT, m=m: e.matmul(out=pg[b][:, c0 + 128:512], lhsT=lhsT,
                                                                               rhs=QT[64 * m:64 * m + 64, h, c0 + 128:512],
                                                                               start=True, stop=True),
                                 reads=[f"KT{h}", f"QT{h}"], writes=[f"pg{b}"])
                qk_info[kb] = (bp, c0, r)

            def emit_exp(kb):
                bp, c0, r = qk_info.pop(kb)
                pp = kb % 2
                cbias = SLOPES[h] * (128.0 * r - 129.0)
                P.op("act", lambda e: e.activation(out=PT2[pp][:, :, c0:512], in_=pg4[:, bp:bp + 2, c0:512], func=AF.Exp,
                                                   bias=cbias, scale=1.0),
                     reads=[f"pg{bp}", f"pg{bp + 1}"], writes=[f"PT{pp}"])

            def emit_pv(kb, m):
                r = kb - 4 * t
                pp = kb % 2
                for a in range(max(r, 0), 4):
                    i_ = acc_idx(m, a)
                    P.op("pe", lambda e, a=a, i_=i_: e.matmul(out=acc_ap(m, a, 0, 129), lhsT=PT2[pp][:, m, a * 128:(a + 1) * 128],
                                                              rhs=V[:, kb, h, 0:129], start=(kb == 0 and i_ in (0, 4, 6)), stop=False,
                                                              skip_group_check=True),
                         reads=[f"PT{pp}", "V", "Vones"], writes=[f"poa{i_ // 3}"])

            for kb in range(nkb + 2):
                if kb < nkb:
                    emit_qk_pair(kb)
                kp = kb - 2
                if kp >= 0:
                    emit_pv(kp, 0)
                    emit_pv(kp, 1)
                    r_done = kp - 4 * t
                    if r_done >= 1:
                        attention_evac(t, h, r_done)
                if kb < nkb:
                    emit_exp(kb)
                if t == 0:
                    if kb == 1:
                        for kk in sorted(deferred):
                            for f in deferred[kk]:
                                f()
                else:
                    for f in deferred.get(kb, ()):
                        f()

        def attention_evac(t, h, stage):
            bank = stage - 1
            n = 3 if bank < 2 else 2
            i0 = 3 * bank
            zs = poa[bank][:, 0:n * 130].rearrange("p (i c) -> p i c", c=130)[:, :, 128]
            P.op("dve", lambda e: e.reciprocal(out=rz[:, i0:i0 + n], in_=zs),
                 reads=[f"poa{bank}"], writes=[f"rz{i0 + k}" for k in range(n)])
            for i_ in range(i0, i0 + n):
                if i_ % 2 == 1:
                    P.op("dve", lambda e, i_=i_: e.tensor_scalar(out=rz[:, i_:i_ + 1], in0=rz[:, i_:i_ + 1], scalar1=lams[:, 4:5],
                                                                 scalar2=None, op0=ALU.mult),
                         reads=[f"rz{i_}", "nlam"], writes=[f"rz{i_}"])

            def part0(a):
                i_ = acc_idx(0, a)
                oi2 = a % 2
                P.op("dve", lambda e: e.tensor_scalar(out=otmp[oi2][:, :], in0=acc_ap(0, a, 0, 128), scalar1=rz[:, i_:i_ + 1],
                                                      scalar2=None, op0=ALU.mult),
                     reads=[f"poa{i_ // 3}", f"rz{i_}"], writes=[f"otmp{oi2}"])

            def part1(a):
                i_ = acc_idx(1, a)
                oi2 = a % 2
                P.op("dve", lambda e: e.scalar_tensor_tensor(out=oaf[a][:, :], in0=acc_ap(1, a, 0, 128), scalar=rz[:, i_:i_ + 1],
                                                             in1=otmp[oi2][:, :], op0=ALU.mult, op1=ALU.add),
                     reads=[f"poa{i_ // 3}", f"rz{i_}", f"otmp{oi2}"], writes=[f"oaf{a}"])
                if debug and t == 0:
                    P.op("dve", lambda e: e.tensor_copy(out=dbgoa[:, a, h, :], in_=oaf[a][:, :]), reads=[f"oaf{a}"], writes=["dbgoa"])

            if stage == 1:
                part0(0); part1(0); part0(1)
            elif stage == 2:
                part1(1); part0(2); part1(2)
            else:
                part0(3); part1(3)

        def attention_norm(t, h):
            for a in range(4):
                ci = a * 4 + h
                P.op("dve", lambda e, a=a, ci=ci: e.scalar_tensor_tensor(out=junkb[:, :], in0=oaf[a][:, :], scalar=1.0, in1=oaf[a][:, :],
                                                                         op0=ALU.mult, op1=ALU.mult, accum_out=ssa[:, ci:ci + 1]),
                     reads=[f"oaf{a}"], writes=["junkb", f"ssa{ci}"])
            for a in range(4):
                ci = a * 4 + h
                P.op("dve", lambda e, ci=ci: e.tensor_scalar(out=rsa[:, ci:ci + 1], in0=ssa[:, ci:ci + 1], scalar1=1.0 / 128, scalar2=EPS,
                                                            op0=ALU.mult, op1=ALU.add), reads=[f"ssa{ci}"], writes=[f"rsa{ci}"])
                P.op("pool", lambda e, ci=ci: e.tensor_tensor(out=rstda[:, ci:ci + 1], in0=rsa[:, ci:ci + 1], in1=lams[:, 5:6], op=ALU.pow),
                     reads=[f"rsa{ci}", "nhalf"], writes=[f"rstda{ci}"])

        def attention_norm2(t, h):
            for a in range(4):
                ci = a * 4 + h
                P.op("dve", lambda e, a=a, ci=ci: e.scalar_tensor_tensor(
                    out=gatedA[:, a, h * 128:(h + 1) * 128], in0=oaf[a][:, :], scalar=rstda[:, ci:ci + 1],
                    in1=szA[:, a, h * 128:(h + 1) * 128], op0=ALU.mult, op1=ALU.mult),
                    reads=[f"oaf{a}", f"rstda{ci}", f"szA{a}"], writes=[f"gatedA{a}"])

        def attn_post(t):
            for a in range(4):
                for hd in range(4):
                    P.op("pe", lambda e, a=a, hd=hd: e.transpose(out=ptr[:, 4 + hd, :], in_=gatedA[:, a, hd * 128:(hd + 1) * 128],
                                                                 identity=ident[:]),
                         reads=[f"gatedA{a}", "ident"], writes=["ptr"])
                P.op("dve", lambda e, a=a: e.tensor_copy(out=gaT[:, :, a * 128:(a + 1) * 128], in_=ptr[:, 4:8, :]),
                     reads=["ptr"], writes=[f"gaT{a}", "QT0", "QT1", "QT2", "QT3"])
            if debug and t == 0:
                dump("d_gaT", gaT[:], ["gaT0", "gaT1", "gaT2", "gaT3"], [128, 4, 512])
                dump("d_gbT", gbT[:], ["gbT0", "gbT1", "gbT2", "gbT3"], [128, 4, 512])

        def phase_e(t, mid):
            gaT_keys = ["gaT0", "gaT1", "gaT2", "gaT3"]
            gbT_keys = ["gbT0", "gbT1", "gbT2", "gbT3"]
            ytmp = [gl[:, 0, :], gl[:, 1, :]]
            ytk = ["gl0", "gl1"]
            sl_ga0 = acquire(CH_GA0)
            sl_gb0 = acquire(CH_GB0)
            sl_ga1 = sl_gb1 = None
            for j in range(8):
                if j == 4:
                    release()
                    release()
                    sl_ga1 = acquire(CH_GA1)
                    sl_gb1 = acquire(CH_GB1)
                sga = sl_ga0 if j < 4 else sl_ga1
                sgb = sl_gb0 if j < 4 else sl_gb1
                jj = j % 4
                ti = j % 2
                bga = fm_matmuls(sga, jj)
                P.op("act", lambda e, ti=ti, bga=bga: e.activation(out=tga[ti], in_=pg[bga][:, :], func=AF.Tanh, scale=0.5),
                     reads=[f"pg{bga}"], writes=[f"tga{ti}"])
                bgb = fm_matmuls(sgb, jj)
                P.op("act", lambda e, ti=ti, bgb=bgb: e.activation(out=tgb[ti][:, :], in_=pg[bgb][:, :], func=AF.Tanh, scale=0.5),
                     reads=[f"pg{bgb}"], writes=["tgb0"])
                if j == 0:
                    mid()
                bya = alloc_pg()
                for kc in range(4):
                    P.op("pe", lambda e, kc=kc, j=j, bya=bya: e.matmul(out=pg[bya][:, :], lhsT=wupA[:, kc, j * 128:(j + 1) * 128],
                                                                      rhs=gaT[:, kc, :], start=(kc == 0), stop=(kc == 3)),
                         reads=["wupA"] + gaT_keys, writes=[f"pg{bya}"])
                P.op("dve", lambda e, ti=ti, bya=bya: e.scalar_tensor_tensor(out=ytmp[ti], in0=tga[ti], scalar=1.0, in1=pg[bya][:, :],
                                                                             op0=ALU.add, op1=ALU.mult),
                     reads=[f"pg{bya}", f"tga{ti}"], writes=[ytk[ti]])
                byb = alloc_pg()
                for kc in range(4):
                    P.op("pe", lambda e, kc=kc, j=j, byb=byb: e.matmul(out=pg[byb][:, :], lhsT=wupB[:, kc, j * 128:(j + 1) * 128],
                                                                      rhs=gbT[:, kc, :], start=(kc == 0), stop=(kc == 3)),
                         reads=["wupB"] + gbT_keys, writes=[f"pg{byb}"])
                P.op("dve", lambda e, ti=ti, byb=byb: e.scalar_tensor_tensor(out=tgb[ti][:, :], in0=tgb[ti][:, :], scalar=1.0, in1=pg[byb][:, :],
                                                                             op0=ALU.add, op1=ALU.mult),
                     reads=[f"pg{byb}", "tgb0"], writes=["tgb0"])
                P.op("dve", lambda e, ti=ti, j=j: e.tensor_tensor(out=yT[:, j, :], in0=ytmp[ti], in1=tgb[ti][:, :], op=ALU.add),
                     reads=[ytk[ti], "tgb0"], writes=[f"yT{j}", (f"szA{j}" if j < 4 else f"szB{j - 4}")])
            release()
            release()
            if debug and t == 0:
                dump("d_yT", yT[:], [f"yT{j}" for j in range(8)], [128, 8, 512])

        def phase_f(t, hooks):
            sl_wo0 = acquire(CH_WO0)
            sl_wo1 = acquire(CH_WO1)
            zg = [cum, gl]
            zgk = [["cum0", "cum1"], ["gl0", "gl1"]]
            load_xr(4 * t)
            load_xr(4 * t + 1)

            def zbank(s, hh):
                i = 2 * s + hh
                if 4 <= i < 7:
                    return poa[i - 4][:, :], f"poa{i - 4}"
                b = alloc_pg()
                return pg[b], f"pg{b}"

            def f_mm(s):
                g = 4 * t + s
                for hh in range(2):
                    zb_, zk_ = zbank(s, hh)
                    for kc in range(8):
                        slw = sl_wo0 if kc < 4 else sl_wo1
                        kk = kc % 4
                        P.op("pe", lambda e, kc=kc, kk=kk, slw=slw, hh=hh, zb_=zb_: e.matmul(
                            out=zb_, lhsT=yT[:, kc, s * 128:(s + 1) * 128],
                            rhs=wbuf[slw][:, kk * 1024 + hh * 512: kk * 1024 + (hh + 1) * 512], start=(kc == 0), stop=(kc == 7)),
                            reads=[f"yT{kc}", f"w{slw}_{kk * 2 + hh}"], writes=[zk_])
                    P.op("act", lambda e, zb_=zb_, hh=hh: e.activation(out=ttmp[hh], in_=zb_, func=AF.Square,
                                                                      accum_out=ssz[:, 2 * g + hh:2 * g + hh + 1]),
                         reads=[zk_], writes=[f"tga{hh}", f"ssz{g}_{hh}"])
                    P.op("dve", lambda e, zb_=zb_, hh=hh: e.tensor_tensor(out=zg[s % 2][:, hh, :], in0=zb_,
                                                                         in1=gpost[:, hh * 512:(hh + 1) * 512], op=ALU.mult),
                         reads=[zk_, "gpost", f"ssz{g}_{hh}"], writes=[zgk[s % 2][hh]])
                P.op("dve", lambda e: e.tensor_tensor(out=rsz[:, g:g + 1], in0=ssz[:, 2 * g:2 * g + 1], in1=ssz[:, 2 * g + 1:2 * g + 2],
                                                      op=ALU.add), reads=[f"ssz{g}_0", f"ssz{g}_1"], writes=[f"rsz{g}"])
                P.op("dve", lambda e: e.tensor_scalar(out=rsz[:, g:g + 1], in0=rsz[:, g:g + 1], scalar1=1.0 / D, scalar2=EPS,
                                                      op0=ALU.mult, op1=ALU.add), reads=[f"rsz{g}"], writes=[f"rsz{g}"])
                P.op("pool", lambda e: e.tensor_tensor(out=rstdz[:, g:g + 1], in0=rsz[:, g:g + 1], in1=lams[:, 5:6], op=ALU.pow),
                     reads=[f"rsz{g}", "nhalf"], writes=[f"rstdz{g}"])

            def f_fin(s):
                g = 4 * t + s
                ri = g % 2
                for hh in range(2):
                    xk = f"xn{ri}" if hh == 0 else f"xn{ri}b"
                    P.op("dve", lambda e, hh=hh: e.scalar_tensor_tensor(out=xr[ri][:, hh * 512:(hh + 1) * 512], in0=zg[s % 2][:, hh, :],
                                                                        scalar=rstdz[:, g:g + 1], in1=xr[ri][:, hh * 512:(hh + 1) * 512],
                                                                        op0=ALU.mult, op1=ALU.add),
                         reads=[zgk[s % 2][hh], f"rstdz{g}", xk], writes=[xk])
                P.op("pool", lambda e: e.dma_start(out=out[g * 128:(g + 1) * 128, :], in_=xr[ri][:]),
                     reads=[f"xn{ri}", f"xn{ri}b"], dma=f"st{ri}")
                if s + 2 < 4:
                    load_xr(g + 2)

            steps = [("mm", 0), ("mm", 1), ("fin", 0), ("mm", 2), ("fin", 1), ("mm", 3), ("fin", 2), ("fin", 3)]
            for kind, s_ in steps:
                (f_mm if kind == "mm" else f_fin)(s_)
                for f in hooks.get((kind, s_), ()):
                    f()
            release()
            release()

        load_xa(0, 0)
        load_xa(0, 1)
        for t in range(ntiles):
            if stop_after == "prologue":
                break
            if t == 0:
                phase_a(t, [0, 1])
                phase_a(t, [2, 3])
                for _ in range(NSLOT):
                    emit_load()
            if stop_after == "a":
                break
            phase_b1(t)
            if stop_after == "b1":
                break
            phase_b2(t)
            if stop_after == "b2":
                break
            if t == 0:
                conv_up()
            gla_at(t, 0)
            gla_rest(t, 0)
            deferred = {1: [lambda: gla_at(t, 1), lambda: gla_rest(t, 1)], 2: [lambda: gla_out_ew(t, 0)],
                        3: [lambda: gla_out_ew2(t, 0)], 4: [lambda: gla_out_pe(t, 0)]}
            for s in range(4):
                attention_jobs(t, s, deferred)
                deferred = {1: [], 2: [lambda s=s: attention_norm(t, s)], 3: [lambda s=s: attention_norm2(t, s)], 4: []}
                if s < 3:
                    deferred[2].append(lambda s=s: gla_out_ew(t, s + 1))
                    deferred[3].append(lambda s=s: gla_out_ew2(t, s + 1))
                    deferred[4].append(lambda s=s: gla_out_pe(t, s + 1))
                if s < 2:
                    deferred[1].append(lambda s=s: gla_at(t, s + 2))
                    deferred[1].append(lambda s=s: gla_rest(t, s + 2))
            for kk in (1, 2, 3, 4):
                for f in deferred[kk]:
                    f()
            if debug and t == 0:
                P.op("sp", lambda e: e.dma_start(out=dbg["d_ob"], in_=dbgbuf[:, 0:4, :]), reads=["dbgbuf"], dma="dbg")
                dump("d_oacc", dbgoa[:], ["dbgoa"], [128, 4, 4, 128])
                dump("d_state", state[:], ["state0", "state1"], [128, 2, 128])
            if t + 1 < ntiles:
                load_xa(t + 1, 0)
                load_xa(t + 1, 1)
                a_stats(t + 1, 0)
                a_stats(t + 1, 1)
            phase_e(t, lambda: attn_post(t))
            if t + 1 < ntiles:
                nt_ = t + 1
                hooks = {("mm", 2): [lambda: a_norm_tr(nt_, 0)], ("mm", 3): [lambda: a_norm_tr(nt_, 1)],
                         ("fin", 2): [lambda: a_stats(nt_, 2)],
                         ("fin", 3): [lambda: a_norm_tr(nt_, 2), lambda: a_stats(nt_, 3), lambda: a_norm_tr(nt_, 3)]}
            else:
                hooks = {}
            phase_f(t, hooks)
        P.op("sp", lambda e: None, reads=[], writes=["xn0", "xn0b", "xn1", "xn1b"] + (["dbgbuf"] if debug else []))
        P.emit({"pe": block.tensor, "act": block.scalar, "dve": block.vector, "pool": block.gpsimd, "sp": block.sync}, sems)
    return nc


def _host_consts():
    bf = ml_dtypes.bfloat16
    k = np.arange(128)[:, None]
    q = np.arange(128)[None, :]
    tri = (q >= k).astype(np.float32)
    negm = np.where(k > q, NEG, 0.0).astype(np.float32)
    btab = np.zeros((128, 128), np.float32)
    for h in range(4):
        for i in range(32):
            btab[:, h * 32 + i] = SLOPES[h] * (np.arange(128) + 128.0 * (i - 28) - 256.0)
    rmask = np.ones((128, 512), np.float32)
    rmask[:, ::128] = 0.0
    return {
        "ident": np.eye(128, dtype=np.float32).astype(bf),
        "tri4": np.tile(tri, (1, 4)).astype(bf),
        "negmask": negm.astype(bf),
        "biastab": btab,
        "eftab": np.stack([np.exp(SLOPES[h] * (np.arange(128) - 127.0)) for h in range(4)], axis=1).astype(np.float32),
        "resetmask": rmask.astype(bf),
    }


_CACHE = {}


def kernel(x, g_pre, w_in, lam_q1, lam_k1, lam_q2, lam_k2, g_sub_a, w_alpha, b_alpha, g_sub_b, w_up_a, w_up_b, w_out, g_post,
           _ntiles=NT, _debug=False, _cores=8, _stop=None):
    f = np.float32
    x = np.asarray(x, f)
    key = (_ntiles, _debug, _stop)
    if key not in _CACHE:
        _CACHE[key] = _build(_ntiles, _debug, _stop)
    nc = _CACHE[key]
    smalls = np.zeros((128, 12), f)
    smalls[:, 0:8] = np.asarray(g_pre, f)[0].reshape(8, 128).T
    smalls[:, 8] = np.asarray(g_sub_a, f)[0]
    smalls[:, 9] = np.asarray(g_sub_b, f)[0]
    smalls[:, 10:12] = np.asarray(b_alpha, f)[0].reshape(2, 128).T
    lamv = np.concatenate([np.asarray(v, f)[0] for v in (lam_q1, lam_k1, lam_q2, lam_k2)])[None, :].repeat(128, 0)
    shared = {
        "w_in": np.ascontiguousarray(np.asarray(w_in, f)[0]),
        "w_up_a": np.ascontiguousarray(np.asarray(w_up_a, f)[0]),
        "w_up_b": np.ascontiguousarray(np.asarray(w_up_b, f)[0]),
        "w_out": np.ascontiguousarray(np.asarray(w_out, f)[0]),
        "w_alpha": np.ascontiguousarray(np.asarray(w_alpha, f)[0]),
        "smalls": smalls,
        "lamv": np.ascontiguousarray(lamv),
        "gpost": np.ascontiguousarray(np.asarray(g_post, f)[0][None, :].repeat(128, 0)),
    }
    shared.update(_host_consts())
    in_maps = [dict(shared, x=np.ascontiguousarray(x[b])) for b in range(_cores)]
    res = run_bass_kernel_spmd(nc, in_maps, core_ids=list(range(_cores)))
    if _debug:
        return res
    return np.stack([res.results[b]["out"] for b in range(_cores)], axis=0).astype(np.float32)
```

```python
import math
from contextlib import ExitStack

import numpy as np
import ml_dtypes

import concourse.bass as bass
import concourse.mybir as mybir
from concourse.bass_utils import run_bass_kernel_spmd

F32 = mybir.dt.float32
BF16 = mybir.dt.bfloat16
AF = mybir.ActivationFunctionType
ALU = mybir.AluOpType
AX = mybir.AxisListType

S = 4096
D = 1024
T = 512
NT = S // T
DIN = 5648
EPS = 1e-6
LAM_INIT = 0.8 - 0.6 * math.exp(-0.3 * 0)
SLOPES = [2.0 ** (-8.0 * (h + 1) / 4) for h in range(4)]
C_QA, C_KA, C_VA, C_ZA, C_QB, C_KB, C_VB, C_ZB, C_LR, C_GA, C_GB = 0, 512, 1024, 1536, 2048, 2304, 2560, 3072, 3584, 3600, 4624
CHUNK_COL = [C_QA, C_KA, C_VA, C_ZA, C_QB, C_VB, C_ZB, C_GA, C_GA + 512, C_GB, C_GB + 512]
CH_QA, CH_KA, CH_VA, CH_ZA, CH_QK, CH_VB, CH_ZB, CH_GA0, CH_GA1, CH_GB0, CH_GB1, CH_WO0, CH_WO1 = range(13)
NEG = -30000.0
NSLOT = 4

ENGS = ("pe", "act", "dve", "pool", "sp")


class _Op:
    __slots__ = ("eng", "fn", "deps", "is_dma", "sem", "val", "needs_inc")

    def __init__(self, eng, fn, is_dma):
        self.eng = eng
        self.fn = fn
        self.deps = []
        self.is_dma = is_dma
        self.sem = None
        self.val = 0
        self.needs_inc = is_dma


class _Prog:
    def __init__(self, same_eng_sync=("act", "dve", "pool")):
        self.ops = []
        self.last_writer = {}
        self.readers = {}
        self.same_eng_sync = set(same_eng_sync)
        self.dma_slots = []

    def op(self, eng, fn, reads=(), writes=(), dma=None):
        o = _Op(eng, fn, dma is not None)
        deps = {}
        for k in reads:
            w = self.last_writer.get(k)
            if w is not None:
                deps[id(w)] = w
        for k in writes:
            w = self.last_writer.get(k)
            if w is not None:
                deps[id(w)] = w
            for r in self.readers.get(k, ()):
                deps[id(r)] = r
        o.deps = list(deps.values())
        for k in reads:
            self.readers.setdefault(k, []).append(o)
        for k in writes:
            self.last_writer[k] = o
            self.readers[k] = []
        if dma is not None:
            o.sem = dma
            if dma not in self.dma_slots:
                self.dma_slots.append(dma)
        self.ops.append(o)
        return o

    def emit(self, block_engines, sems):
        ops = self.ops
        for o in ops:
            for d in o.deps:
                if d.is_dma:
                    continue
                if d.eng != o.eng or (d.eng in self.same_eng_sync):
                    d.needs_inc = True
        cnt = {e: 0 for e in ENGS}
        dcnt = {}
        for o in ops:
            if o.is_dma:
                slot = o.sem
                dcnt[slot] = dcnt.get(slot, 0) + 16
                o.sem = sems[slot]
                o.val = dcnt[slot]
            elif o.needs_inc:
                cnt[o.eng] += 1
                o.sem = sems[o.eng]
                o.val = cnt[o.eng]
        same = self.same_eng_sync

        def make(eng_name):
            my_ops = [o for o in ops if o.eng == eng_name]

            def body(e):
                waited = {}
                for o in my_ops:
                    need = {}
                    for d in o.deps:
                        if (not d.is_dma) and d.eng == eng_name and eng_name not in same:
                            continue
                        key = id(d.sem)
                        if d.val > need.get(key, (None, 0))[1]:
                            need[key] = (d.sem, d.val)
                    for key, (s, v) in need.items():
                        if waited.get(key, 0) >= v:
                            continue
                        e.wait_ge(s, v)
                        waited[key] = v
                    ins = o.fn(e)
                    if o.needs_inc and ins is not None:
                        ins.then_inc(o.sem, 16 if o.is_dma else 1)

            return body

        for eng_name, deco in block_engines.items():
            deco(make(eng_name))


def _build(ntiles=NT, debug=False, stop_after=None):
    nc = bass.Bass("TRN2", target_bir_lowering=False)

    def din(name, shape, dt=F32):
        return nc.dram_tensor(name, shape, dt, kind="ExternalInput").ap()

    x = din("x", [S, D])
    w_in = din("w_in", [D, DIN])
    w_up_a = din("w_up_a", [512, D])
    w_up_b = din("w_up_b", [512, D])
    w_out = din("w_out", [D, D])
    w_alpha = din("w_alpha", [16, 256])
    smalls_d = din("smalls", [128, 12])
    lamv_d = din("lamv", [128, 256])
    gpost_d = din("gpost", [128, D])
    ident_d = din("ident", [128, 128], BF16)
    tri_d = din("tri4", [128, 512], BF16)
    negm_d = din("negmask", [128, 128], BF16)
    btab_d = din("biastab", [128, 128])
    ef_d = din("eftab", [128, 4])
    rmask_d = din("resetmask", [128, 512], BF16)
    out = nc.dram_tensor("out", [S, D], F32, kind="ExternalOutput").ap()
    wsc = nc.dram_tensor("wsc", [13, 128, 4096], BF16).ap()
    dbg = {}
    if debug:
        for nm, shp in (("d_hT", [128, 8, 512]), ("d_QT", [128, 4, 512]), ("d_oacc", [128, 4, 4, 128]),
                        ("d_ob", [128, 4, 512]), ("d_yT", [128, 8, 512]), ("d_gaT", [128, 4, 512]),
                        ("d_gbT", [128, 4, 512]),
                        ("d_eb", [128, 2, 512]), ("d_state", [128, 2, 128])):
            dbg[nm] = nc.dram_tensor(nm, shp, F32, kind="ExternalOutput").ap()

    P = _Prog()
    with ExitStack() as es:
        def SB(name, shape, dt):
            return es.enter_context(nc.sbuf_tensor("sb_" + name, shape, dt))

        def PSM(name, shape, dt):
            return es.enter_context(nc.psum_tensor("ps_" + name, shape, dt))

        KT = SB("KT", [128, 4, S], BF16)
        V = SB("V", [128, 4 * ntiles, 4, 130], BF16)
        wupA = SB("wupA", [128, 4, 1024], BF16)
        wupB = SB("wupB", [128, 4, 1024], BF16)
        wbuf = [SB(f"wbuf{i}", [128, 4096], BF16) for i in range(NSLOT)]
        wlr = SB("wlr", [128, 8, 16], BF16)
        walpha = SB("walpha", [16, 256], BF16)
        xn = [SB(f"xn{i}", [128, 1024], F32) for i in range(2)]
        xr = xn
        xa = [SB(f"xa{i}", [128, 1024], F32) for i in range(2)]
        hb = [SB(f"hb{i}", [128, 1024], BF16) for i in range(2)]
        hT = SB("hT", [128, 8, T], BF16)
        gated = [hb[i][:, 0:512] for i in range(2)]
        obf = [hb[i][:, 512:1024] for i in range(2)]
        QT = SB("QT", [128, 4, T], BF16)
        yT = SB("yT", [128, 8, T], BF16)
        szA = yT[:, 0:4, :]
        szB = yT[:, 4:8, :]
        gbT = SB("gbT", [128, 4, T], BF16)
        gl = SB("gl", [128, 2, T], F32)
        sq = gl[:, 0, :]
        lamv = gl[:, 0, 0:256]
        lamt = gl[:, 0, 256:384]
        wlr_st = gl[:, 1, 0:128].rearrange("p (kc c) -> p kc c", kc=8)
        cum = SB("cum", [128, 2, T], F32)
        walpha_st = cum[0:16, 0, 0:256]
        enb = cum
        qtT = SB("qtT", [128, 2, 2, T], BF16)
        ktT = SB("ktT", [128, 2, T], BF16)
        khat = SB("khat", [128, 4, 2, 128], BF16)
        vb = SB("vb", [128, 4, 512], BF16)
        lrT = SB("lrT", [16, T], BF16)
        state = SB("state", [128, 2, 128], F32)
        stbf = SB("stbf", [128, 2, 128], BF16)
        PT2 = [SB(f"PT{i}", [128, 2, 512], BF16) for i in range(2)]
        gatedA = SB("gatedA", [128, 4, 512], BF16)
        khT = gatedA[:, 0:2, :]
        oaf = [SB(f"oaf{i}", [128, 128], F32) for i in range(4)]
        junkb = SB("junkb", [128, 128], BF16)
        otmp = [SB(f"otmp{i}", [128, 128], F32) for i in range(2)]
        gaT = QT
        eb = SB("eb", [128, 2, T], F32)
        tga = [eb[:, 0, :], eb[:, 1, :]]
        ttmp = tga
        ztmp = [PT2[i][:].rearrange("p a b -> p (a b)").bitcast(F32) for i in range(2)]
        eblast = SB("eblast", [128, 2, 4], F32)
        tgb = [SB("tgb0", [128, 512], F32)] * 2
        ATs = [SB("ATs0", [128, 512], BF16)] * 2
        ident = SB("ident", [128, 128], BF16)
        tri4 = SB("tri4", [128, 512], BF16)
        negm = SB("negm", [128, 128], BF16)
        btab = SB("btab", [128, 128], F32)
        eft = SB("eft", [128, 4], F32)
        rmask = SB("rmask", [128, 512], BF16)
        gpost = SB("gpost", [128, D], F32)
        smalls = SB("smalls", [128, 12], F32)
        lams = SB("lams", [128, 8], F32)
        nbal = SB("nbal", [128, 2], F32)
        ss = SB("ss", [128, 32], F32)
        rs = SB("rs", [128, 32], F32)
        rstd = SB("rstd", [128, 32], F32)
        ssa = SB("ssa", [128, 16], F32)
        rsa = SB("rsa", [128, 16], F32)
        rstda = SB("rstda", [128, 16], F32)
        ssb = SB("ssb", [128, 16], F32)
        rsb = SB("rsb", [128, 16], F32)
        rstdb = SB("rstdb", [128, 16], F32)
        ssz = SB("ssz", [128, 64], F32)
        rsz = SB("rsz", [128, 32], F32)
        rstdz = SB("rstdz", [128, 32], F32)
        rz = SB("rz", [128, 8], F32)
        dbgbuf = SB("dbgbuf", [128, 8, 512], F32) if debug else None
        dbgoa = SB("dbgoa", [128, 4, 4, 128], F32) if debug else None
        pg4 = PSM("pg4", [128, 4, 512], F32)
        pg = [pg4[:, i, :] for i in range(4)]
        ptr = PSM("ptr", [128, 8, 128], BF16)
        poa = [PSM(f"poa{i}", [128, 512], F32) for i in range(3)]

        dma_names = (["stg%d" % i for i in range(8)] + ["wst%d" % i for i in range(NSLOT)] + ["wld%d" % i for i in range(NSLOT)]
                     + ["cst", "xn0", "xn1", "xr0", "xr1", "xa0", "xa1", "st0", "st1", "dbg"])
        sems = {}
        for nm in list(ENGS) + dma_names:
            sems[nm] = es.enter_context(nc.semaphore("s_" + nm))
        _build.sbuf_left = nc.sbuf_bytes_remaining
        block = es.enter_context(nc.Block())

        gctr = [0]

        def alloc_pg():
            b = gctr[0] % 4
            gctr[0] += 1
            return b

        def wkeys(slot):
            return [f"w{slot}_{i}" for i in range(8)]

        def ev_engine(i):
            return "dve"

        for dst, src, key in ((ident, ident_d, "ident"), (tri4, tri_d, "tri4"), (negm, negm_d, "negm"),
                              (btab, btab_d, "btab"), (eft, ef_d, "eft"), (rmask, rmask_d, "rmask"), (gpost, gpost_d, "gpost"),
                              (smalls, smalls_d, "smalls"), (lamv, lamv_d, "gl0"), (walpha_st, w_alpha, "cum0")):
            P.op("sp", lambda e, dst=dst, src=src: e.dma_start(out=dst[:], in_=src), writes=[key], dma="cst")
        P.op("sp", lambda e: e.dma_start(out=wlr_st[:], in_=w_in[:, C_LR:C_LR + 16].rearrange("(kc p) c -> p kc c", p=128)),
             writes=["gl1"], dma="cst")
        _last_c = P.ops[-1]
        for _k in ("ident", "tri4", "negm", "btab", "eft", "rmask", "gpost", "smalls", "gl0", "cum0", "gl1"):
            P.last_writer[_k] = _last_c
        P.op("pool", lambda e: e.memset(lams[:, 5:6], -0.5), writes=["nhalf"])
        P.op("pool", lambda e: e.memset(V[:, :, :, 128:130], 1.0), writes=["Vones"])
        P.op("pool", lambda e: e.memset(state[:], 0.0), writes=["state0", "state1"])
        P.op("pool", lambda e: e.memset(qtT[:], 0.0), writes=["qtT0", "qtT1"])
        P.op("pool", lambda e: e.memset(stbf[:], 0.0), writes=["stbf0", "stbf1"])
        P.op("dve", lambda e: e.tensor_tensor(out=lamt[:, 0:64], in0=lamv[:, 0:64], in1=lamv[:, 64:128], op=ALU.mult),
             reads=[], writes=["gl0"])
        P.op("dve", lambda e: e.tensor_tensor(out=lamt[:, 64:128], in0=lamv[:, 128:192], in1=lamv[:, 192:256], op=ALU.mult),
             reads=[], writes=["gl0"])
        P.op("dve", lambda e: e.reduce_sum(out=lams[:, 0:2], in_=lamt[:].rearrange("p (a b) -> p a b", a=2), axis=AX.X),
             reads=["gl0"], writes=["lams01"])
        P.op("act", lambda e: e.activation(out=lams[:, 2:4], in_=lams[:, 0:2], func=AF.Exp), reads=["lams01"], writes=["lams23"])
        P.op("dve", lambda e: e.tensor_tensor(out=lams[:, 4:5], in0=lams[:, 3:4], in1=lams[:, 2:3], op=ALU.subtract),
             reads=["lams23"], writes=["nlam"])
        P.op("dve", lambda e: e.tensor_scalar(out=lams[:, 4:5], in0=lams[:, 4:5], scalar1=-LAM_INIT, scalar2=None, op0=ALU.add),
             reads=["nlam"], writes=["nlam"])
        P.op("dve", lambda e: e.tensor_scalar(out=nbal[:], in0=smalls[:, 10:12], scalar1=-1.0, scalar2=None, op0=ALU.mult),
             reads=["smalls"], writes=["nbal"])
        P.op("dve", lambda e: e.tensor_copy(out=walpha[:], in_=walpha_st[:]), reads=["cum0"], writes=["walpha"])
        for kc in range(8):
            P.op("dve", lambda e, kc=kc: e.tensor_scalar(out=wlr[:, kc, :], in0=wlr_st[:, kc, :], scalar1=smalls[:, kc:kc + 1],
                                                        scalar2=None, op0=ALU.mult),
                 reads=["gl1", "smalls"], writes=["wlr"])

        stgK = [KT[:, i // 2, 2048 + (i % 2) * 1024:2048 + (i % 2 + 1) * 1024].bitcast(F32) for i in range(8)]
        stg_keys = [f"stgK{i}" for i in range(8)]
        pctr = [0]

        def conv_piece(src_ap, dst_ap, dst_keys, scale_ap, scale_c):
            i = pctr[0] % 8
            pctr[0] += 1
            sap = stgK[i]
            P.op("sp", lambda e: e.dma_start(out=sap, in_=src_ap), writes=[stg_keys[i]], dma=f"stg{i}")
            if scale_ap is None:
                if i % 2 == 0:
                    P.op("dve", lambda e: e.tensor_scalar(out=dst_ap, in0=sap, scalar1=scale_c, scalar2=None, op0=ALU.mult),
                         reads=[stg_keys[i]], writes=dst_keys)
                else:
                    P.op("act", lambda e: e.activation(out=dst_ap, in_=sap, func=AF.Copy, scale=scale_c),
                         reads=[stg_keys[i]], writes=dst_keys)
            elif i % 2 == 0 or scale_c != 1.0:
                P.op("dve", lambda e: e.tensor_scalar(out=dst_ap, in0=sap, scalar1=scale_ap, scalar2=scale_c,
                                                      op0=ALU.mult, op1=ALU.mult),
                     reads=[stg_keys[i], "smalls"], writes=dst_keys)
            else:
                P.op("act", lambda e: e.activation(out=dst_ap, in_=sap, func=AF.Copy, scale=scale_ap),
                     reads=[stg_keys[i], "smalls"], writes=dst_keys)

        def conv_chunk(c, slot):
            if c < 11:
                for kc in range(8):
                    conv_piece(w_in[kc * 128:(kc + 1) * 128, CHUNK_COL[c]:CHUNK_COL[c] + 512],
                               wbuf[slot][:, kc * 512:(kc + 1) * 512], [f"w{slot}_{kc}"], smalls[:, kc:kc + 1], 1.0)
            else:
                half = c - 11
                for kk in range(4):
                    kc = half * 4 + kk
                    for hh in range(2):
                        conv_piece(w_out[kc * 128:(kc + 1) * 128, hh * 512:(hh + 1) * 512],
                                   wbuf[slot][:, kk * 1024 + hh * 512: kk * 1024 + (hh + 1) * 512], [f"w{slot}_{kk * 2 + hh}"],
                                   None, 0.5)
            P.op("pool", lambda e: e.dma_start(out=wsc[c], in_=wbuf[slot][:]), reads=wkeys(slot),
                 writes=[f"wsc{c}"], dma=f"wst{slot}")

        def conv_up():
            for kc in range(4):
                for hh in range(2):
                    conv_piece(w_up_a[kc * 128:(kc + 1) * 128, hh * 512:(hh + 1) * 512], wupA[:, kc, hh * 512:(hh + 1) * 512],
                               ["wupA"], smalls[:, 8:9], (1.0 - LAM_INIT) * 0.5)
                    conv_piece(w_up_b[kc * 128:(kc + 1) * 128, hh * 512:(hh + 1) * 512], wupB[:, kc, hh * 512:(hh + 1) * 512],
                               ["wupB"], smalls[:, 9:10], 0.5)

        TILE_SEQ = [CH_VB, CH_ZB, CH_QK, CH_QA, CH_KA, CH_VA, CH_ZA, CH_GA0, CH_GB0, CH_GA1, CH_GB1, CH_WO0, CH_WO1]
        wseq = TILE_SEQ * ntiles
        wstate = {"loaded": 0, "acq": 0}
        slot_base = 0

        def emit_load():
            i = wstate["loaded"]
            if i >= len(wseq):
                return
            c = wseq[i]
            slot = (slot_base + i) % NSLOT
            if i < len(TILE_SEQ):
                conv_chunk(c, slot)
            else:
                P.op("sp", lambda e: e.dma_start(out=wbuf[slot][:], in_=wsc[c]), reads=[f"wsc{c}"], writes=wkeys(slot),
                     dma=f"wld{slot}")
            wstate["loaded"] += 1

        def acquire(c):
            i = wstate["acq"]
            assert wseq[i] == c, (wseq[i], c)
            assert i < wstate["loaded"]
            wstate["acq"] += 1
            return (slot_base + i) % NSLOT

        def release():
            emit_load()

        def xa_buf(s_):
            return xa[s_ % 2], [f"xa{s_ % 2}"]

        def load_xa(t_, s_):
            buf, keys = xa_buf(s_)
            g = 4 * t_ + s_
            P.op("sp", lambda e: e.dma_start(out=buf[:], in_=x[g * 128:(g + 1) * 128, :]), writes=keys, dma=f"xa{s_ % 2}")

        def load_xr(g):
            i = g % 2
            P.op("sp", lambda e: e.dma_start(out=xr[i][:], in_=x[g * 128:(g + 1) * 128, :]),
                 writes=[f"xn{i}", f"xn{i}b"], dma=f"xr{i}")

        HT_KEYS = ["hT0", "hT1", "hT2", "hT3"]

        def fm_matmuls(slot, j, rows=128, lhs_from=None):
            b = alloc_pg()
            for kc in range(8):
                if lhs_from is None:
                    lhsT = wbuf[slot][:, kc * 512 + j * 128: kc * 512 + j * 128 + rows]
                    rk = [f"w{slot}_{kc}"]
                else:
                    lhsT = lhs_from[:, kc, :]
                    rk = ["wlr"]
                P.op("pe", lambda e, lhsT=lhsT, kc=kc: e.matmul(out=pg[b][0:rows, :], lhsT=lhsT, rhs=hT[:, kc, :],
                                                                start=(kc == 0), stop=(kc == 7)),
                     reads=rk + HT_KEYS, writes=[f"pg{b}"])
            return b

        def tm_matmuls(slot, s):
            b = alloc_pg()
            for kc in range(8):
                P.op("pe", lambda e, kc=kc: e.matmul(out=pg[b][:, :], lhsT=hT[:, kc, s * 128:(s + 1) * 128],
                                                     rhs=wbuf[slot][:, kc * 512:(kc + 1) * 512],
                                                     start=(kc == 0), stop=(kc == 7)),
                     reads=[f"w{slot}_{kc}", f"hT{s}"], writes=[f"pg{b}"])
            return b

        def dump(name, src_ap, rkeys, shape):
            if not debug:
                return
            n = 1
            for d_ in shape[1:]:
                n *= d_
            view = dbgbuf[:].rearrange("p a b -> p (a b)")[:, 0:n]
            if len(shape) == 3:
                view = view.rearrange("p (a b) -> p a b", a=shape[1])
            elif len(shape) == 4:
                view = view.rearrange("p (a b c) -> p a b c", a=shape[1], b=shape[2])
            P.op("dve", lambda e: e.tensor_copy(out=view, in_=src_ap), reads=rkeys, writes=["dbgbuf"])
            P.op("sp", lambda e: e.dma_start(out=dbg[name], in_=view), reads=["dbgbuf"], dma="dbg")

        def pow_cols(dst, src, col0, n, rkeys, wkey):
            for q_ in range(n):
                P.op("pool", lambda e, q_=q_: e.tensor_tensor(out=dst[:, col0 + q_:col0 + q_ + 1], in0=src[:, col0 + q_:col0 + q_ + 1],
                                                             in1=lams[:, 5:6], op=ALU.pow),
                     reads=rkeys + ["nhalf"], writes=[f"{wkey}_{q_}"])

        def a_stats(t, s):
            junk = PT2[0][:].rearrange("p a b -> p (a b)")
            g = 4 * t + s
            xbuf, xkeys = xa_buf(s)
            P.op("act", lambda e: e.activation(out=junk, in_=xbuf[:], func=AF.Square, accum_out=ss[:, g:g + 1]),
                 reads=xkeys, writes=["PT0", f"ss{g}"])
            P.op("dve", lambda e: e.tensor_scalar(out=rs[:, g:g + 1], in0=ss[:, g:g + 1], scalar1=1.0 / D, scalar2=EPS,
                                                  op0=ALU.mult, op1=ALU.add), reads=[f"ss{g}"], writes=[f"rs{g}"])
            P.op("pool", lambda e: e.tensor_tensor(out=rstd[:, g:g + 1], in0=rs[:, g:g + 1], in1=lams[:, 5:6], op=ALU.pow),
                 reads=[f"rs{g}", "nhalf"], writes=[f"rstd{g}"])

        def a_norm_tr(t, s):
            g = 4 * t + s
            i = g % 2
            xbuf, xkeys = xa_buf(s)
            P.op("dve", lambda e: e.tensor_scalar(out=hb[i][:], in0=xbuf[:], scalar1=rstd[:, g:g + 1], scalar2=None, op0=ALU.mult),
                 reads=xkeys + [f"rstd{g}"], writes=[f"hbL{i}", f"hbR{i}"])
            if s + 2 < 4:
                load_xa(t, s + 2)
            for kc in range(8):
                P.op("pe", lambda e, kc=kc: e.transpose(out=ptr[:, kc, :], in_=hb[i][:, kc * 128:(kc + 1) * 128], identity=ident[:]),
                     reads=[f"hbL{i}", f"hbR{i}", "ident"], writes=["ptr"])
            if s % 2 == 0:
                P.op("dve", lambda e: e.tensor_copy(out=hT[:, :, s * 128:(s + 1) * 128], in_=ptr[:, :, :]),
                     reads=["ptr"], writes=[f"hT{s}"])
            else:
                P.op("act", lambda e: e.activation(out=hT[:, :, s * 128:(s + 1) * 128], in_=ptr[:, :, :], func=AF.Copy),
                     reads=["ptr"], writes=[f"hT{s}"])

        def phase_a(t, subs):
            for s in subs:
                a_stats(t, s)
            for s in subs:
                a_norm_tr(t, s)
            if debug and t == 0 and 3 in subs:
                dump("d_hT", hT[:], HT_KEYS, [128, 8, 512])

        def phase_b1(t):
            sl_vb = acquire(CH_VB)
            for s in range(2):
                b = tm_matmuls(sl_vb, s)
                P.op("dve", lambda e, s=s, b=b: e.tensor_copy(out=vb[:, s, :], in_=pg[b][:, :]), reads=[f"pg{b}"], writes=[f"vb{s}"])
            b = fm_matmuls(None, 0, rows=16, lhs_from=wlr)
            P.op("dve", lambda e, b=b: e.tensor_copy(out=lrT[:, :], in_=pg[b][0:16, :]), reads=[f"pg{b}"], writes=["lrT"])
            for c in range(2):
                b = alloc_pg()
                P.op("pe", lambda e, c=c, b=b: e.matmul(out=pg[b][:, :], lhsT=walpha[:, c * 128:(c + 1) * 128], rhs=lrT[:, :],
                                                        start=True, stop=True), reads=["walpha", "lrT"], writes=[f"pg{b}"])
                P.op("act", lambda e, c=c, b=b: e.activation(out=gl[:, c, :], in_=pg[b][:, :], func=AF.Exp, scale=-1.0,
                                                             bias=nbal[:, c:c + 1]), reads=[f"pg{b}", "nbal"], writes=[f"gl{c}"])
            for c in range(2):
                P.op("act", lambda e, c=c: e.activation(out=gl[:, c, :], in_=gl[:, c, :], func=AF.Ln, bias=1.0, scale=1.0),
                     reads=[f"gl{c}"], writes=[f"gl{c}"])
                P.op("dve", lambda e, c=c: e.tensor_tensor_scan(out=cum[:, c, :], data0=rmask[:, :], data1=gl[:, c, :], initial=0.0,
                                                                op0=ALU.mult, op1=ALU.add),
                     reads=[f"gl{c}", "rmask"], writes=[f"cum{c}"])
            for c in range(2):
                P.op("act", lambda e, c=c: e.activation(out=eb[:, c, :], in_=cum[:, c, :], func=AF.Exp, scale=-1.0 / 16.0),
                     reads=[f"cum{c}"], writes=[f"tga{c}"])
                P.op("act", lambda e, c=c: e.activation(out=enb[:, c, :], in_=cum[:, c, :], func=AF.Exp, scale=1.0 / 16.0),
                     reads=[f"cum{c}"], writes=[f"cum{c}"])
            if debug and t == 0:
                dump("d_eb", eb[:], ["tga0", "tga1"], [128, 2, 512])
            for s in range(2, 4):
                b = tm_matmuls(sl_vb, s)
                P.op("dve", lambda e, s=s, b=b: e.tensor_copy(out=vb[:, s, :], in_=pg[b][:, :]), reads=[f"pg{b}"], writes=[f"vb{s}"])
            release()
            sl_zb = acquire(CH_ZB)
            for s in range(4):
                b = tm_matmuls(sl_zb, s)
                i = s % 2
                P.op("act", lambda e, i=i, b=b: e.activation(out=ztmp[i], in_=pg[b][:, :], func=AF.Tanh, scale=0.5),
                     reads=[f"pg{b}"], writes=[f"PT{i}"])
                P.op("dve", lambda e, i=i, b=b, s=s: e.scalar_tensor_tensor(out=szB[:, s, :], in0=ztmp[i], scalar=1.0,
                                                                            in1=pg[b][:, :], op0=ALU.add, op1=ALU.mult),
                     reads=[f"pg{b}", f"PT{i}"], writes=[f"szB{s}", f"yT{4 + s}"])
            release()
            sl_qk = acquire(CH_QK)
            for c in range(2):
                b = fm_matmuls(sl_qk, c)
                for hh in range(2):
                    P.op("dve", lambda e, c=c, hh=hh, b=b: e.scalar_tensor_tensor(
                        out=qtT[64 * hh:64 * hh + 64, c, hh, :], in0=pg[b][64 * hh:64 * hh + 64, :], scalar=0.125,
                        in1=eb[64 * hh:64 * hh + 64, c, :], op0=ALU.mult, op1=ALU.mult),
                        reads=[f"pg{b}", f"tga{c}"], writes=[f"qtT{c}"])
            for c in range(2):
                b = fm_matmuls(sl_qk, 2 + c)
                P.op("dve", lambda e, c=c, b=b: e.tensor_tensor(out=ktT[:, c, :], in0=pg[b][:, :], in1=enb[:, c, :], op=ALU.mult),
                     reads=[f"pg{b}", f"cum{c}"], writes=[f"ktT{c}"])
                for s in range(4):
                    P.op("dve", lambda e, c=c, s=s, b=b: e.scalar_tensor_tensor(
                        out=khT[:, c, s * 128:(s + 1) * 128], in0=pg[b][:, s * 128:(s + 1) * 128],
                        scalar=eb[:, c, s * 128 + 127:s * 128 + 128], in1=enb[:, c, s * 128:(s + 1) * 128],
                        op0=ALU.mult, op1=ALU.mult),
                        reads=[f"pg{b}", f"cum{c}", f"tga{c}"], writes=[f"gatedA{c}"])
                P.op("dve", lambda e, c=c: e.tensor_copy(out=eblast[:, c, :],
                                                         in_=eb[:, c, :].rearrange("p (s j) -> p s j", j=128)[:, :, 127]),
                     reads=[f"tga{c}"], writes=[f"eblast{c}"])
            release()
            for s in range(4):
                for c in range(2):
                    P.op("pe", lambda e, s=s, c=c: e.transpose(out=ptr[:, c * 4 + s, :], in_=khT[:, c, s * 128:(s + 1) * 128],
                                                               identity=ident[:]),
                         reads=[f"gatedA{c}", "ident"], writes=["ptr"])
            for c in range(2):
                P.op("dve", lambda e, c=c: e.tensor_copy(out=khat[:, :, c, :], in_=ptr[:, c * 4:(c + 1) * 4, :]),
                     reads=["ptr"], writes=[f"khat{c}"])

        def phase_b2(t):
            sl_qa = acquire(CH_QA)
            for j in range(4):
                b = fm_matmuls(sl_qa, j)
                P.op("dve", lambda e, j=j, b=b: e.tensor_scalar(out=QT[:, j, :], in0=pg[b][:, :], scalar1=0.125, scalar2=None,
                                                                op0=ALU.mult), reads=[f"pg{b}"], writes=[f"QT{j}", "gaT0", "gaT1", "gaT2", "gaT3"])
            release()
            sl_ka = acquire(CH_KA)
            for j in range(4):
                b = fm_matmuls(sl_ka, j)
                P.op("dve", lambda e, j=j, b=b: e.tensor_copy(out=KT[:, j, t * T:(t + 1) * T], in_=pg[b][:, :]),
                     reads=[f"pg{b}"], writes=[f"KT{j}"] + (stg_keys if t == 4 else []))
            release()
            sl_va = acquire(CH_VA)
            for s in range(4):
                b = tm_matmuls(sl_va, s)
                g = 4 * t + s
                for hd in range(4):
                    P.op("dve", lambda e, g=g, b=b, hd=hd: e.tensor_scalar(out=V[:, g, hd, 0:128], in0=pg[b][:, hd * 128:(hd + 1) * 128],
                                                                          scalar1=eft[:, hd:hd + 1], scalar2=None, op0=ALU.mult),
                         reads=[f"pg{b}", "eft"], writes=["V"])
                P.op("dve", lambda e, g=g: e.tensor_copy(out=V[:, g, :, 128:129], in_=eft[:, :].unsqueeze(2)),
                     reads=["eft"], writes=["V"])
            release()
            sl_za = acquire(CH_ZA)
            for s in range(4):
                b = tm_matmuls(sl_za, s)
                i = s % 2
                P.op("act", lambda e, i=i, b=b: e.activation(out=ztmp[i], in_=pg[b][:, :], func=AF.Tanh, scale=0.5),
                     reads=[f"pg{b}"], writes=[f"PT{i}"])
                P.op("dve", lambda e, i=i, b=b, s=s: e.scalar_tensor_tensor(out=szA[:, s, :], in0=ztmp[i], scalar=1.0,
                                                                            in1=pg[b][:, :], op0=ALU.add, op1=ALU.mult),
                     reads=[f"pg{b}", f"PT{i}"], writes=[f"szA{s}", f"yT{s}"])
            release()
            if debug and t == 0:
                dump("d_QT", QT[:], ["QT0", "QT1", "QT2", "QT3"], [128, 4, 512])

        def gla_at(t, s):
            bA = alloc_pg()
            for c in range(2):
                P.op("pe", lambda e, c=c: e.matmul(
                    out=pg[bA][:, c * 256:(c + 1) * 256], lhsT=ktT[:, c, s * 128:(s + 1) * 128],
                    rhs=qtT[:, c, :, s * 128:(s + 1) * 128], start=True, stop=True),
                    reads=[f"ktT{c}", f"qtT{c}"], writes=[f"pg{bA}"])
            P.op("dve", lambda e: e.tensor_tensor(out=ATs[0][:, :], in0=pg[bA][:, :], in1=tri4[:, :], op=ALU.mult),
                 reads=[f"pg{bA}", "tri4"], writes=["ATs0"])

        def gla_rest(t, s):
            bS = alloc_pg()
            for c in range(2):
                P.op("pe", lambda e, c=c: e.matmul(out=pg[bS][:, c * 256:(c + 1) * 256], lhsT=khat[:, s, c, :],
                                                   rhs=vb[:, s, c * 256:(c + 1) * 256], start=True, stop=True),
                     reads=[f"khat{c}", f"vb{s}"], writes=[f"pg{bS}"])
            bO = alloc_pg()
            for hd in range(4):
                c, hh = hd // 2, hd % 2
                P.op("pe", lambda e, c=c, hh=hh, hd=hd: e.matmul(
                    out=pg[bO][:, hd * 128:(hd + 1) * 128], lhsT=qtT[:, c, hh, s * 128:(s + 1) * 128],
                    rhs=stbf[:, c, :], start=True, stop=False),
                    reads=[f"qtT{c}", f"stbf{c}"], writes=[f"pg{bO}"])
                P.op("pe", lambda e, hd=hd: e.matmul(
                    out=pg[bO][:, hd * 128:(hd + 1) * 128], lhsT=ATs[0][:, hd * 128:(hd + 1) * 128],
                    rhs=vb[:, s, hd * 128:(hd + 1) * 128], start=False, stop=True),
                    reads=["ATs0", f"vb{s}"], writes=[f"pg{bO}"])
            for c in range(2):
                for hh in range(2):
                    P.op("dve", lambda e, c=c, hh=hh: e.scalar_tensor_tensor(
                        out=state[64 * hh:64 * hh + 64, c, :], in0=state[64 * hh:64 * hh + 64, c, :],
                        scalar=eblast[64 * hh:64 * hh + 64, c, s:s + 1],
                        in1=pg[bS][64 * hh:64 * hh + 64, c * 256 + hh * 128:c * 256 + (hh + 1) * 128],
                        op0=ALU.mult, op1=ALU.add),
                        reads=[f"pg{bS}", f"eblast{c}", f"state{c}"], writes=[f"state{c}"])
                P.op("pool", lambda e, c=c: e.tensor_copy(out=stbf[:, c, :], in_=state[:, c, :]),
                     reads=[f"state{c}"], writes=[f"stbf{c}"])
            oi = s % 2
            P.op("dve", lambda e: e.tensor_copy(out=gl[:, oi, :], in_=pg[bO][:, :]), reads=[f"pg{bO}"], writes=[f"gl{oi}"])
            if debug and t == 0:
                P.op("dve", lambda e: e.tensor_copy(out=dbgbuf[:, s, :], in_=gl[:, oi, :]), reads=[f"gl{oi}"], writes=["dbgbuf"])

        def gla_out_ew(t, s):
            oi = s % 2
            for hd in range(4):
                P.op("dve", lambda e, hd=hd: e.scalar_tensor_tensor(out=junkb[:, :], in0=gl[:, oi, hd * 128:(hd + 1) * 128], scalar=1.0,
                                                                    in1=gl[:, oi, hd * 128:(hd + 1) * 128], op0=ALU.mult, op1=ALU.mult,
                                                                    accum_out=ssb[:, s * 4 + hd:s * 4 + hd + 1]),
                     reads=[f"gl{oi}"], writes=["junkb", f"ssb{s}_{hd}"])
            P.op("dve", lambda e: e.tensor_scalar(out=rsb[:, s * 4:(s + 1) * 4], in0=ssb[:, s * 4:(s + 1) * 4],
                                                  scalar1=1.0 / 128, scalar2=EPS, op0=ALU.mult, op1=ALU.add),
                 reads=[f"ssb{s}_{hd}" for hd in range(4)], writes=[f"rsb{s}"])
            pow_cols(rstdb, rsb, s * 4, 4, [f"rsb{s}"], f"rstdb{s}")

        def gla_out_ew2(t, s):
            oi = s % 2
            gi = s % 2
            for hd in range(4):
                P.op("dve", lambda e, hd=hd: e.scalar_tensor_tensor(
                    out=gated[gi][:, hd * 128:(hd + 1) * 128], in0=gl[:, oi, hd * 128:(hd + 1) * 128],
                    scalar=rstdb[:, s * 4 + hd:s * 4 + hd + 1], in1=szB[:, s, hd * 128:(hd + 1) * 128],
                    op0=ALU.mult, op1=ALU.mult),
                    reads=[f"gl{oi}", f"rstdb{s}_{hd}", f"szB{s}"], writes=[f"hbL{gi}"])

        def gla_out_pe(t, s):
            gi = s % 2
            for hd in range(4):
                P.op("pe", lambda e, hd=hd: e.transpose(out=ptr[:, hd, :], in_=gated[gi][:, hd * 128:(hd + 1) * 128],
                                                        identity=ident[:]),
                     reads=[f"hbL{gi}", "ident"], writes=["ptr"])
            P.op("dve", lambda e: e.tensor_copy(out=gbT[:, :, s * 128:(s + 1) * 128], in_=ptr[:, 0:4, :]),
                 reads=["ptr"], writes=[f"gbT{s}"])

        def acc_idx(m, a):
            return 2 * a + m

        def acc_ap(m, a, lo, hi):
            i_ = acc_idx(m, a)
            return poa[i_ // 3][:, (i_ % 3) * 130 + lo:(i_ % 3) * 130 + hi]

        pair_ctr = [0]

        def alloc_pair():
            bp = 2 * (pair_ctr[0] % 2)
            pair_ctr[0] += 1
            return bp

        def attention_jobs(t, h, deferred):
            nkb = 4 * t + 4

            def emit_qk_pair(kb):
                r = kb - 4 * t
                bp = alloc_pair()
                c0 = 0 if r < 0 else 128 * r
                for m in range(2):
                    b = bp + m
                    lhsT = KT[64 * m:64 * m + 64, h, kb * 128:(kb + 1) * 128]
                    if r < 0:
                        P.op("pe", lambda e, b=b, lhsT=lhsT, m=m: e.matmul(out=pg[b][:, :], lhsT=lhsT, rhs=QT[64 * m:64 * m + 64, h, :],
                                                                           start=True, stop=True),
                             reads=[f"KT{h}", f"QT{h}"], writes=[f"pg{b}"])
                    else:
                        P.op("pe", lambda e, b=b, lhsT=lhsT, m=m: e.matmul(out=pg[b][:, c0:c0 + 128], lhsT=lhsT,
                                                                           rhs=QT[64 * m:64 * m + 64, h, c0:c0 + 128],
                                                                           start=True, stop=False),
                             reads=[f"KT{h}", f"QT{h}"], writes=[f"pg{b}"])
                        P.op("pe", lambda e, b=b: e.matmul(out=pg[b][:, c0:c0 + 128], lhsT=ident[:, :], rhs=negm[:, :],
                                                           start=False, stop=True),
                             reads=["ident", "negm"], writes=[f"pg{b}"])
                        if c0 + 128 < 512:
                            P.op("pe", lambda e, b=b, lhsT=lhsT, m=m: e.matmul(out=pg[b][:, c0 + 128:512], lhsT=lhsT,
                                                                               rhs=QT[64 * m:64 * m + 64, h, c0 + 128:512],
                                                                               start=True, stop=True),
                                 reads=[f"KT{h}", f"QT{h}"], writes=[f"pg{b}"])
                pp = kb % 2
                cbias = SLOPES[h] * (128.0 * r - 129.0)
                P.op("act", lambda e: e.activation(out=PT2[pp][:, :, c0:512], in_=pg4[:, bp:bp + 2, c0:512], func=AF.Exp,
                                                   bias=cbias, scale=1.0),
                     reads=[f"pg{bp}", f"pg{bp + 1}"], writes=[f"PT{pp}"])

            def emit_pv(kb, m):
                r = kb - 4 * t
                pp = kb % 2
                for a in range(max(r, 0), 4):
                    i_ = acc_idx(m, a)
                    P.op("pe", lambda e, a=a, i_=i_: e.matmul(out=acc_ap(m, a, 0, 129), lhsT=PT2[pp][:, m, a * 128:(a + 1) * 128],
                                                              rhs=V[:, kb, h, 0:129], start=(kb == 0 and i_ in (0, 4, 6)), stop=False,
                                                              skip_group_check=True),
                         reads=[f"PT{pp}", "V", "Vones"], writes=[f"poa{i_ // 3}"])

            for kb in range(nkb + 1):
                if kb < nkb:
                    emit_qk_pair(kb)
                if kb >= 1:
                    emit_pv(kb - 1, 0)
                    emit_pv(kb - 1, 1)
                    r_done = (kb - 1) - 4 * t
                    if r_done >= 1:
                        attention_evac(t, h, r_done)
                if t == 0:
                    if kb == 1:
                        for kk in sorted(deferred):
                            for f in deferred[kk]:
                                f()
                else:
                    for f in deferred.get(kb, ()):
                        f()

        def attention_evac(t, h, stage):
            bank = stage - 1
            n = 3 if bank < 2 else 2
            i0 = 3 * bank
            zs = poa[bank][:, 0:n * 130].rearrange("p (i c) -> p i c", c=130)[:, :, 128]
            P.op("dve", lambda e: e.reciprocal(out=rz[:, i0:i0 + n], in_=zs),
                 reads=[f"poa{bank}"], writes=[f"rz{i0 + k}" for k in range(n)])
            for i_ in range(i0, i0 + n):
                if i_ % 2 == 1:
                    P.op("dve", lambda e, i_=i_: e.tensor_scalar(out=rz[:, i_:i_ + 1], in0=rz[:, i_:i_ + 1], scalar1=lams[:, 4:5],
                                                                 scalar2=None, op0=ALU.mult),
                         reads=[f"rz{i_}", "nlam"], writes=[f"rz{i_}"])

            def part0(a):
                i_ = acc_idx(0, a)
                oi2 = a % 2
                P.op("dve", lambda e: e.tensor_scalar(out=otmp[oi2][:, :], in0=acc_ap(0, a, 0, 128), scalar1=rz[:, i_:i_ + 1],
                                                      scalar2=None, op0=ALU.mult),
                     reads=[f"poa{i_ // 3}", f"rz{i_}"], writes=[f"otmp{oi2}"])

            def part1(a):
                i_ = acc_idx(1, a)
                oi2 = a % 2
                P.op("dve", lambda e: e.scalar_tensor_tensor(out=oaf[a][:, :], in0=acc_ap(1, a, 0, 128), scalar=rz[:, i_:i_ + 1],
                                                             in1=otmp[oi2][:, :], op0=ALU.mult, op1=ALU.add),
                     reads=[f"poa{i_ // 3}", f"rz{i_}", f"otmp{oi2}"], writes=[f"oaf{a}"])
                if debug and t == 0:
                    P.op("dve", lambda e: e.tensor_copy(out=dbgoa[:, a, h, :], in_=oaf[a][:, :]), reads=[f"oaf{a}"], writes=["dbgoa"])

            if stage == 1:
                part0(0); part1(0); part0(1)
            elif stage == 2:
                part1(1); part0(2); part1(2)
            else:
                part0(3); part1(3)

        def attention_norm(t, h):
            for a in range(4):
                ci = a * 4 + h
                P.op("dve", lambda e, a=a, ci=ci: e.scalar_tensor_tensor(out=junkb[:, :], in0=oaf[a][:, :], scalar=1.0, in1=oaf[a][:, :],
                                                                         op0=ALU.mult, op1=ALU.mult, accum_out=ssa[:, ci:ci + 1]),
                     reads=[f"oaf{a}"], writes=["junkb", f"ssa{ci}"])
            for a in range(4):
                ci = a * 4 + h
                P.op("dve", lambda e, ci=ci: e.tensor_scalar(out=rsa[:, ci:ci + 1], in0=ssa[:, ci:ci + 1], scalar1=1.0 / 128, scalar2=EPS,
                                                            op0=ALU.mult, op1=ALU.add), reads=[f"ssa{ci}"], writes=[f"rsa{ci}"])
                P.op("pool", lambda e, ci=ci: e.tensor_tensor(out=rstda[:, ci:ci + 1], in0=rsa[:, ci:ci + 1], in1=lams[:, 5:6], op=ALU.pow),
                     reads=[f"rsa{ci}", "nhalf"], writes=[f"rstda{ci}"])

        def attention_norm2(t, h):
            for a in range(4):
                ci = a * 4 + h
                P.op("dve", lambda e, a=a, ci=ci: e.scalar_tensor_tensor(
                    out=gatedA[:, a, h * 128:(h + 1) * 128], in0=oaf[a][:, :], scalar=rstda[:, ci:ci + 1],
                    in1=szA[:, a, h * 128:(h + 1) * 128], op0=ALU.mult, op1=ALU.mult),
                    reads=[f"oaf{a}", f"rstda{ci}", f"szA{a}"], writes=[f"gatedA{a}"])

        def attn_post(t):
            for a in range(4):
                for hd in range(4):
                    P.op("pe", lambda e, a=a, hd=hd: e.transpose(out=ptr[:, 4 + hd, :], in_=gatedA[:, a, hd * 128:(hd + 1) * 128],
                                                                 identity=ident[:]),
                         reads=[f"gatedA{a}", "ident"], writes=["ptr"])
                P.op("dve", lambda e, a=a: e.tensor_copy(out=gaT[:, :, a * 128:(a + 1) * 128], in_=ptr[:, 4:8, :]),
                     reads=["ptr"], writes=[f"gaT{a}", "QT0", "QT1", "QT2", "QT3"])
            if debug and t == 0:
                dump("d_gaT", gaT[:], ["gaT0", "gaT1", "gaT2", "gaT3"], [128, 4, 512])
                dump("d_gbT", gbT[:], ["gbT0", "gbT1", "gbT2", "gbT3"], [128, 4, 512])

        def phase_e(t, mid):
            gaT_keys = ["gaT0", "gaT1", "gaT2", "gaT3"]
            gbT_keys = ["gbT0", "gbT1", "gbT2", "gbT3"]
            ytmp = [gl[:, 0, :], gl[:, 1, :]]
            ytk = ["gl0", "gl1"]
            sl_ga0 = acquire(CH_GA0)
            sl_gb0 = acquire(CH_GB0)
            sl_ga1 = sl_gb1 = None
            for j in range(8):
                if j == 4:
                    release()
                    release()
                    sl_ga1 = acquire(CH_GA1)
                    sl_gb1 = acquire(CH_GB1)
                sga = sl_ga0 if j < 4 else sl_ga1
                sgb = sl_gb0 if j < 4 else sl_gb1
                jj = j % 4
                ti = j % 2
                bga = fm_matmuls(sga, jj)
                P.op("act", lambda e, ti=ti, bga=bga: e.activation(out=tga[ti], in_=pg[bga][:, :], func=AF.Tanh, scale=0.5),
                     reads=[f"pg{bga}"], writes=[f"tga{ti}"])
                bgb = fm_matmuls(sgb, jj)
                P.op("act", lambda e, ti=ti, bgb=bgb: e.activation(out=tgb[ti][:, :], in_=pg[bgb][:, :], func=AF.Tanh, scale=0.5),
                     reads=[f"pg{bgb}"], writes=["tgb0"])
                if j == 0:
                    mid()
                bya = alloc_pg()
                for kc in range(4):
                    P.op("pe", lambda e, kc=kc, j=j, bya=bya: e.matmul(out=pg[bya][:, :], lhsT=wupA[:, kc, j * 128:(j + 1) * 128],
                                                                      rhs=gaT[:, kc, :], start=(kc == 0), stop=(kc == 3)),
                         reads=["wupA"] + gaT_keys, writes=[f"pg{bya}"])
                P.op("dve", lambda e, ti=ti, bya=bya: e.scalar_tensor_tensor(out=ytmp[ti], in0=tga[ti], scalar=1.0, in1=pg[bya][:, :],
                                                                             op0=ALU.add, op1=ALU.mult),
                     reads=[f"pg{bya}", f"tga{ti}"], writes=[ytk[ti]])
                byb = alloc_pg()
                for kc in range(4):
                    P.op("pe", lambda e, kc=kc, j=j, byb=byb: e.matmul(out=pg[byb][:, :], lhsT=wupB[:, kc, j * 128:(j + 1) * 128],
                                                                      rhs=gbT[:, kc, :], start=(kc == 0), stop=(kc == 3)),
                         reads=["wupB"] + gbT_keys, writes=[f"pg{byb}"])
                P.op("dve", lambda e, ti=ti, byb=byb: e.scalar_tensor_tensor(out=tgb[ti][:, :], in0=tgb[ti][:, :], scalar=1.0, in1=pg[byb][:, :],
                                                                             op0=ALU.add, op1=ALU.mult),
                     reads=[f"pg{byb}", "tgb0"], writes=["tgb0"])
                P.op("dve", lambda e, ti=ti, j=j: e.tensor_tensor(out=yT[:, j, :], in0=ytmp[ti], in1=tgb[ti][:, :], op=ALU.add),
                     reads=[ytk[ti], "tgb0"], writes=[f"yT{j}", (f"szA{j}" if j < 4 else f"szB{j - 4}")])
            release()
            release()
            if debug and t == 0:
                dump("d_yT", yT[:], [f"yT{j}" for j in range(8)], [128, 8, 512])

        def phase_f(t, hooks):
            sl_wo0 = acquire(CH_WO0)
            sl_wo1 = acquire(CH_WO1)
            zg = [cum, gl]
            zgk = [["cum0", "cum1"], ["gl0", "gl1"]]
            load_xr(4 * t)
            load_xr(4 * t + 1)

            def zbank(s, hh):
                i = 2 * s + hh
                if 4 <= i < 7:
                    return poa[i - 4][:, :], f"poa{i - 4}"
                b = alloc_pg()
                return pg[b], f"pg{b}"

            def f_mm(s):
                g = 4 * t + s
                for hh in range(2):
                    zb_, zk_ = zbank(s, hh)
                    for kc in range(8):
                        slw = sl_wo0 if kc < 4 else sl_wo1
                        kk = kc % 4
                        P.op("pe", lambda e, kc=kc, kk=kk, slw=slw, hh=hh, zb_=zb_: e.matmul(
                            out=zb_, lhsT=yT[:, kc, s * 128:(s + 1) * 128],
                            rhs=wbuf[slw][:, kk * 1024 + hh * 512: kk * 1024 + (hh + 1) * 512], start=(kc == 0), stop=(kc == 7)),
                            reads=[f"yT{kc}", f"w{slw}_{kk * 2 + hh}"], writes=[zk_])
                    P.op("act", lambda e, zb_=zb_, hh=hh: e.activation(out=ttmp[hh], in_=zb_, func=AF.Square,
                                                                      accum_out=ssz[:, 2 * g + hh:2 * g + hh + 1]),
                         reads=[zk_], writes=[f"tga{hh}", f"ssz{g}_{hh}"])
                    P.op("dve", lambda e, zb_=zb_, hh=hh: e.tensor_tensor(out=zg[s % 2][:, hh, :], in0=zb_,
                                                                         in1=gpost[:, hh * 512:(hh + 1) * 512], op=ALU.mult),
                         reads=[zk_, "gpost", f"ssz{g}_{hh}"], writes=[zgk[s % 2][hh]])
                P.op("dve", lambda e: e.tensor_tensor(out=rsz[:, g:g + 1], in0=ssz[:, 2 * g:2 * g + 1], in1=ssz[:, 2 * g + 1:2 * g + 2],
                                                      op=ALU.add), reads=[f"ssz{g}_0", f"ssz{g}_1"], writes=[f"rsz{g}"])
                P.op("dve", lambda e: e.tensor_scalar(out=rsz[:, g:g + 1], in0=rsz[:, g:g + 1], scalar1=1.0 / D, scalar2=EPS,
                                                      op0=ALU.mult, op1=ALU.add), reads=[f"rsz{g}"], writes=[f"rsz{g}"])
                P.op("pool", lambda e: e.tensor_tensor(out=rstdz[:, g:g + 1], in0=rsz[:, g:g + 1], in1=lams[:, 5:6], op=ALU.pow),
                     reads=[f"rsz{g}", "nhalf"], writes=[f"rstdz{g}"])

            def f_fin(s):
                g = 4 * t + s
                ri = g % 2
                for hh in range(2):
                    xk = f"xn{ri}" if hh == 0 else f"xn{ri}b"
                    P.op("dve", lambda e, hh=hh: e.scalar_tensor_tensor(out=xr[ri][:, hh * 512:(hh + 1) * 512], in0=zg[s % 2][:, hh, :],
                                                                        scalar=rstdz[:, g:g + 1], in1=xr[ri][:, hh * 512:(hh + 1) * 512],
                                                                        op0=ALU.mult, op1=ALU.add),
                         reads=[zgk[s % 2][hh], f"rstdz{g}", xk], writes=[xk])
                P.op("pool", lambda e: e.dma_start(out=out[g * 128:(g + 1) * 128, :], in_=xr[ri][:]),
                     reads=[f"xn{ri}", f"xn{ri}b"], dma=f"st{ri}")
                if s + 2 < 4:
                    load_xr(g + 2)

            steps = [("mm", 0), ("mm", 1), ("fin", 0), ("mm", 2), ("fin", 1), ("mm", 3), ("fin", 2), ("fin", 3)]
            for kind, s_ in steps:
                (f_mm if kind == "mm" else f_fin)(s_)
                for f in hooks.get((kind, s_), ()):
                    f()
            release()
            release()

        load_xa(0, 0)
        load_xa(0, 1)
        for t in range(ntiles):
            if stop_after == "prologue":
                break
            if t == 0:
                phase_a(t, [0, 1])
                phase_a(t, [2, 3])
                for _ in range(NSLOT):
                    emit_load()
            if stop_after == "a":
                break
            phase_b1(t)
            if stop_after == "b1":
                break
            phase_b2(t)
            if stop_after == "b2":
                break
            if t == 0:
                conv_up()
            gla_at(t, 0)
            gla_rest(t, 0)
            deferred = {1: [lambda: gla_at(t, 1), lambda: gla_rest(t, 1)], 2: [lambda: gla_out_ew(t, 0)],
                        3: [lambda: gla_out_ew2(t, 0)], 4: [lambda: gla_out_pe(t, 0)]}
            for s in range(4):
                attention_jobs(t, s, deferred)
                deferred = {1: [], 2: [lambda s=s: attention_norm(t, s)], 3: [lambda s=s: attention_norm2(t, s)], 4: []}
                if s < 3:
                    deferred[2].append(lambda s=s: gla_out_ew(t, s + 1))
                    deferred[3].append(lambda s=s: gla_out_ew2(t, s + 1))
                    deferred[4].append(lambda s=s: gla_out_pe(t, s + 1))
                if s < 2:
                    deferred[1].append(lambda s=s: gla_at(t, s + 2))
                    deferred[1].append(lambda s=s: gla_rest(t, s + 2))
            for kk in (1, 2, 3, 4):
                for f in deferred[kk]:
                    f()
            if debug and t == 0:
                P.op("sp", lambda e: e.dma_start(out=dbg["d_ob"], in_=dbgbuf[:, 0:4, :]), reads=["dbgbuf"], dma="dbg")
                dump("d_oacc", dbgoa[:], ["dbgoa"], [128, 4, 4, 128])
                dump("d_state", state[:], ["state0", "state1"], [128, 2, 128])
            if t + 1 < ntiles:
                load_xa(t + 1, 0)
                load_xa(t + 1, 1)
                a_stats(t + 1, 0)
                a_stats(t + 1, 1)
            phase_e(t, lambda: attn_post(t))
            if t + 1 < ntiles:
                nt_ = t + 1
                hooks = {("mm", 2): [lambda: a_norm_tr(nt_, 0)], ("mm", 3): [lambda: a_norm_tr(nt_, 1), lambda: a_stats(nt_, 2)],
                         ("fin", 2): [lambda: a_norm_tr(nt_, 2), lambda: a_stats(nt_, 3)],
                         ("fin", 3): [lambda: a_norm_tr(nt_, 3)]}
            else:
                hooks = {}
            phase_f(t, hooks)
        P.op("sp", lambda e: None, reads=[], writes=["xn0", "xn0b", "xn1", "xn1b"] + (["dbgbuf"] if debug else []))
        P.emit({"pe": block.tensor, "act": block.scalar, "dve": block.vector, "pool": block.gpsimd, "sp": block.sync}, sems)
    return nc


def _host_consts():
    bf = ml_dtypes.bfloat16
    k = np.arange(128)[:, None]
    q = np.arange(128)[None, :]
    tri = (q >= k).astype(np.float32)
    negm = np.where(k > q, NEG, 0.0).astype(np.float32)
    btab = np.zeros((128, 128), np.float32)
    for h in range(4):
        for i in range(32):
            btab[:, h * 32 + i] = SLOPES[h] * (np.arange(128) + 128.0 * (i - 28) - 256.0)
    rmask = np.ones((128, 512), np.float32)
    rmask[:, ::128] = 0.0
    return {
        "ident": np.eye(128, dtype=np.float32).astype(bf),
        "tri4": np.tile(tri, (1, 4)).astype(bf),
        "negmask": negm.astype(bf),
        "biastab": btab,
        "eftab": np.stack([np.exp(SLOPES[h] * (np.arange(128) - 127.0)) for h in range(4)], axis=1).astype(np.float32),
        "resetmask": rmask.astype(bf),
    }


_CACHE = {}


def kernel(x, g_pre, w_in, lam_q1, lam_k1, lam_q2, lam_k2, g_sub_a, w_alpha, b_alpha, g_sub_b, w_up_a, w_up_b, w_out, g_post,
           _ntiles=NT, _debug=False, _cores=8, _stop=None):
    f = np.float32
    x = np.asarray(x, f)
    key = (_ntiles, _debug, _stop)
    if key not in _CACHE:
        _CACHE[key] = _build(_ntiles, _debug, _stop)
    nc = _CACHE[key]
    smalls = np.zeros((128, 12), f)
    smalls[:, 0:8] = np.asarray(g_pre, f)[0].reshape(8, 128).T
    smalls[:, 8] = np.asarray(g_sub_a, f)[0]
    smalls[:, 9] = np.asarray(g_sub_b, f)[0]
    smalls[:, 10:12] = np.asarray(b_alpha, f)[0].reshape(2, 128).T
    lamv = np.concatenate([np.asarray(v, f)[0] for v in (lam_q1, lam_k1, lam_q2, lam_k2)])[None, :].repeat(128, 0)
    shared = {
        "w_in": np.ascontiguousarray(np.asarray(w_in, f)[0]),
        "w_up_a": np.ascontiguousarray(np.asarray(w_up_a, f)[0]),
        "w_up_b": np.ascontiguousarray(np.asarray(w_up_b, f)[0]),
        "w_out": np.ascontiguousarray(np.asarray(w_out, f)[0]),
        "w_alpha": np.ascontiguousarray(np.asarray(w_alpha, f)[0]),
        "smalls": smalls,
        "lamv": np.ascontiguousarray(lamv),
        "gpost": np.ascontiguousarray(np.asarray(g_post, f)[0][None, :].repeat(128, 0)),
    }
    shared.update(_host_consts())
    in_maps = [dict(shared, x=np.ascontiguousarray(x[b])) for b in range(_cores)]
    res = run_bass_kernel_spmd(nc, in_maps, core_ids=list(range(_cores)))
    if _debug:
        return res
    return np.stack([res.results[b]["out"] for b in range(_cores)], axis=0).astype(np.float32)
```

```python
import math
from contextlib import ExitStack

import numpy as np
import ml_dtypes

import concourse.bass as bass
import concourse.mybir as mybir
from concourse.bass_utils import run_bass_kernel_spmd

F32 = mybir.dt.float32
BF16 = mybir.dt.bfloat16
AF = mybir.ActivationFunctionType
ALU = mybir.AluOpType
AX = mybir.AxisListType

S = 4096
D = 1024
T = 512
NT = S // T
DIN = 5648
EPS = 1e-6
LAM_INIT = 0.8 - 0.6 * math.exp(-0.3 * 0)
SLOPES = [2.0 ** (-8.0 * (h + 1) / 4) for h in range(4)]
C_QA, C_KA, C_VA, C_ZA, C_QB, C_KB, C_VB, C_ZB, C_LR, C_GA, C_GB = 0, 512, 1024, 1536, 2048, 2304, 2560, 3072, 3584, 3600, 4624
CHUNK_COL = [C_QA, C_KA, C_VA, C_ZA, C_QB, C_VB, C_ZB, C_GA, C_GA + 512, C_GB, C_GB + 512]
CH_QA, CH_KA, CH_VA, CH_ZA, CH_QK, CH_VB, CH_ZB, CH_GA0, CH_GA1, CH_GB0, CH_GB1, CH_WO0, CH_WO1 = range(13)
NEG = -30000.0
NSLOT = 4

ENGS = ("pe", "act", "dve", "pool", "sp")


class _Op:
    __slots__ = ("eng", "fn", "deps", "is_dma", "sem", "val", "needs_inc")

    def __init__(self, eng, fn, is_dma):
        self.eng = eng
        self.fn = fn
        self.deps = []
        self.is_dma = is_dma
        self.sem = None
        self.val = 0
        self.needs_inc = is_dma


class _Prog:
    def __init__(self, same_eng_sync=("act", "dve", "pool")):
        self.ops = []
        self.last_writer = {}
        self.readers = {}
        self.same_eng_sync = set(same_eng_sync)
        self.dma_slots = []

    def op(self, eng, fn, reads=(), writes=(), dma=None):
        o = _Op(eng, fn, dma is not None)
        deps = {}
        for k in reads:
            w = self.last_writer.get(k)
            if w is not None:
                deps[id(w)] = w
        for k in writes:
            w = self.last_writer.get(k)
            if w is not None:
                deps[id(w)] = w
            for r in self.readers.get(k, ()):
                deps[id(r)] = r
        o.deps = list(deps.values())
        for k in reads:
            self.readers.setdefault(k, []).append(o)
        for k in writes:
            self.last_writer[k] = o
            self.readers[k] = []
        if dma is not None:
            o.sem = dma
            if dma not in self.dma_slots:
                self.dma_slots.append(dma)
        self.ops.append(o)
        return o

    def emit(self, block_engines, sems):
        ops = self.ops
        for o in ops:
            for d in o.deps:
                if d.is_dma:
                    continue
                if d.eng != o.eng or (d.eng in self.same_eng_sync):
                    d.needs_inc = True
        cnt = {e: 0 for e in ENGS}
        dcnt = {}
        for o in ops:
            if o.is_dma:
                slot = o.sem
                dcnt[slot] = dcnt.get(slot, 0) + 16
                o.sem = sems[slot]
                o.val = dcnt[slot]
            elif o.needs_inc:
                cnt[o.eng] += 1
                o.sem = sems[o.eng]
                o.val = cnt[o.eng]
        same = self.same_eng_sync

        def make(eng_name):
            my_ops = [o for o in ops if o.eng == eng_name]

            def body(e):
                waited = {}
                for o in my_ops:
                    need = {}
                    for d in o.deps:
                        if (not d.is_dma) and d.eng == eng_name and eng_name not in same:
                            continue
                        key = id(d.sem)
                        if d.val > need.get(key, (None, 0))[1]:
                            need[key] = (d.sem, d.val)
                    for key, (s, v) in need.items():
                        if waited.get(key, 0) >= v:
                            continue
                        e.wait_ge(s, v)
                        waited[key] = v
                    ins = o.fn(e)
                    if o.needs_inc and ins is not None:
                        ins.then_inc(o.sem, 16 if o.is_dma else 1)

            return body

        for eng_name, deco in block_engines.items():
            deco(make(eng_name))


def _build(ntiles=NT, debug=False, stop_after=None):
    nc = bass.Bass("TRN2", target_bir_lowering=False)

    def din(name, shape, dt=F32):
        return nc.dram_tensor(name, shape, dt, kind="ExternalInput").ap()

    x = din("x", [S, D])
    w_in = din("w_in", [D, DIN])
    w_up_a = din("w_up_a", [512, D])
    w_up_b = din("w_up_b", [512, D])
    w_out = din("w_out", [D, D])
    w_alpha = din("w_alpha", [16, 256])
    smalls_d = din("smalls", [128, 12])
    lamv_d = din("lamv", [128, 256])
    gpost_d = din("gpost", [128, D])
    ident_d = din("ident", [128, 128], BF16)
    tri_d = din("tri4", [128, 512], BF16)
    negm_d = din("negmask", [128, 128], BF16)
    btab_d = din("biastab", [128, 128])
    ef_d = din("eftab", [128, 4])
    rmask_d = din("resetmask", [128, 512], BF16)
    out = nc.dram_tensor("out", [S, D], F32, kind="ExternalOutput").ap()
    wsc = nc.dram_tensor("wsc", [13, 128, 4096], BF16).ap()
    dbg = {}
    if debug:
        for nm, shp in (("d_hT", [128, 8, 512]), ("d_QT", [128, 4, 512]), ("d_oacc", [128, 4, 4, 128]),
                        ("d_ob", [128, 4, 512]), ("d_yT", [128, 8, 512]), ("d_gaT", [128, 4, 512]),
                        ("d_gbT", [128, 4, 512]),
                        ("d_eb", [128, 2, 512]), ("d_state", [128, 2, 128])):
            dbg[nm] = nc.dram_tensor(nm, shp, F32, kind="ExternalOutput").ap()

    P = _Prog()
    with ExitStack() as es:
        def SB(name, shape, dt):
            return es.enter_context(nc.sbuf_tensor("sb_" + name, shape, dt))

        def PSM(name, shape, dt):
            return es.enter_context(nc.psum_tensor("ps_" + name, shape, dt))

        KT = SB("KT", [128, 4, S], BF16)
        V = SB("V", [128, 4 * ntiles, 4, 130], BF16)
        wupA = SB("wupA", [128, 4, 1024], BF16)
        wupB = SB("wupB", [128, 4, 1024], BF16)
        wbuf = [SB(f"wbuf{i}", [128, 4096], BF16) for i in range(NSLOT)]
        wlr = SB("wlr", [128, 8, 16], BF16)
        walpha = SB("walpha", [16, 256], BF16)
        xn = [SB(f"xn{i}", [128, 1024], F32) for i in range(2)]
        xr = xn
        xa = [SB(f"xa{i}", [128, 1024], F32) for i in range(2)]
        hb = [SB(f"hb{i}", [128, 1024], BF16) for i in range(2)]
        hT = SB("hT", [128, 8, T], BF16)
        gated = [hb[i][:, 0:512] for i in range(2)]
        obf = [hb[i][:, 512:1024] for i in range(2)]
        QT = SB("QT", [128, 4, T], BF16)
        yT = SB("yT", [128, 8, T], BF16)
        szA = yT[:, 0:4, :]
        szB = yT[:, 4:8, :]
        gbT = SB("gbT", [128, 4, T], BF16)
        gl = SB("gl", [128, 2, T], F32)
        sq = gl[:, 0, :]
        lamv = gl[:, 0, 0:256]
        lamt = gl[:, 0, 256:384]
        wlr_st = gl[:, 1, 0:128].rearrange("p (kc c) -> p kc c", kc=8)
        cum = SB("cum", [128, 2, T], F32)
        walpha_st = cum[0:16, 0, 0:256]
        enb = cum
        qtT = SB("qtT", [128, 2, 2, T], BF16)
        ktT = SB("ktT", [128, 2, T], BF16)
        khat = SB("khat", [128, 4, 2, 128], BF16)
        vb = SB("vb", [128, 4, 512], BF16)
        lrT = SB("lrT", [16, T], BF16)
        state = SB("state", [128, 2, 128], F32)
        stbf = SB("stbf", [128, 2, 128], BF16)
        PT2 = [SB(f"PT{i}", [128, 2, 512], BF16) for i in range(2)]
        gatedA = SB("gatedA", [128, 4, 512], BF16)
        khT = gatedA[:, 0:2, :]
        oaf = [SB(f"oaf{i}", [128, 128], F32) for i in range(4)]
        junkb = SB("junkb", [128, 128], BF16)
        otmp = [SB(f"otmp{i}", [128, 128], F32) for i in range(2)]
        gaT = QT
        eb = SB("eb", [128, 2, T], F32)
        tga = [eb[:, 0, :], eb[:, 1, :]]
        ttmp = tga
        ztmp = [PT2[i][:].rearrange("p a b -> p (a b)").bitcast(F32) for i in range(2)]
        eblast = SB("eblast", [128, 2, 4], F32)
        tgb = [SB("tgb0", [128, 512], F32)] * 2
        ATs = [SB("ATs0", [128, 512], BF16)] * 2
        ident = SB("ident", [128, 128], BF16)
        tri4 = SB("tri4", [128, 512], BF16)
        negm = SB("negm", [128, 128], BF16)
        btab = SB("btab", [128, 128], F32)
        eft = SB("eft", [128, 4], F32)
        rmask = SB("rmask", [128, 512], BF16)
        gpost = SB("gpost", [128, D], F32)
        smalls = SB("smalls", [128, 12], F32)
        lams = SB("lams", [128, 8], F32)
        nbal = SB("nbal", [128, 2], F32)
        ss = SB("ss", [128, 32], F32)
        rs = SB("rs", [128, 32], F32)
        rstd = SB("rstd", [128, 32], F32)
        ssa = SB("ssa", [128, 16], F32)
        rsa = SB("rsa", [128, 16], F32)
        rstda = SB("rstda", [128, 16], F32)
        ssb = SB("ssb", [128, 16], F32)
        rsb = SB("rsb", [128, 16], F32)
        rstdb = SB("rstdb", [128, 16], F32)
        ssz = SB("ssz", [128, 64], F32)
        rsz = SB("rsz", [128, 32], F32)
        rstdz = SB("rstdz", [128, 32], F32)
        rz = SB("rz", [128, 8], F32)
        dbgbuf = SB("dbgbuf", [128, 8, 512], F32) if debug else None
        dbgoa = SB("dbgoa", [128, 4, 4, 128], F32) if debug else None
        pg4 = PSM("pg4", [128, 4, 512], F32)
        pg = [pg4[:, i, :] for i in range(4)]
        ptr = PSM("ptr", [128, 8, 128], BF16)
        poa = [PSM(f"poa{i}", [128, 512], F32) for i in range(3)]

        dma_names = (["stg%d" % i for i in range(8)] + ["wst%d" % i for i in range(NSLOT)] + ["wld%d" % i for i in range(NSLOT)]
                     + ["cst", "xn0", "xn1", "xr0", "xr1", "xa0", "xa1", "st0", "st1", "dbg"])
        sems = {}
        for nm in list(ENGS) + dma_names:
            sems[nm] = es.enter_context(nc.semaphore("s_" + nm))
        _build.sbuf_left = nc.sbuf_bytes_remaining
        block = es.enter_context(nc.Block())

        gctr = [0]

        def alloc_pg():
            b = gctr[0] % 4
            gctr[0] += 1
            return b

        def wkeys(slot):
            return [f"w{slot}_{i}" for i in range(8)]

        def ev_engine(i):
            return "dve"

        for dst, src, key in ((ident, ident_d, "ident"), (tri4, tri_d, "tri4"), (negm, negm_d, "negm"),
                              (btab, btab_d, "btab"), (eft, ef_d, "eft"), (rmask, rmask_d, "rmask"), (gpost, gpost_d, "gpost"),
                              (smalls, smalls_d, "smalls"), (lamv, lamv_d, "gl0"), (walpha_st, w_alpha, "cum0")):
            P.op("sp", lambda e, dst=dst, src=src: e.dma_start(out=dst[:], in_=src), writes=[key], dma="cst")
        P.op("sp", lambda e: e.dma_start(out=wlr_st[:], in_=w_in[:, C_LR:C_LR + 16].rearrange("(kc p) c -> p kc c", p=128)),
             writes=["gl1"], dma="cst")
        _last_c = P.ops[-1]
        for _k in ("ident", "tri4", "negm", "btab", "eft", "rmask", "gpost", "smalls", "gl0", "cum0", "gl1"):
            P.last_writer[_k] = _last_c
        P.op("pool", lambda e: e.memset(lams[:, 5:6], -0.5), writes=["nhalf"])
        P.op("pool", lambda e: e.memset(V[:, :, :, 128:130], 1.0), writes=["Vones"])
        P.op("pool", lambda e: e.memset(state[:], 0.0), writes=["state0", "state1"])
        P.op("pool", lambda e: e.memset(qtT[:], 0.0), writes=["qtT0", "qtT1"])
        P.op("pool", lambda e: e.memset(stbf[:], 0.0), writes=["stbf0", "stbf1"])
        P.op("dve", lambda e: e.tensor_tensor(out=lamt[:, 0:64], in0=lamv[:, 0:64], in1=lamv[:, 64:128], op=ALU.mult),
             reads=[], writes=["gl0"])
        P.op("dve", lambda e: e.tensor_tensor(out=lamt[:, 64:128], in0=lamv[:, 128:192], in1=lamv[:, 192:256], op=ALU.mult),
             reads=[], writes=["gl0"])
        P.op("dve", lambda e: e.reduce_sum(out=lams[:, 0:2], in_=lamt[:].rearrange("p (a b) -> p a b", a=2), axis=AX.X),
             reads=["gl0"], writes=["lams01"])
        P.op("act", lambda e: e.activation(out=lams[:, 2:4], in_=lams[:, 0:2], func=AF.Exp), reads=["lams01"], writes=["lams23"])
        P.op("dve", lambda e: e.tensor_tensor(out=lams[:, 4:5], in0=lams[:, 3:4], in1=lams[:, 2:3], op=ALU.subtract),
             reads=["lams23"], writes=["nlam"])
        P.op("dve", lambda e: e.tensor_scalar(out=lams[:, 4:5], in0=lams[:, 4:5], scalar1=-LAM_INIT, scalar2=None, op0=ALU.add),
             reads=["nlam"], writes=["nlam"])
        P.op("dve", lambda e: e.tensor_scalar(out=nbal[:], in0=smalls[:, 10:12], scalar1=-1.0, scalar2=None, op0=ALU.mult),
             reads=["smalls"], writes=["nbal"])
        P.op("dve", lambda e: e.tensor_copy(out=walpha[:], in_=walpha_st[:]), reads=["cum0"], writes=["walpha"])
        for kc in range(8):
            P.op("dve", lambda e, kc=kc: e.tensor_scalar(out=wlr[:, kc, :], in0=wlr_st[:, kc, :], scalar1=smalls[:, kc:kc + 1],
                                                        scalar2=None, op0=ALU.mult),
                 reads=["gl1", "smalls"], writes=["wlr"])

        stgK = [KT[:, i // 2, 2048 + (i % 2) * 1024:2048 + (i % 2 + 1) * 1024].bitcast(F32) for i in range(8)]
        stg_keys = [f"stgK{i}" for i in range(8)]
        pctr = [0]

        def conv_piece(src_ap, dst_ap, dst_keys, scale_ap, scale_c):
            i = pctr[0] % 8
            pctr[0] += 1
            sap = stgK[i]
            P.op("sp", lambda e: e.dma_start(out=sap, in_=src_ap), writes=[stg_keys[i]], dma=f"stg{i}")
            if scale_ap is None:
                if i % 2 == 0:
                    P.op("dve", lambda e: e.tensor_scalar(out=dst_ap, in0=sap, scalar1=scale_c, scalar2=None, op0=ALU.mult),
                         reads=[stg_keys[i]], writes=dst_keys)
                else:
                    P.op("act", lambda e: e.activation(out=dst_ap, in_=sap, func=AF.Copy, scale=scale_c),
                         reads=[stg_keys[i]], writes=dst_keys)
            elif i % 2 == 0 or scale_c != 1.0:
                P.op("dve", lambda e: e.tensor_scalar(out=dst_ap, in0=sap, scalar1=scale_ap, scalar2=scale_c,
                                                      op0=ALU.mult, op1=ALU.mult),
                     reads=[stg_keys[i], "smalls"], writes=dst_keys)
            else:
                P.op("act", lambda e: e.activation(out=dst_ap, in_=sap, func=AF.Copy, scale=scale_ap),
                     reads=[stg_keys[i], "smalls"], writes=dst_keys)

        def conv_chunk(c, slot):
            if c < 11:
                for kc in range(8):
                    conv_piece(w_in[kc * 128:(kc + 1) * 128, CHUNK_COL[c]:CHUNK_COL[c] + 512],
                               wbuf[slot][:, kc * 512:(kc + 1) * 512], [f"w{slot}_{kc}"], smalls[:, kc:kc + 1], 1.0)
            else:
                half = c - 11
                for kk in range(4):
                    kc = half * 4 + kk
                    for hh in range(2):
                        conv_piece(w_out[kc * 128:(kc + 1) * 128, hh * 512:(hh + 1) * 512],
                                   wbuf[slot][:, kk * 1024 + hh * 512: kk * 1024 + (hh + 1) * 512], [f"w{slot}_{kk * 2 + hh}"],
                                   None, 0.5)
            P.op("pool", lambda e: e.dma_start(out=wsc[c], in_=wbuf[slot][:]), reads=wkeys(slot),
                 writes=[f"wsc{c}"], dma=f"wst{slot}")

        def conv_up():
            for kc in range(4):
                for hh in range(2):
                    conv_piece(w_up_a[kc * 128:(kc + 1) * 128, hh * 512:(hh + 1) * 512], wupA[:, kc, hh * 512:(hh + 1) * 512],
                               ["wupA"], smalls[:, 8:9], (1.0 - LAM_INIT) * 0.5)
                    conv_piece(w_up_b[kc * 128:(kc + 1) * 128, hh * 512:(hh + 1) * 512], wupB[:, kc, hh * 512:(hh + 1) * 512],
                               ["wupB"], smalls[:, 9:10], 0.5)

        TILE_SEQ = [CH_VB, CH_ZB, CH_QK, CH_QA, CH_KA, CH_VA, CH_ZA, CH_GA0, CH_GB0, CH_GA1, CH_GB1, CH_WO0, CH_WO1]
        wseq = TILE_SEQ * ntiles
        wstate = {"loaded": 0, "acq": 0}
        slot_base = 0

        def emit_load():
            i = wstate["loaded"]
            if i >= len(wseq):
                return
            c = wseq[i]
            slot = (slot_base + i) % NSLOT
            if i < len(TILE_SEQ):
                conv_chunk(c, slot)
            else:
                P.op("sp", lambda e: e.dma_start(out=wbuf[slot][:], in_=wsc[c]), reads=[f"wsc{c}"], writes=wkeys(slot),
                     dma=f"wld{slot}")
            wstate["loaded"] += 1

        def acquire(c):
            i = wstate["acq"]
            assert wseq[i] == c, (wseq[i], c)
            assert i < wstate["loaded"]
            wstate["acq"] += 1
            return (slot_base + i) % NSLOT

        def release():
            emit_load()

        def xa_buf(s_):
            return xa[s_ % 2], [f"xa{s_ % 2}"]

        def load_xa(t_, s_):
            buf, keys = xa_buf(s_)
            g = 4 * t_ + s_
            P.op("sp", lambda e: e.dma_start(out=buf[:], in_=x[g * 128:(g + 1) * 128, :]), writes=keys, dma=f"xa{s_ % 2}")

        def load_xr(g):
            i = g % 2
            P.op("sp", lambda e: e.dma_start(out=xr[i][:], in_=x[g * 128:(g + 1) * 128, :]),
                 writes=[f"xn{i}", f"xn{i}b"], dma=f"xr{i}")

        HT_KEYS = ["hT0", "hT1", "hT2", "hT3"]

        def fm_matmuls(slot, j, rows=128, lhs_from=None):
            b = alloc_pg()
            for kc in range(8):
                if lhs_from is None:
                    lhsT = wbuf[slot][:, kc * 512 + j * 128: kc * 512 + j * 128 + rows]
                    rk = [f"w{slot}_{kc}"]
                else:
                    lhsT = lhs_from[:, kc, :]
                    rk = ["wlr"]
                P.op("pe", lambda e, lhsT=lhsT, kc=kc: e.matmul(out=pg[b][0:rows, :], lhsT=lhsT, rhs=hT[:, kc, :],
                                                                start=(kc == 0), stop=(kc == 7)),
                     reads=rk + HT_KEYS, writes=[f"pg{b}"])
            return b

        def tm_matmuls(slot, s):
            b = alloc_pg()
            for kc in range(8):
                P.op("pe", lambda e, kc=kc: e.matmul(out=pg[b][:, :], lhsT=hT[:, kc, s * 128:(s + 1) * 128],
                                                     rhs=wbuf[slot][:, kc * 512:(kc + 1) * 512],
                                                     start=(kc == 0), stop=(kc == 7)),
                     reads=[f"w{slot}_{kc}", f"hT{s}"], writes=[f"pg{b}"])
            return b

        def dump(name, src_ap, rkeys, shape):
            if not debug:
                return
            n = 1
            for d_ in shape[1:]:
                n *= d_
            view = dbgbuf[:].rearrange("p a b -> p (a b)")[:, 0:n]
            if len(shape) == 3:
                view = view.rearrange("p (a b) -> p a b", a=shape[1])
            elif len(shape) == 4:
                view = view.rearrange("p (a b c) -> p a b c", a=shape[1], b=shape[2])
            P.op("dve", lambda e: e.tensor_copy(out=view, in_=src_ap), reads=rkeys, writes=["dbgbuf"])
            P.op("sp", lambda e: e.dma_start(out=dbg[name], in_=view), reads=["dbgbuf"], dma="dbg")

        def pow_cols(dst, src, col0, n, rkeys, wkey):
            for q_ in range(n):
                P.op("pool", lambda e, q_=q_: e.tensor_tensor(out=dst[:, col0 + q_:col0 + q_ + 1], in0=src[:, col0 + q_:col0 + q_ + 1],
                                                             in1=lams[:, 5:6], op=ALU.pow),
                     reads=rkeys + ["nhalf"], writes=[f"{wkey}_{q_}"])

        def a_stats(t, s):
            junk = PT2[0][:].rearrange("p a b -> p (a b)")
            g = 4 * t + s
            xbuf, xkeys = xa_buf(s)
            P.op("act", lambda e: e.activation(out=junk, in_=xbuf[:], func=AF.Square, accum_out=ss[:, g:g + 1]),
                 reads=xkeys, writes=["PT0", f"ss{g}"])
            P.op("dve", lambda e: e.tensor_scalar(out=rs[:, g:g + 1], in0=ss[:, g:g + 1], scalar1=1.0 / D, scalar2=EPS,
                                                  op0=ALU.mult, op1=ALU.add), reads=[f"ss{g}"], writes=[f"rs{g}"])
            P.op("pool", lambda e: e.tensor_tensor(out=rstd[:, g:g + 1], in0=rs[:, g:g + 1], in1=lams[:, 5:6], op=ALU.pow),
                 reads=[f"rs{g}", "nhalf"], writes=[f"rstd{g}"])

        def a_norm_tr(t, s):
            g = 4 * t + s
            i = g % 2
            xbuf, xkeys = xa_buf(s)
            P.op("dve", lambda e: e.tensor_scalar(out=hb[i][:], in0=xbuf[:], scalar1=rstd[:, g:g + 1], scalar2=None, op0=ALU.mult),
                 reads=xkeys + [f"rstd{g}"], writes=[f"hbL{i}", f"hbR{i}"])
            if s + 2 < 4:
                load_xa(t, s + 2)
            for kc in range(8):
                P.op("pe", lambda e, kc=kc: e.transpose(out=ptr[:, kc, :], in_=hb[i][:, kc * 128:(kc + 1) * 128], identity=ident[:]),
                     reads=[f"hbL{i}", f"hbR{i}", "ident"], writes=["ptr"])
            if s % 2 == 0:
                P.op("dve", lambda e: e.tensor_copy(out=hT[:, :, s * 128:(s + 1) * 128], in_=ptr[:, :, :]),
                     reads=["ptr"], writes=[f"hT{s}"])
            else:
                P.op("act", lambda e: e.activation(out=hT[:, :, s * 128:(s + 1) * 128], in_=ptr[:, :, :], func=AF.Copy),
                     reads=["ptr"], writes=[f"hT{s}"])

        def phase_a(t, subs):
            for s in subs:
                a_stats(t, s)
            for s in subs:
                a_norm_tr(t, s)
            if debug and t == 0 and 3 in subs:
                dump("d_hT", hT[:], HT_KEYS, [128, 8, 512])

        def phase_b1(t):
            sl_vb = acquire(CH_VB)
            for s in range(2):
                b = tm_matmuls(sl_vb, s)
                P.op("dve", lambda e, s=s, b=b: e.tensor_copy(out=vb[:, s, :], in_=pg[b][:, :]), reads=[f"pg{b}"], writes=[f"vb{s}"])
            b = fm_matmuls(None, 0, rows=16, lhs_from=wlr)
            P.op("dve", lambda e, b=b: e.tensor_copy(out=lrT[:, :], in_=pg[b][0:16, :]), reads=[f"pg{b}"], writes=["lrT"])
            for c in range(2):
                b = alloc_pg()
                P.op("pe", lambda e, c=c, b=b: e.matmul(out=pg[b][:, :], lhsT=walpha[:, c * 128:(c + 1) * 128], rhs=lrT[:, :],
                                                        start=True, stop=True), reads=["walpha", "lrT"], writes=[f"pg{b}"])
                P.op("act", lambda e, c=c, b=b: e.activation(out=gl[:, c, :], in_=pg[b][:, :], func=AF.Exp, scale=-1.0,
                                                             bias=nbal[:, c:c + 1]), reads=[f"pg{b}", "nbal"], writes=[f"gl{c}"])
            for c in range(2):
                P.op("act", lambda e, c=c: e.activation(out=gl[:, c, :], in_=gl[:, c, :], func=AF.Ln, bias=1.0, scale=1.0),
                     reads=[f"gl{c}"], writes=[f"gl{c}"])
                P.op("dve", lambda e, c=c: e.tensor_tensor_scan(out=cum[:, c, :], data0=rmask[:, :], data1=gl[:, c, :], initial=0.0,
                                                                op0=ALU.mult, op1=ALU.add),
                     reads=[f"gl{c}", "rmask"], writes=[f"cum{c}"])
            for c in range(2):
                P.op("act", lambda e, c=c: e.activation(out=eb[:, c, :], in_=cum[:, c, :], func=AF.Exp, scale=-1.0 / 16.0),
                     reads=[f"cum{c}"], writes=[f"tga{c}"])
                P.op("act", lambda e, c=c: e.activation(out=enb[:, c, :], in_=cum[:, c, :], func=AF.Exp, scale=1.0 / 16.0),
                     reads=[f"cum{c}"], writes=[f"cum{c}"])
            if debug and t == 0:
                dump("d_eb", eb[:], ["tga0", "tga1"], [128, 2, 512])
            for s in range(2, 4):
                b = tm_matmuls(sl_vb, s)
                P.op("dve", lambda e, s=s, b=b: e.tensor_copy(out=vb[:, s, :], in_=pg[b][:, :]), reads=[f"pg{b}"], writes=[f"vb{s}"])
            release()
            sl_zb = acquire(CH_ZB)
            for s in range(4):
                b = tm_matmuls(sl_zb, s)
                i = s % 2
                P.op("act", lambda e, i=i, b=b: e.activation(out=ztmp[i], in_=pg[b][:, :], func=AF.Tanh, scale=0.5),
                     reads=[f"pg{b}"], writes=[f"PT{i}"])
                P.op("dve", lambda e, i=i, b=b, s=s: e.scalar_tensor_tensor(out=szB[:, s, :], in0=ztmp[i], scalar=1.0,
                                                                            in1=pg[b][:, :], op0=ALU.add, op1=ALU.mult),
                     reads=[f"pg{b}", f"PT{i}"], writes=[f"szB{s}", f"yT{4 + s}"])
            release()
            sl_qk = acquire(CH_QK)
            for c in range(2):
                b = fm_matmuls(sl_qk, c)
                for hh in range(2):
                    P.op("dve", lambda e, c=c, hh=hh, b=b: e.scalar_tensor_tensor(
                        out=qtT[64 * hh:64 * hh + 64, c, hh, :], in0=pg[b][64 * hh:64 * hh + 64, :], scalar=0.125,
                        in1=eb[64 * hh:64 * hh + 64, c, :], op0=ALU.mult, op1=ALU.mult),
                        reads=[f"pg{b}", f"tga{c}"], writes=[f"qtT{c}"])
            for c in range(2):
                b = fm_matmuls(sl_qk, 2 + c)
                P.op("dve", lambda e, c=c, b=b: e.tensor_tensor(out=ktT[:, c, :], in0=pg[b][:, :], in1=enb[:, c, :], op=ALU.mult),
                     reads=[f"pg{b}", f"cum{c}"], writes=[f"ktT{c}"])
                for s in range(4):
                    P.op("dve", lambda e, c=c, s=s, b=b: e.scalar_tensor_tensor(
                        out=khT[:, c, s * 128:(s + 1) * 128], in0=pg[b][:, s * 128:(s + 1) * 128],
                        scalar=eb[:, c, s * 128 + 127:s * 128 + 128], in1=enb[:, c, s * 128:(s + 1) * 128],
                        op0=ALU.mult, op1=ALU.mult),
                        reads=[f"pg{b}", f"cum{c}", f"tga{c}"], writes=[f"gatedA{c}"])
                P.op("dve", lambda e, c=c: e.tensor_copy(out=eblast[:, c, :],
                                                         in_=eb[:, c, :].rearrange("p (s j) -> p s j", j=128)[:, :, 127]),
                     reads=[f"tga{c}"], writes=[f"eblast{c}"])
            release()
            for s in range(4):
                for c in range(2):
                    P.op("pe", lambda e, s=s, c=c: e.transpose(out=ptr[:, c * 4 + s, :], in_=khT[:, c, s * 128:(s + 1) * 128],
                                                               identity=ident[:]),
                         reads=[f"gatedA{c}", "ident"], writes=["ptr"])
            for c in range(2):
                P.op("dve", lambda e, c=c: e.tensor_copy(out=khat[:, :, c, :], in_=ptr[:, c * 4:(c + 1) * 4, :]),
                     reads=["ptr"], writes=[f"khat{c}"])

        def phase_b2(t):
            sl_qa = acquire(CH_QA)
            for j in range(4):
                b = fm_matmuls(sl_qa, j)
                P.op("dve", lambda e, j=j, b=b: e.tensor_scalar(out=QT[:, j, :], in0=pg[b][:, :], scalar1=0.125, scalar2=None,
                                                                op0=ALU.mult), reads=[f"pg{b}"], writes=[f"QT{j}", "gaT0", "gaT1", "gaT2", "gaT3"])
            release()
            sl_ka = acquire(CH_KA)
            for j in range(4):
                b = fm_matmuls(sl_ka, j)
                P.op("dve", lambda e, j=j, b=b: e.tensor_copy(out=KT[:, j, t * T:(t + 1) * T], in_=pg[b][:, :]),
                     reads=[f"pg{b}"], writes=[f"KT{j}"] + (stg_keys if t == 4 else []))
            release()
            sl_va = acquire(CH_VA)
            for s in range(4):
                b = tm_matmuls(sl_va, s)
                g = 4 * t + s
                for hd in range(4):
                    P.op("dve", lambda e, g=g, b=b, hd=hd: e.tensor_scalar(out=V[:, g, hd, 0:128], in0=pg[b][:, hd * 128:(hd + 1) * 128],
                                                                          scalar1=eft[:, hd:hd + 1], scalar2=None, op0=ALU.mult),
                         reads=[f"pg{b}", "eft"], writes=["V"])
                P.op("dve", lambda e, g=g: e.tensor_copy(out=V[:, g, :, 128:129], in_=eft[:, :].unsqueeze(2)),
                     reads=["eft"], writes=["V"])
            release()
            sl_za = acquire(CH_ZA)
            for s in range(4):
                b = tm_matmuls(sl_za, s)
                i = s % 2
                P.op("act", lambda e, i=i, b=b: e.activation(out=ztmp[i], in_=pg[b][:, :], func=AF.Tanh, scale=0.5),
                     reads=[f"pg{b}"], writes=[f"PT{i}"])
                P.op("dve", lambda e, i=i, b=b, s=s: e.scalar_tensor_tensor(out=szA[:, s, :], in0=ztmp[i], scalar=1.0,
                                                                            in1=pg[b][:, :], op0=ALU.add, op1=ALU.mult),
                     reads=[f"pg{b}", f"PT{i}"], writes=[f"szA{s}", f"yT{s}"])
            release()
            if debug and t == 0:
                dump("d_QT", QT[:], ["QT0", "QT1", "QT2", "QT3"], [128, 4, 512])

        def gla_at(t, s):
            bA = alloc_pg()
            for c in range(2):
                P.op("pe", lambda e, c=c: e.matmul(
                    out=pg[bA][:, c * 256:(c + 1) * 256], lhsT=ktT[:, c, s * 128:(s + 1) * 128],
                    rhs=qtT[:, c, :, s * 128:(s + 1) * 128], start=True, stop=True),
                    reads=[f"ktT{c}", f"qtT{c}"], writes=[f"pg{bA}"])
            P.op("dve", lambda e: e.tensor_tensor(out=ATs[0][:, :], in0=pg[bA][:, :], in1=tri4[:, :], op=ALU.mult),
                 reads=[f"pg{bA}", "tri4"], writes=["ATs0"])

        def gla_rest(t, s):
            bS = alloc_pg()
            for c in range(2):
                P.op("pe", lambda e, c=c: e.matmul(out=pg[bS][:, c * 256:(c + 1) * 256], lhsT=khat[:, s, c, :],
                                                   rhs=vb[:, s, c * 256:(c + 1) * 256], start=True, stop=True),
                     reads=[f"khat{c}", f"vb{s}"], writes=[f"pg{bS}"])
            bO = alloc_pg()
            for hd in range(4):
                c, hh = hd // 2, hd % 2
                P.op("pe", lambda e, c=c, hh=hh, hd=hd: e.matmul(
                    out=pg[bO][:, hd * 128:(hd + 1) * 128], lhsT=qtT[:, c, hh, s * 128:(s + 1) * 128],
                    rhs=stbf[:, c, :], start=True, stop=False),
                    reads=[f"qtT{c}", f"stbf{c}"], writes=[f"pg{bO}"])
                P.op("pe", lambda e, hd=hd: e.matmul(
                    out=pg[bO][:, hd * 128:(hd + 1) * 128], lhsT=ATs[0][:, hd * 128:(hd + 1) * 128],
                    rhs=vb[:, s, hd * 128:(hd + 1) * 128], start=False, stop=True),
                    reads=["ATs0", f"vb{s}"], writes=[f"pg{bO}"])
            for c in range(2):
                for hh in range(2):
                    P.op("dve", lambda e, c=c, hh=hh: e.scalar_tensor_tensor(
                        out=state[64 * hh:64 * hh + 64, c, :], in0=state[64 * hh:64 * hh + 64, c, :],
                        scalar=eblast[64 * hh:64 * hh + 64, c, s:s + 1],
                        in1=pg[bS][64 * hh:64 * hh + 64, c * 256 + hh * 128:c * 256 + (hh + 1) * 128],
                        op0=ALU.mult, op1=ALU.add),
                        reads=[f"pg{bS}", f"eblast{c}", f"state{c}"], writes=[f"state{c}"])
                P.op("pool", lambda e, c=c: e.tensor_copy(out=stbf[:, c, :], in_=state[:, c, :]),
                     reads=[f"state{c}"], writes=[f"stbf{c}"])
            oi = s % 2
            P.op("dve", lambda e: e.tensor_copy(out=gl[:, oi, :], in_=pg[bO][:, :]), reads=[f"pg{bO}"], writes=[f"gl{oi}"])
            if debug and t == 0:
                P.op("dve", lambda e: e.tensor_copy(out=dbgbuf[:, s, :], in_=gl[:, oi, :]), reads=[f"gl{oi}"], writes=["dbgbuf"])

        def gla_out_ew(t, s):
            oi = s % 2
            for hd in range(4):
                P.op("dve", lambda e, hd=hd: e.scalar_tensor_tensor(out=junkb[:, :], in0=gl[:, oi, hd * 128:(hd + 1) * 128], scalar=1.0,
                                                                    in1=gl[:, oi, hd * 128:(hd + 1) * 128], op0=ALU.mult, op1=ALU.mult,
                                                                    accum_out=ssb[:, s * 4 + hd:s * 4 + hd + 1]),
                     reads=[f"gl{oi}"], writes=["junkb", f"ssb{s}_{hd}"])
            P.op("dve", lambda e: e.tensor_scalar(out=rsb[:, s * 4:(s + 1) * 4], in0=ssb[:, s * 4:(s + 1) * 4],
                                                  scalar1=1.0 / 128, scalar2=EPS, op0=ALU.mult, op1=ALU.add),
                 reads=[f"ssb{s}_{hd}" for hd in range(4)], writes=[f"rsb{s}"])
            pow_cols(rstdb, rsb, s * 4, 4, [f"rsb{s}"], f"rstdb{s}")

        def gla_out_ew2(t, s):
            oi = s % 2
            gi = s % 2
            for hd in range(4):
                P.op("dve", lambda e, hd=hd: e.scalar_tensor_tensor(
                    out=gated[gi][:, hd * 128:(hd + 1) * 128], in0=gl[:, oi, hd * 128:(hd + 1) * 128],
                    scalar=rstdb[:, s * 4 + hd:s * 4 + hd + 1], in1=szB[:, s, hd * 128:(hd + 1) * 128],
                    op0=ALU.mult, op1=ALU.mult),
                    reads=[f"gl{oi}", f"rstdb{s}_{hd}", f"szB{s}"], writes=[f"hbL{gi}"])

        def gla_out_pe(t, s):
            gi = s % 2
            for hd in range(4):
                P.op("pe", lambda e, hd=hd: e.transpose(out=ptr[:, hd, :], in_=gated[gi][:, hd * 128:(hd + 1) * 128],
                                                        identity=ident[:]),
                     reads=[f"hbL{gi}", "ident"], writes=["ptr"])
            P.op("dve", lambda e: e.tensor_copy(out=gbT[:, :, s * 128:(s + 1) * 128], in_=ptr[:, 0:4, :]),
                 reads=["ptr"], writes=[f"gbT{s}"])

        def acc_idx(m, a):
            return 2 * a + m

        def acc_ap(m, a, lo, hi):
            i_ = acc_idx(m, a)
            return poa[i_ // 3][:, (i_ % 3) * 130 + lo:(i_ % 3) * 130 + hi]

        pair_ctr = [0]

        def alloc_pair():
            bp = 2 * (pair_ctr[0] % 2)
            pair_ctr[0] += 1
            return bp

        def attention_jobs(t, h, deferred):
            nkb = 4 * t + 4

            def emit_qk_pair(kb):
                r = kb - 4 * t
                bp = alloc_pair()
                c0 = 0 if r < 0 else 128 * r
                for m in range(2):
                    b = bp + m
                    lhsT = KT[64 * m:64 * m + 64, h, kb * 128:(kb + 1) * 128]
                    if r < 0:
                        P.op("pe", lambda e, b=b, lhsT=lhsT, m=m: e.matmul(out=pg[b][:, :], lhsT=lhsT, rhs=QT[64 * m:64 * m + 64, h, :],
                                                                           start=True, stop=True),
                             reads=[f"KT{h}", f"QT{h}"], writes=[f"pg{b}"])
                    else:
                        P.op("pe", lambda e, b=b, lhsT=lhsT, m=m: e.matmul(out=pg[b][:, c0:c0 + 128], lhsT=lhsT,
                                                                           rhs=QT[64 * m:64 * m + 64, h, c0:c0 + 128],
                                                                           start=True, stop=False),
                             reads=[f"KT{h}", f"QT{h}"], writes=[f"pg{b}"])
                        P.op("pe", lambda e, b=b: e.matmul(out=pg[b][:, c0:c0 + 128], lhsT=ident[:, :], rhs=negm[:, :],
                                                           start=False, stop=True),
                             reads=["ident", "negm"], writes=[f"pg{b}"])
                        if c0 + 128 < 512:
                            P.op("pe", lambda e, b=b, lhsT=lhsT, m=m: e.matmul(out=pg[b][:, c0 + 128:512], lhsT=lhsT,
                                                                               rhs=QT[64 * m:64 * m + 64, h, c0 + 128:512],
                                                                               start=True, stop=True),
                                 reads=[f"KT{h}", f"QT{h}"], writes=[f"pg{b}"])
                pp = kb % 2
                cbias = SLOPES[h] * (128.0 * r - 129.0)
                P.op("act", lambda e: e.activation(out=PT2[pp][:, :, c0:512], in_=pg4[:, bp:bp + 2, c0:512], func=AF.Exp,
                                                   bias=cbias, scale=1.0),
                     reads=[f"pg{bp}", f"pg{bp + 1}"], writes=[f"PT{pp}"])

            def emit_pv(kb, m):
                r = kb - 4 * t
                pp = kb % 2
                for a in range(max(r, 0), 4):
                    i_ = acc_idx(m, a)
                    P.op("pe", lambda e, a=a, i_=i_: e.matmul(out=acc_ap(m, a, 0, 129), lhsT=PT2[pp][:, m, a * 128:(a + 1) * 128],
                                                              rhs=V[:, kb, h, 0:129], start=(kb == 0 and i_ in (0, 4, 6)), stop=False,
                                                              skip_group_check=True),
                         reads=[f"PT{pp}", "V", "Vones"], writes=[f"poa{i_ // 3}"])

            for kb in range(nkb + 1):
                if kb < nkb:
                    emit_qk_pair(kb)
                if kb >= 1:
                    emit_pv(kb - 1, 0)
                    emit_pv(kb - 1, 1)
                    r_done = (kb - 1) - 4 * t
                    if r_done >= 1:
                        attention_evac(t, h, r_done)
                if t == 0:
                    if kb == 1:
                        for kk in sorted(deferred):
                            for f in deferred[kk]:
                                f()
                else:
                    for f in deferred.get(kb, ()):
                        f()

        def attention_evac(t, h, stage):
            bank = stage - 1
            n = 3 if bank < 2 else 2
            i0 = 3 * bank
            zs = poa[bank][:, 0:n * 130].rearrange("p (i c) -> p i c", c=130)[:, :, 128]
            P.op("dve", lambda e: e.reciprocal(out=rz[:, i0:i0 + n], in_=zs),
                 reads=[f"poa{bank}"], writes=[f"rz{i0 + k}" for k in range(n)])
            for i_ in range(i0, i0 + n):
                if i_ % 2 == 1:
                    P.op("dve", lambda e, i_=i_: e.tensor_scalar(out=rz[:, i_:i_ + 1], in0=rz[:, i_:i_ + 1], scalar1=lams[:, 4:5],
                                                                 scalar2=None, op0=ALU.mult),
                         reads=[f"rz{i_}", "nlam"], writes=[f"rz{i_}"])

            def part0(a):
                i_ = acc_idx(0, a)
                oi2 = a % 2
                P.op("dve", lambda e: e.tensor_scalar(out=otmp[oi2][:, :], in0=acc_ap(0, a, 0, 128), scalar1=rz[:, i_:i_ + 1],
                                                      scalar2=None, op0=ALU.mult),
                     reads=[f"poa{i_ // 3}", f"rz{i_}"], writes=[f"otmp{oi2}"])

            def part1(a):
                i_ = acc_idx(1, a)
                oi2 = a % 2
                P.op("dve", lambda e: e.scalar_tensor_tensor(out=oaf[a][:, :], in0=acc_ap(1, a, 0, 128), scalar=rz[:, i_:i_ + 1],
                                                             in1=otmp[oi2][:, :], op0=ALU.mult, op1=ALU.add),
                     reads=[f"poa{i_ // 3}", f"rz{i_}", f"otmp{oi2}"], writes=[f"oaf{a}"])
                if debug and t == 0:
                    P.op("dve", lambda e: e.tensor_copy(out=dbgoa[:, a, h, :], in_=oaf[a][:, :]), reads=[f"oaf{a}"], writes=["dbgoa"])

            if stage == 1:
                part0(0); part1(0); part0(1)
            elif stage == 2:
                part1(1); part0(2); part1(2)
            else:
                part0(3); part1(3)

        def attention_norm(t, h):
            for a in range(4):
                ci = a * 4 + h
                P.op("dve", lambda e, a=a, ci=ci: e.scalar_tensor_tensor(out=junkb[:, :], in0=oaf[a][:, :], scalar=1.0, in1=oaf[a][:, :],
                                                                         op0=ALU.mult, op1=ALU.mult, accum_out=ssa[:, ci:ci + 1]),
                     reads=[f"oaf{a}"], writes=["junkb", f"ssa{ci}"])
            for a in range(4):
                ci = a * 4 + h
                P.op("dve", lambda e, ci=ci: e.tensor_scalar(out=rsa[:, ci:ci + 1], in0=ssa[:, ci:ci + 1], scalar1=1.0 / 128, scalar2=EPS,
                                                            op0=ALU.mult, op1=ALU.add), reads=[f"ssa{ci}"], writes=[f"rsa{ci}"])
                P.op("pool", lambda e, ci=ci: e.tensor_tensor(out=rstda[:, ci:ci + 1], in0=rsa[:, ci:ci + 1], in1=lams[:, 5:6], op=ALU.pow),
                     reads=[f"rsa{ci}", "nhalf"], writes=[f"rstda{ci}"])

        def attention_norm2(t, h):
            for a in range(4):
                ci = a * 4 + h
                P.op("dve", lambda e, a=a, ci=ci: e.scalar_tensor_tensor(
                    out=gatedA[:, a, h * 128:(h + 1) * 128], in0=oaf[a][:, :], scalar=rstda[:, ci:ci + 1],
                    in1=szA[:, a, h * 128:(h + 1) * 128], op0=ALU.mult, op1=ALU.mult),
                    reads=[f"oaf{a}", f"rstda{ci}", f"szA{a}"], writes=[f"gatedA{a}"])

        def attn_post(t):
            for a in range(4):
                for hd in range(4):
                    P.op("pe", lambda e, a=a, hd=hd: e.transpose(out=ptr[:, 4 + hd, :], in_=gatedA[:, a, hd * 128:(hd + 1) * 128],
                                                                 identity=ident[:]),
                         reads=[f"gatedA{a}", "ident"], writes=["ptr"])
                P.op("dve", lambda e, a=a: e.tensor_copy(out=gaT[:, :, a * 128:(a + 1) * 128], in_=ptr[:, 4:8, :]),
                     reads=["ptr"], writes=[f"gaT{a}", "QT0", "QT1", "QT2", "QT3"])
            if debug and t == 0:
                dump("d_gaT", gaT[:], ["gaT0", "gaT1", "gaT2", "gaT3"], [128, 4, 512])
                dump("d_gbT", gbT[:], ["gbT0", "gbT1", "gbT2", "gbT3"], [128, 4, 512])

        def phase_e(t, mid):
            gaT_keys = ["gaT0", "gaT1", "gaT2", "gaT3"]
            gbT_keys = ["gbT0", "gbT1", "gbT2", "gbT3"]
            ytmp = [gl[:, 0, :], gl[:, 1, :]]
            ytk = ["gl0", "gl1"]
            sl_ga0 = acquire(CH_GA0)
            sl_gb0 = acquire(CH_GB0)
            sl_ga1 = sl_gb1 = None
            for j in range(8):
                if j == 4:
                    release()
                    release()
                    sl_ga1 = acquire(CH_GA1)
                    sl_gb1 = acquire(CH_GB1)
                sga = sl_ga0 if j < 4 else sl_ga1
                sgb = sl_gb0 if j < 4 else sl_gb1
                jj = j % 4
                ti = j % 2
                bga = fm_matmuls(sga, jj)
                P.op("act", lambda e, ti=ti, bga=bga: e.activation(out=tga[ti], in_=pg[bga][:, :], func=AF.Tanh, scale=0.5),
                     reads=[f"pg{bga}"], writes=[f"tga{ti}"])
                bgb = fm_matmuls(sgb, jj)
                P.op("act", lambda e, ti=ti, bgb=bgb: e.activation(out=tgb[ti][:, :], in_=pg[bgb][:, :], func=AF.Tanh, scale=0.5),
                     reads=[f"pg{bgb}"], writes=["tgb0"])
                if j == 0:
                    mid()
                bya = alloc_pg()
                for kc in range(4):
                    P.op("pe", lambda e, kc=kc, j=j, bya=bya: e.matmul(out=pg[bya][:, :], lhsT=wupA[:, kc, j * 128:(j + 1) * 128],
                                                                      rhs=gaT[:, kc, :], start=(kc == 0), stop=(kc == 3)),
                         reads=["wupA"] + gaT_keys, writes=[f"pg{bya}"])
                P.op("dve", lambda e, ti=ti, bya=bya: e.scalar_tensor_tensor(out=ytmp[ti], in0=tga[ti], scalar=1.0, in1=pg[bya][:, :],
                                                                             op0=ALU.add, op1=ALU.mult),
                     reads=[f"pg{bya}", f"tga{ti}"], writes=[ytk[ti]])
                byb = alloc_pg()
                for kc in range(4):
                    P.op("pe", lambda e, kc=kc, j=j, byb=byb: e.matmul(out=pg[byb][:, :], lhsT=wupB[:, kc, j * 128:(j + 1) * 128],
                                                                      rhs=gbT[:, kc, :], start=(kc == 0), stop=(kc == 3)),
                         reads=["wupB"] + gbT_keys, writes=[f"pg{byb}"])
                P.op("dve", lambda e, ti=ti, byb=byb: e.scalar_tensor_tensor(out=tgb[ti][:, :], in0=tgb[ti][:, :], scalar=1.0, in1=pg[byb][:, :],
                                                                             op0=ALU.add, op1=ALU.mult),
                     reads=[f"pg{byb}", "tgb0"], writes=["tgb0"])
                P.op("dve", lambda e, ti=ti, j=j: e.tensor_tensor(out=yT[:, j, :], in0=ytmp[ti], in1=tgb[ti][:, :], op=ALU.add),
                     reads=[ytk[ti], "tgb0"], writes=[f"yT{j}", (f"szA{j}" if j < 4 else f"szB{j - 4}")])
            release()
            release()
            if debug and t == 0:
                dump("d_yT", yT[:], [f"yT{j}" for j in range(8)], [128, 8, 512])

        def phase_f(t, hooks):
            sl_wo0 = acquire(CH_WO0)
            sl_wo1 = acquire(CH_WO1)
            zg = [cum, gl]
            zgk = [["cum0", "cum1"], ["gl0", "gl1"]]
            load_xr(4 * t)
            load_xr(4 * t + 1)

            def zbank(s, hh):
                i = 2 * s + hh
                if 4 <= i < 7:
                    return poa[i - 4][:, :], f"poa{i - 4}"
                b = alloc_pg()
                return pg[b], f"pg{b}"

            def f_mm(s):
                g = 4 * t + s
                for hh in range(2):
                    zb_, zk_ = zbank(s, hh)
                    for kc in range(8):
                        slw = sl_wo0 if kc < 4 else sl_wo1
                        kk = kc % 4
                        P.op("pe", lambda e, kc=kc, kk=kk, slw=slw, hh=hh, zb_=zb_: e.matmul(
                            out=zb_, lhsT=yT[:, kc, s * 128:(s + 1) * 128],
                            rhs=wbuf[slw][:, kk * 1024 + hh * 512: kk * 1024 + (hh + 1) * 512], start=(kc == 0), stop=(kc == 7)),
                            reads=[f"yT{kc}", f"w{slw}_{kk * 2 + hh}"], writes=[zk_])
                    P.op("act", lambda e, zb_=zb_, hh=hh: e.activation(out=ttmp[hh], in_=zb_, func=AF.Square,
                                                                      accum_out=ssz[:, 2 * g + hh:2 * g + hh + 1]),
                         reads=[zk_], writes=[f"tga{hh}", f"ssz{g}_{hh}"])
                    P.op("dve", lambda e, zb_=zb_, hh=hh: e.tensor_tensor(out=zg[s % 2][:, hh, :], in0=zb_,
                                                                         in1=gpost[:, hh * 512:(hh + 1) * 512], op=ALU.mult),
                         reads=[zk_, "gpost", f"ssz{g}_{hh}"], writes=[zgk[s % 2][hh]])
                P.op("dve", lambda e: e.tensor_tensor(out=rsz[:, g:g + 1], in0=ssz[:, 2 * g:2 * g + 1], in1=ssz[:, 2 * g + 1:2 * g + 2],
                                                      op=ALU.add), reads=[f"ssz{g}_0", f"ssz{g}_1"], writes=[f"rsz{g}"])
                P.op("dve", lambda e: e.tensor_scalar(out=rsz[:, g:g + 1], in0=rsz[:, g:g + 1], scalar1=1.0 / D, scalar2=EPS,
                                                      op0=ALU.mult, op1=ALU.add), reads=[f"rsz{g}"], writes=[f"rsz{g}"])
                P.op("pool", lambda e: e.tensor_tensor(out=rstdz[:, g:g + 1], in0=rsz[:, g:g + 1], in1=lams[:, 5:6], op=ALU.pow),
                     reads=[f"rsz{g}", "nhalf"], writes=[f"rstdz{g}"])

            def f_fin(s):
                g = 4 * t + s
                ri = g % 2
                for hh in range(2):
                    xk = f"xn{ri}" if hh == 0 else f"xn{ri}b"
                    P.op("dve", lambda e, hh=hh: e.scalar_tensor_tensor(out=xr[ri][:, hh * 512:(hh + 1) * 512], in0=zg[s % 2][:, hh, :],
                                                                        scalar=rstdz[:, g:g + 1], in1=xr[ri][:, hh * 512:(hh + 1) * 512],
                                                                        op0=ALU.mult, op1=ALU.add),
                         reads=[zgk[s % 2][hh], f"rstdz{g}", xk], writes=[xk])
                P.op("pool", lambda e: e.dma_start(out=out[g * 128:(g + 1) * 128, :], in_=xr[ri][:]),
                     reads=[f"xn{ri}", f"xn{ri}b"], dma=f"st{ri}")
                if s + 2 < 4:
                    load_xr(g + 2)

            steps = [("mm", 0), ("mm", 1), ("fin", 0), ("mm", 2), ("fin", 1), ("mm", 3), ("fin", 2), ("fin", 3)]
            for kind, s_ in steps:
                (f_mm if kind == "mm" else f_fin)(s_)
                for f in hooks.get((kind, s_), ()):
                    f()
            release()
            release()

        load_xa(0, 0)
        load_xa(0, 1)
        for t in range(ntiles):
            if stop_after == "prologue":
                break
            if t == 0:
                phase_a(t, [0, 1])
                phase_a(t, [2, 3])
                for _ in range(NSLOT):
                    emit_load()
            if stop_after == "a":
                break
            phase_b1(t)
            if stop_after == "b1":
                break
            gla_at(t, 0)
            phase_b2(t)
            if stop_after == "b2":
                break
            if t == 0:
                conv_up()
            gla_rest(t, 0)
            deferred = {1: [lambda: gla_at(t, 1), lambda: gla_rest(t, 1)], 2: [lambda: gla_out_ew(t, 0)],
                        3: [lambda: gla_out_ew2(t, 0)], 4: [lambda: gla_out_pe(t, 0)]}
            for s in range(4):
                attention_jobs(t, s, deferred)
                deferred = {1: [], 2: [lambda s=s: attention_norm(t, s)], 3: [lambda s=s: attention_norm2(t, s)], 4: []}
                if s < 3:
                    deferred[2].append(lambda s=s: gla_out_ew(t, s + 1))
                    deferred[3].append(lambda s=s: gla_out_ew2(t, s + 1))
                    deferred[4].append(lambda s=s: gla_out_pe(t, s + 1))
                if s < 2:
                    deferred[1].append(lambda s=s: gla_at(t, s + 2))
                    deferred[1].append(lambda s=s: gla_rest(t, s + 2))
            for kk in (1, 2, 3, 4):
                for f in deferred[kk]:
                    f()
            if debug and t == 0:
                P.op("sp", lambda e: e.dma_start(out=dbg["d_ob"], in_=dbgbuf[:, 0:4, :]), reads=["dbgbuf"], dma="dbg")
                dump("d_oacc", dbgoa[:], ["dbgoa"], [128, 4, 4, 128])
                dump("d_state", state[:], ["state0", "state1"], [128, 2, 128])
            if t + 1 < ntiles:
                load_xa(t + 1, 0)
                load_xa(t + 1, 1)
                a_stats(t + 1, 0)
                a_stats(t + 1, 1)
            phase_e(t, lambda: attn_post(t))
            if t + 1 < ntiles:
                nt_ = t + 1
                hooks = {("mm", 2): [lambda: a_norm_tr(nt_, 0)], ("mm", 3): [lambda: a_norm_tr(nt_, 1), lambda: a_stats(nt_, 2)],
                         ("fin", 2): [lambda: a_norm_tr(nt_, 2), lambda: a_stats(nt_, 3)],
                         ("fin", 3): [lambda: a_norm_tr(nt_, 3)]}
            else:
                hooks = {}
            phase_f(t, hooks)
        P.op("sp", lambda e: None, reads=[], writes=["xn0", "xn0b", "xn1", "xn1b"] + (["dbgbuf"] if debug else []))
        P.emit({"pe": block.tensor, "act": block.scalar, "dve": block.vector, "pool": block.gpsimd, "sp": block.sync}, sems)
    return nc


def _host_consts():
    bf = ml_dtypes.bfloat16
    k = np.arange(128)[:, None]
    q = np.arange(128)[None, :]
    tri = (q >= k).astype(np.float32)
    negm = np.where(k > q, NEG, 0.0).astype(np.float32)
    btab = np.zeros((128, 128), np.float32)
    for h in range(4):
        for i in range(32):
            btab[:, h * 32 + i] = SLOPES[h] * (np.arange(128) + 128.0 * (i - 28) - 256.0)
    rmask = np.ones((128, 512), np.float32)
    rmask[:, ::128] = 0.0
    return {
        "ident": np.eye(128, dtype=np.float32).astype(bf),
        "tri4": np.tile(tri, (1, 4)).astype(bf),
        "negmask": negm.astype(bf),
        "biastab": btab,
        "eftab": np.stack([np.exp(SLOPES[h] * (np.arange(128) - 127.0)) for h in range(4)], axis=1).astype(np.float32),
        "resetmask": rmask.astype(bf),
    }


_CACHE = {}


def kernel(x, g_pre, w_in, lam_q1, lam_k1, lam_q2, lam_k2, g_sub_a, w_alpha, b_alpha, g_sub_b, w_up_a, w_up_b, w_out, g_post,
           _ntiles=NT, _debug=False, _cores=8, _stop=None):
    f = np.float32
    x = np.asarray(x, f)
    key = (_ntiles, _debug, _stop)
    if key not in _CACHE:
        _CACHE[key] = _build(_ntiles, _debug, _stop)
    nc = _CACHE[key]
    smalls = np.zeros((128, 12), f)
    smalls[:, 0:8] = np.asarray(g_pre, f)[0].reshape(8, 128).T
    smalls[:, 8] = np.asarray(g_sub_a, f)[0]
    smalls[:, 9] = np.asarray(g_sub_b, f)[0]
    smalls[:, 10:12] = np.asarray(b_alpha, f)[0].reshape(2, 128).T
    lamv = np.concatenate([np.asarray(v, f)[0] for v in (lam_q1, lam_k1, lam_q2, lam_k2)])[None, :].repeat(128, 0)
    shared = {
        "w_in": np.ascontiguousarray(np.asarray(w_in, f)[0]),
        "w_up_a": np.ascontiguousarray(np.asarray(w_up_a, f)[0]),
        "w_up_b": np.ascontiguousarray(np.asarray(w_up_b, f)[0]),
        "w_out": np.ascontiguousarray(np.asarray(w_out, f)[0]),
        "w_alpha": np.ascontiguousarray(np.asarray(w_alpha, f)[0]),
        "smalls": smalls,
        "lamv": np.ascontiguousarray(lamv),
        "gpost": np.ascontiguousarray(np.asarray(g_post, f)[0][None, :].repeat(128, 0)),
    }
    shared.update(_host_consts())
    in_maps = [dict(shared, x=np.ascontiguousarray(x[b])) for b in range(_cores)]
    res = run_bass_kernel_spmd(nc, in_maps, core_ids=list(range(_cores)))
    if _debug:
        return res
    return np.stack([res.results[b]["out"] for b in range(_cores)], axis=0).astype(np.float32)
```

```python
import math
from contextlib import ExitStack

import numpy as np
import ml_dtypes

import concourse.bass as bass
import concourse.mybir as mybir
from concourse.bass_utils import run_bass_kernel_spmd

F32 = mybir.dt.float32
BF16 = mybir.dt.bfloat16
AF = mybir.ActivationFunctionType
ALU = mybir.AluOpType
AX = mybir.AxisListType

S = 4096
D = 1024
T = 512
NT = S // T
DIN = 5648
EPS = 1e-6
LAM_INIT = 0.8 - 0.6 * math.exp(-0.3 * 0)
SLOPES = [2.0 ** (-8.0 * (h + 1) / 4) for h in range(4)]
C_QA, C_KA, C_VA, C_ZA, C_QB, C_KB, C_VB, C_ZB, C_LR, C_GA, C_GB = 0, 512, 1024, 1536, 2048, 2304, 2560, 3072, 3584, 3600, 4624
CHUNK_COL = [C_QA, C_KA, C_VA, C_ZA, C_QB, C_VB, C_ZB, C_GA, C_GA + 512, C_GB, C_GB + 512]
CH_QA, CH_KA, CH_VA, CH_ZA, CH_QK, CH_VB, CH_ZB, CH_GA0, CH_GA1, CH_GB0, CH_GB1, CH_WO0, CH_WO1 = range(13)
NEG = -30000.0
NSLOT = 4

ENGS = ("pe", "act", "dve", "pool", "sp")


class _Op:
    __slots__ = ("eng", "fn", "deps", "is_dma", "sem", "val", "needs_inc")

    def __init__(self, eng, fn, is_dma):
        self.eng = eng
        self.fn = fn
        self.deps = []
        self.is_dma = is_dma
        self.sem = None
        self.val = 0
        self.needs_inc = is_dma


class _Prog:
    def __init__(self, same_eng_sync=("act", "dve", "pool")):
        self.ops = []
        self.last_writer = {}
        self.readers = {}
        self.same_eng_sync = set(same_eng_sync)
        self.dma_slots = []

    def op(self, eng, fn, reads=(), writes=(), dma=None):
        o = _Op(eng, fn, dma is not None)
        deps = {}
        for k in reads:
            w = self.last_writer.get(k)
            if w is not None:
                deps[id(w)] = w
        for k in writes:
            w = self.last_writer.get(k)
            if w is not None:
                deps[id(w)] = w
            for r in self.readers.get(k, ()):
                deps[id(r)] = r
        o.deps = list(deps.values())
        for k in reads:
            self.readers.setdefault(k, []).append(o)
        for k in writes:
            self.last_writer[k] = o
            self.readers[k] = []
        if dma is not None:
            o.sem = dma
            if dma not in self.dma_slots:
                self.dma_slots.append(dma)
        self.ops.append(o)
        return o

    def emit(self, block_engines, sems):
        ops = self.ops
        for o in ops:
            for d in o.deps:
                if d.is_dma:
                    continue
                if d.eng != o.eng or (d.eng in self.same_eng_sync):
                    d.needs_inc = True
        cnt = {e: 0 for e in ENGS}
        dcnt = {}
        for o in ops:
            if o.is_dma:
                slot = o.sem
                dcnt[slot] = dcnt.get(slot, 0) + 16
                o.sem = sems[slot]
                o.val = dcnt[slot]
            elif o.needs_inc:
                cnt[o.eng] += 1
                o.sem = sems[o.eng]
                o.val = cnt[o.eng]
        same = self.same_eng_sync

        def make(eng_name):
            my_ops = [o for o in ops if o.eng == eng_name]

            def body(e):
                waited = {}
                for o in my_ops:
                    need = {}
                    for d in o.deps:
                        if (not d.is_dma) and d.eng == eng_name and eng_name not in same:
                            continue
                        key = id(d.sem)
                        if d.val > need.get(key, (None, 0))[1]:
                            need[key] = (d.sem, d.val)
                    for key, (s, v) in need.items():
                        if waited.get(key, 0) >= v:
                            continue
                        e.wait_ge(s, v)
                        waited[key] = v
                    ins = o.fn(e)
                    if o.needs_inc and ins is not None:
                        ins.then_inc(o.sem, 16 if o.is_dma else 1)

            return body

        for eng_name, deco in block_engines.items():
            deco(make(eng_name))


def _build(ntiles=NT, debug=False, stop_after=None):
    nc = bass.Bass("TRN2", target_bir_lowering=False)

    def din(name, shape, dt=F32):
        return nc.dram_tensor(name, shape, dt, kind="ExternalInput").ap()

    x = din("x", [S, D])
    w_in = din("w_in", [D, DIN])
    w_up_a = din("w_up_a", [512, D])
    w_up_b = din("w_up_b", [512, D])
    w_out = din("w_out", [D, D])
    w_alpha = din("w_alpha", [16, 256])
    smalls_d = din("smalls", [128, 12])
    lamv_d = din("lamv", [128, 256])
    gpost_d = din("gpost", [128, D])
    ident_d = din("ident", [128, 128], BF16)
    tri_d = din("tri4", [128, 512], BF16)
    negm_d = din("negmask", [128, 128], BF16)
    btab_d = din("biastab", [128, 128])
    ef_d = din("eftab", [128, 4])
    rmask_d = din("resetmask", [128, 512], BF16)
    out = nc.dram_tensor("out", [S, D], F32, kind="ExternalOutput").ap()
    wsc = nc.dram_tensor("wsc", [13, 128, 4096], BF16).ap()
    dbg = {}
    if debug:
        for nm, shp in (("d_hT", [128, 8, 512]), ("d_QT", [128, 4, 512]), ("d_oacc", [128, 4, 4, 128]),
                        ("d_ob", [128, 4, 512]), ("d_yT", [128, 8, 512]), ("d_gaT", [128, 4, 512]),
                        ("d_gbT", [128, 4, 512]),
                        ("d_eb", [128, 2, 512]), ("d_state", [128, 2, 128])):
            dbg[nm] = nc.dram_tensor(nm, shp, F32, kind="ExternalOutput").ap()

    P = _Prog()
    with ExitStack() as es:
        def SB(name, shape, dt):
            return es.enter_context(nc.sbuf_tensor("sb_" + name, shape, dt))

        def PSM(name, shape, dt):
            return es.enter_context(nc.psum_tensor("ps_" + name, shape, dt))

        KT = SB("KT", [128, 4, S], BF16)
        V = SB("V", [128, 4 * ntiles, 4, 130], BF16)
        wupA = SB("wupA", [128, 4, 1024], BF16)
        wupB = SB("wupB", [128, 4, 1024], BF16)
        wbuf = [SB(f"wbuf{i}", [128, 4096], BF16) for i in range(NSLOT)]
        wlr = SB("wlr", [128, 8, 16], BF16)
        walpha = SB("walpha", [16, 256], BF16)
        xn = [SB(f"xn{i}", [128, 1024], F32) for i in range(2)]
        xr = xn
        xa = [SB(f"xa{i}", [128, 1024], F32) for i in range(2)]
        hb = [SB(f"hb{i}", [128, 1024], BF16) for i in range(2)]
        hT = SB("hT", [128, 8, T], BF16)
        gated = [hb[i][:, 0:512] for i in range(2)]
        obf = [hb[i][:, 512:1024] for i in range(2)]
        QT = SB("QT", [128, 4, T], BF16)
        yT = SB("yT", [128, 8, T], BF16)
        szA = yT[:, 0:4, :]
        szB = yT[:, 4:8, :]
        gbT = SB("gbT", [128, 4, T], BF16)
        gl = SB("gl", [128, 2, T], F32)
        sq = gl[:, 0, :]
        lamv = gl[:, 0, 0:256]
        lamt = gl[:, 0, 256:384]
        wlr_st = gl[:, 1, 0:128].rearrange("p (kc c) -> p kc c", kc=8)
        cum = SB("cum", [128, 2, T], F32)
        walpha_st = cum[0:16, 0, 0:256]
        enb = cum
        qtT = SB("qtT", [128, 2, 2, T], BF16)
        ktT = SB("ktT", [128, 2, T], BF16)
        khat = SB("khat", [128, 4, 2, 128], BF16)
        vb = SB("vb", [128, 4, 512], BF16)
        lrT = SB("lrT", [16, T], BF16)
        state = SB("state", [128, 2, 128], F32)
        stbf = SB("stbf", [128, 2, 128], BF16)
        PT2 = [SB(f"PT{i}", [128, 2, 512], BF16) for i in range(2)]
        gatedA = SB("gatedA", [128, 4, 512], BF16)
        khT = gatedA[:, 0:2, :]
        oaf = [SB(f"oaf{i}", [128, 128], F32) for i in range(4)]
        junkb = SB("junkb", [128, 128], BF16)
        otmp = [SB(f"otmp{i}", [128, 128], F32) for i in range(2)]
        gaT = QT
        eb = SB("eb", [128, 2, T], F32)
        tga = [eb[:, 0, :], eb[:, 1, :]]
        ttmp = tga
        ztmp = [PT2[i][:].rearrange("p a b -> p (a b)").bitcast(F32) for i in range(2)]
        eblast = SB("eblast", [128, 2, 4], F32)
        tgb = [SB("tgb0", [128, 512], F32)] * 2
        ATs = [SB("ATs0", [128, 512], BF16)] * 2
        ident = SB("ident", [128, 128], BF16)
        tri4 = SB("tri4", [128, 512], BF16)
        negm = SB("negm", [128, 128], BF16)
        btab = SB("btab", [128, 128], F32)
        eft = SB("eft", [128, 4], F32)
        rmask = SB("rmask", [128, 512], BF16)
        gpost = SB("gpost", [128, D], F32)
        smalls = SB("smalls", [128, 12], F32)
        lams = SB("lams", [128, 8], F32)
        nbal = SB("nbal", [128, 2], F32)
        ss = SB("ss", [128, 32], F32)
        rs = SB("rs", [128, 32], F32)
        rstd = SB("rstd", [128, 32], F32)
        ssa = SB("ssa", [128, 16], F32)
        rsa = SB("rsa", [128, 16], F32)
        rstda = SB("rstda", [128, 16], F32)
        ssb = SB("ssb", [128, 16], F32)
        rsb = SB("rsb", [128, 16], F32)
        rstdb = SB("rstdb", [128, 16], F32)
        ssz = SB("ssz", [128, 64], F32)
        rsz = SB("rsz", [128, 32], F32)
        rstdz = SB("rstdz", [128, 32], F32)
        rz = SB("rz", [128, 8], F32)
        dbgbuf = SB("dbgbuf", [128, 8, 512], F32) if debug else None
        dbgoa = SB("dbgoa", [128, 4, 4, 128], F32) if debug else None
        pg4 = PSM("pg4", [128, 4, 512], F32)
        pg = [pg4[:, i, :] for i in range(4)]
        ptr = PSM("ptr", [128, 8, 128], BF16)
        poa = [PSM(f"poa{i}", [128, 512], F32) for i in range(3)]

        dma_names = (["stg%d" % i for i in range(8)] + ["wst%d" % i for i in range(NSLOT)] + ["wld%d" % i for i in range(NSLOT)]
                     + ["cst", "xn0", "xn1", "xr0", "xr1", "xa0", "xa1", "st0", "st1", "dbg"])
        sems = {}
        for nm in list(ENGS) + dma_names:
            sems[nm] = es.enter_context(nc.semaphore("s_" + nm))
        _build.sbuf_left = nc.sbuf_bytes_remaining
        block = es.enter_context(nc.Block())

        gctr = [0]

        def alloc_pg():
            b = gctr[0] % 4
            gctr[0] += 1
            return b

        def wkeys(slot):
            return [f"w{slot}_{i}" for i in range(8)]

        def ev_engine(i):
            return "dve"

        for dst, src, key in ((ident, ident_d, "ident"), (tri4, tri_d, "tri4"), (negm, negm_d, "negm"),
                              (btab, btab_d, "btab"), (eft, ef_d, "eft"), (rmask, rmask_d, "rmask"), (gpost, gpost_d, "gpost"),
                              (smalls, smalls_d, "smalls"), (lamv, lamv_d, "gl0"), (walpha_st, w_alpha, "cum0")):
            P.op("sp", lambda e, dst=dst, src=src: e.dma_start(out=dst[:], in_=src), writes=[key], dma="cst")
        P.op("sp", lambda e: e.dma_start(out=wlr_st[:], in_=w_in[:, C_LR:C_LR + 16].rearrange("(kc p) c -> p kc c", p=128)),
             writes=["gl1"], dma="cst")
        _last_c = P.ops[-1]
        for _k in ("ident", "tri4", "negm", "btab", "eft", "rmask", "gpost", "smalls", "gl0", "cum0", "gl1"):
            P.last_writer[_k] = _last_c
        P.op("pool", lambda e: e.memset(lams[:, 5:6], -0.5), writes=["nhalf"])
        P.op("pool", lambda e: e.memset(V[:, :, :, 128:130], 1.0), writes=["Vones"])
        P.op("pool", lambda e: e.memset(state[:], 0.0), writes=["state0", "state1"])
        P.op("pool", lambda e: e.memset(qtT[:], 0.0), writes=["qtT0", "qtT1"])
        P.op("pool", lambda e: e.memset(stbf[:], 0.0), writes=["stbf0", "stbf1"])
        P.op("dve", lambda e: e.tensor_tensor(out=lamt[:, 0:64], in0=lamv[:, 0:64], in1=lamv[:, 64:128], op=ALU.mult),
             reads=[], writes=["gl0"])
        P.op("dve", lambda e: e.tensor_tensor(out=lamt[:, 64:128], in0=lamv[:, 128:192], in1=lamv[:, 192:256], op=ALU.mult),
             reads=[], writes=["gl0"])
        P.op("dve", lambda e: e.reduce_sum(out=lams[:, 0:2], in_=lamt[:].rearrange("p (a b) -> p a b", a=2), axis=AX.X),
             reads=["gl0"], writes=["lams01"])
        P.op("act", lambda e: e.activation(out=lams[:, 2:4], in_=lams[:, 0:2], func=AF.Exp), reads=["lams01"], writes=["lams23"])
        P.op("dve", lambda e: e.tensor_tensor(out=lams[:, 4:5], in0=lams[:, 3:4], in1=lams[:, 2:3], op=ALU.subtract),
             reads=["lams23"], writes=["nlam"])
        P.op("dve", lambda e: e.tensor_scalar(out=lams[:, 4:5], in0=lams[:, 4:5], scalar1=-LAM_INIT, scalar2=None, op0=ALU.add),
             reads=["nlam"], writes=["nlam"])
        P.op("dve", lambda e: e.tensor_scalar(out=nbal[:], in0=smalls[:, 10:12], scalar1=-1.0, scalar2=None, op0=ALU.mult),
             reads=["smalls"], writes=["nbal"])
        P.op("dve", lambda e: e.tensor_copy(out=walpha[:], in_=walpha_st[:]), reads=["cum0"], writes=["walpha"])
        for kc in range(8):
            P.op("dve", lambda e, kc=kc: e.tensor_scalar(out=wlr[:, kc, :], in0=wlr_st[:, kc, :], scalar1=smalls[:, kc:kc + 1],
                                                        scalar2=None, op0=ALU.mult),
                 reads=["gl1", "smalls"], writes=["wlr"])

        stgK = [KT[:, i // 2, 2048 + (i % 2) * 1024:2048 + (i % 2 + 1) * 1024].bitcast(F32) for i in range(8)]
        stg_keys = [f"stgK{i}" for i in range(8)]
        pctr = [0]

        def conv_piece(src_ap, dst_ap, dst_keys, scale_ap, scale_c):
            i = pctr[0] % 8
            pctr[0] += 1
            sap = stgK[i]
            P.op("sp", lambda e: e.dma_start(out=sap, in_=src_ap), writes=[stg_keys[i]], dma=f"stg{i}")
            if scale_ap is None:
                if i % 2 == 0:
                    P.op("dve", lambda e: e.tensor_scalar(out=dst_ap, in0=sap, scalar1=scale_c, scalar2=None, op0=ALU.mult),
                         reads=[stg_keys[i]], writes=dst_keys)
                else:
                    P.op("act", lambda e: e.activation(out=dst_ap, in_=sap, func=AF.Copy, scale=scale_c),
                         reads=[stg_keys[i]], writes=dst_keys)
            elif i % 2 == 0 or scale_c != 1.0:
                P.op("dve", lambda e: e.tensor_scalar(out=dst_ap, in0=sap, scalar1=scale_ap, scalar2=scale_c,
                                                      op0=ALU.mult, op1=ALU.mult),
                     reads=[stg_keys[i], "smalls"], writes=dst_keys)
            else:
                P.op("act", lambda e: e.activation(out=dst_ap, in_=sap, func=AF.Copy, scale=scale_ap),
                     reads=[stg_keys[i], "smalls"], writes=dst_keys)

        def conv_chunk(c, slot):
            if c < 11:
                for kc in range(8):
                    conv_piece(w_in[kc * 128:(kc + 1) * 128, CHUNK_COL[c]:CHUNK_COL[c] + 512],
                               wbuf[slot][:, kc * 512:(kc + 1) * 512], [f"w{slot}_{kc}"], smalls[:, kc:kc + 1], 1.0)
            else:
                half = c - 11
                for kk in range(4):
                    kc = half * 4 + kk
                    for hh in range(2):
                        conv_piece(w_out[kc * 128:(kc + 1) * 128, hh * 512:(hh + 1) * 512],
                                   wbuf[slot][:, kk * 1024 + hh * 512: kk * 1024 + (hh + 1) * 512], [f"w{slot}_{kk * 2 + hh}"],
                                   None, 0.5)
            P.op("pool", lambda e: e.dma_start(out=wsc[c], in_=wbuf[slot][:]), reads=wkeys(slot),
                 writes=[f"wsc{c}"], dma=f"wst{slot}")

        def conv_up():
            for kc in range(4):
                for hh in range(2):
                    conv_piece(w_up_a[kc * 128:(kc + 1) * 128, hh * 512:(hh + 1) * 512], wupA[:, kc, hh * 512:(hh + 1) * 512],
                               ["wupA"], smalls[:, 8:9], (1.0 - LAM_INIT) * 0.5)
                    conv_piece(w_up_b[kc * 128:(kc + 1) * 128, hh * 512:(hh + 1) * 512], wupB[:, kc, hh * 512:(hh + 1) * 512],
                               ["wupB"], smalls[:, 9:10], 0.5)

        TILE_SEQ = [CH_VB, CH_ZB, CH_QK, CH_QA, CH_KA, CH_VA, CH_ZA, CH_GA0, CH_GB0, CH_GA1, CH_GB1, CH_WO0, CH_WO1]
        wseq = TILE_SEQ * ntiles
        wstate = {"loaded": 0, "acq": 0}
        slot_base = 0

        def emit_load():
            i = wstate["loaded"]
            if i >= len(wseq):
                return
            c = wseq[i]
            slot = (slot_base + i) % NSLOT
            if i < len(TILE_SEQ):
                conv_chunk(c, slot)
            else:
                P.op("sp", lambda e: e.dma_start(out=wbuf[slot][:], in_=wsc[c]), reads=[f"wsc{c}"], writes=wkeys(slot),
                     dma=f"wld{slot}")
            wstate["loaded"] += 1

        def acquire(c):
            i = wstate["acq"]
            assert wseq[i] == c, (wseq[i], c)
            assert i < wstate["loaded"]
            wstate["acq"] += 1
            return (slot_base + i) % NSLOT

        def release():
            emit_load()

        def xa_buf(s_):
            return xa[s_ % 2], [f"xa{s_ % 2}"]

        def load_xa(t_, s_):
            buf, keys = xa_buf(s_)
            g = 4 * t_ + s_
            P.op("sp", lambda e: e.dma_start(out=buf[:], in_=x[g * 128:(g + 1) * 128, :]), writes=keys, dma=f"xa{s_ % 2}")

        def load_xr(g):
            i = g % 2
            P.op("sp", lambda e: e.dma_start(out=xr[i][:], in_=x[g * 128:(g + 1) * 128, :]),
                 writes=[f"xn{i}", f"xn{i}b"], dma=f"xr{i}")

        HT_KEYS = ["hT0", "hT1", "hT2", "hT3"]

        def fm_matmuls(slot, j, rows=128, lhs_from=None):
            b = alloc_pg()
            for kc in range(8):
                if lhs_from is None:
                    lhsT = wbuf[slot][:, kc * 512 + j * 128: kc * 512 + j * 128 + rows]
                    rk = [f"w{slot}_{kc}"]
                else:
                    lhsT = lhs_from[:, kc, :]
                    rk = ["wlr"]
                P.op("pe", lambda e, lhsT=lhsT, kc=kc: e.matmul(out=pg[b][0:rows, :], lhsT=lhsT, rhs=hT[:, kc, :],
                                                                start=(kc == 0), stop=(kc == 7)),
                     reads=rk + HT_KEYS, writes=[f"pg{b}"])
            return b

        def tm_matmuls(slot, s):
            b = alloc_pg()
            for kc in range(8):
                P.op("pe", lambda e, kc=kc: e.matmul(out=pg[b][:, :], lhsT=hT[:, kc, s * 128:(s + 1) * 128],
                                                     rhs=wbuf[slot][:, kc * 512:(kc + 1) * 512],
                                                     start=(kc == 0), stop=(kc == 7)),
                     reads=[f"w{slot}_{kc}", f"hT{s}"], writes=[f"pg{b}"])
            return b

        def dump(name, src_ap, rkeys, shape):
            if not debug:
                return
            n = 1
            for d_ in shape[1:]:
                n *= d_
            view = dbgbuf[:].rearrange("p a b -> p (a b)")[:, 0:n]
            if len(shape) == 3:
                view = view.rearrange("p (a b) -> p a b", a=shape[1])
            elif len(shape) == 4:
                view = view.rearrange("p (a b c) -> p a b c", a=shape[1], b=shape[2])
            P.op("dve", lambda e: e.tensor_copy(out=view, in_=src_ap), reads=rkeys, writes=["dbgbuf"])
            P.op("sp", lambda e: e.dma_start(out=dbg[name], in_=view), reads=["dbgbuf"], dma="dbg")

        def pow_cols(dst, src, col0, n, rkeys, wkey):
            for q_ in range(n):
                P.op("pool", lambda e, q_=q_: e.tensor_tensor(out=dst[:, col0 + q_:col0 + q_ + 1], in0=src[:, col0 + q_:col0 + q_ + 1],
                                                             in1=lams[:, 5:6], op=ALU.pow),
                     reads=rkeys + ["nhalf"], writes=[f"{wkey}_{q_}"])

        def a_stats(t, s):
            junk = PT2[0][:].rearrange("p a b -> p (a b)")
            g = 4 * t + s
            xbuf, xkeys = xa_buf(s)
            P.op("act", lambda e: e.activation(out=junk, in_=xbuf[:], func=AF.Square, accum_out=ss[:, g:g + 1]),
                 reads=xkeys, writes=["PT0", f"ss{g}"])
            P.op("dve", lambda e: e.tensor_scalar(out=rs[:, g:g + 1], in0=ss[:, g:g + 1], scalar1=1.0 / D, scalar2=EPS,
                                                  op0=ALU.mult, op1=ALU.add), reads=[f"ss{g}"], writes=[f"rs{g}"])
            P.op("pool", lambda e: e.tensor_tensor(out=rstd[:, g:g + 1], in0=rs[:, g:g + 1], in1=lams[:, 5:6], op=ALU.pow),
                 reads=[f"rs{g}", "nhalf"], writes=[f"rstd{g}"])

        def a_norm_tr(t, s):
            g = 4 * t + s
            i = g % 2
            xbuf, xkeys = xa_buf(s)
            P.op("dve", lambda e: e.tensor_scalar(out=hb[i][:], in0=xbuf[:], scalar1=rstd[:, g:g + 1], scalar2=None, op0=ALU.mult),
                 reads=xkeys + [f"rstd{g}"], writes=[f"hbL{i}", f"hbR{i}"])
            if s + 2 < 4:
                load_xa(t, s + 2)
            for kc in range(8):
                P.op("pe", lambda e, kc=kc: e.transpose(out=ptr[:, kc, :], in_=hb[i][:, kc * 128:(kc + 1) * 128], identity=ident[:]),
                     reads=[f"hbL{i}", f"hbR{i}", "ident"], writes=["ptr"])
            if s % 2 == 0:
                P.op("dve", lambda e: e.tensor_copy(out=hT[:, :, s * 128:(s + 1) * 128], in_=ptr[:, :, :]),
                     reads=["ptr"], writes=[f"hT{s}"])
            else:
                P.op("act", lambda e: e.activation(out=hT[:, :, s * 128:(s + 1) * 128], in_=ptr[:, :, :], func=AF.Copy),
                     reads=["ptr"], writes=[f"hT{s}"])

        def phase_a(t, subs):
            for s in subs:
                a_stats(t, s)
            for s in subs:
                a_norm_tr(t, s)
            if debug and t == 0 and 3 in subs:
                dump("d_hT", hT[:], HT_KEYS, [128, 8, 512])

        def phase_b1(t):
            sl_vb = acquire(CH_VB)
            for s in range(2):
                b = tm_matmuls(sl_vb, s)
                P.op("dve", lambda e, s=s, b=b: e.tensor_copy(out=vb[:, s, :], in_=pg[b][:, :]), reads=[f"pg{b}"], writes=[f"vb{s}"])
            b = fm_matmuls(None, 0, rows=16, lhs_from=wlr)
            P.op("dve", lambda e, b=b: e.tensor_copy(out=lrT[:, :], in_=pg[b][0:16, :]), reads=[f"pg{b}"], writes=["lrT"])
            for c in range(2):
                b = alloc_pg()
                P.op("pe", lambda e, c=c, b=b: e.matmul(out=pg[b][:, :], lhsT=walpha[:, c * 128:(c + 1) * 128], rhs=lrT[:, :],
                                                        start=True, stop=True), reads=["walpha", "lrT"], writes=[f"pg{b}"])
                P.op("act", lambda e, c=c, b=b: e.activation(out=gl[:, c, :], in_=pg[b][:, :], func=AF.Exp, scale=-1.0,
                                                             bias=nbal[:, c:c + 1]), reads=[f"pg{b}", "nbal"], writes=[f"gl{c}"])
            for c in range(2):
                P.op("act", lambda e, c=c: e.activation(out=gl[:, c, :], in_=gl[:, c, :], func=AF.Ln, bias=1.0, scale=1.0),
                     reads=[f"gl{c}"], writes=[f"gl{c}"])
                P.op("dve", lambda e, c=c: e.tensor_tensor_scan(out=cum[:, c, :], data0=rmask[:, :], data1=gl[:, c, :], initial=0.0,
                                                                op0=ALU.mult, op1=ALU.add),
                     reads=[f"gl{c}", "rmask"], writes=[f"cum{c}"])
            for c in range(2):
                P.op("act", lambda e, c=c: e.activation(out=eb[:, c, :], in_=cum[:, c, :], func=AF.Exp, scale=-1.0 / 16.0),
                     reads=[f"cum{c}"], writes=[f"tga{c}"])
                P.op("act", lambda e, c=c: e.activation(out=enb[:, c, :], in_=cum[:, c, :], func=AF.Exp, scale=1.0 / 16.0),
                     reads=[f"cum{c}"], writes=[f"cum{c}"])
            if debug and t == 0:
                dump("d_eb", eb[:], ["tga0", "tga1"], [128, 2, 512])
            for s in range(2, 4):
                b = tm_matmuls(sl_vb, s)
                P.op("dve", lambda e, s=s, b=b: e.tensor_copy(out=vb[:, s, :], in_=pg[b][:, :]), reads=[f"pg{b}"], writes=[f"vb{s}"])
            release()
            sl_zb = acquire(CH_ZB)
            for s in range(4):
                b = tm_matmuls(sl_zb, s)
                i = s % 2
                P.op("act", lambda e, i=i, b=b: e.activation(out=ztmp[i], in_=pg[b][:, :], func=AF.Tanh, scale=0.5),
                     reads=[f"pg{b}"], writes=[f"PT{i}"])
                P.op("dve", lambda e, i=i, b=b, s=s: e.scalar_tensor_tensor(out=szB[:, s, :], in0=ztmp[i], scalar=1.0,
                                                                            in1=pg[b][:, :], op0=ALU.add, op1=ALU.mult),
                     reads=[f"pg{b}", f"PT{i}"], writes=[f"szB{s}", f"yT{4 + s}"])
            release()
            sl_qk = acquire(CH_QK)
            for c in range(2):
                b = fm_matmuls(sl_qk, c)
                for hh in range(2):
                    P.op("dve", lambda e, c=c, hh=hh, b=b: e.scalar_tensor_tensor(
                        out=qtT[64 * hh:64 * hh + 64, c, hh, :], in0=pg[b][64 * hh:64 * hh + 64, :], scalar=0.125,
                        in1=eb[64 * hh:64 * hh + 64, c, :], op0=ALU.mult, op1=ALU.mult),
                        reads=[f"pg{b}", f"tga{c}"], writes=[f"qtT{c}"])
            for c in range(2):
                b = fm_matmuls(sl_qk, 2 + c)
                P.op("dve", lambda e, c=c, b=b: e.tensor_tensor(out=ktT[:, c, :], in0=pg[b][:, :], in1=enb[:, c, :], op=ALU.mult),
                     reads=[f"pg{b}", f"cum{c}"], writes=[f"ktT{c}"])
                for s in range(4):
                    P.op("dve", lambda e, c=c, s=s, b=b: e.scalar_tensor_tensor(
                        out=khT[:, c, s * 128:(s + 1) * 128], in0=pg[b][:, s * 128:(s + 1) * 128],
                        scalar=eb[:, c, s * 128 + 127:s * 128 + 128], in1=enb[:, c, s * 128:(s + 1) * 128],
                        op0=ALU.mult, op1=ALU.mult),
                        reads=[f"pg{b}", f"cum{c}", f"tga{c}"], writes=[f"gatedA{c}"])
                P.op("dve", lambda e, c=c: e.tensor_copy(out=eblast[:, c, :],
                                                         in_=eb[:, c, :].rearrange("p (s j) -> p s j", j=128)[:, :, 127]),
                     reads=[f"tga{c}"], writes=[f"eblast{c}"])
            release()
            for s in range(4):
                for c in range(2):
                    P.op("pe", lambda e, s=s, c=c: e.transpose(out=ptr[:, c * 4 + s, :], in_=khT[:, c, s * 128:(s + 1) * 128],
                                                               identity=ident[:]),
                         reads=[f"gatedA{c}", "ident"], writes=["ptr"])
            for c in range(2):
                P.op("dve", lambda e, c=c: e.tensor_copy(out=khat[:, :, c, :], in_=ptr[:, c * 4:(c + 1) * 4, :]),
                     reads=["ptr"], writes=[f"khat{c}"])

        def phase_b2(t):
            sl_qa = acquire(CH_QA)
            for j in range(4):
                b = fm_matmuls(sl_qa, j)
                P.op("dve", lambda e, j=j, b=b: e.tensor_scalar(out=QT[:, j, :], in0=pg[b][:, :], scalar1=0.125, scalar2=None,
                                                                op0=ALU.mult), reads=[f"pg{b}"], writes=[f"QT{j}", "gaT0", "gaT1", "gaT2", "gaT3"])
            release()
            sl_ka = acquire(CH_KA)
            for j in range(4):
                b = fm_matmuls(sl_ka, j)
                P.op("dve", lambda e, j=j, b=b: e.tensor_copy(out=KT[:, j, t * T:(t + 1) * T], in_=pg[b][:, :]),
                     reads=[f"pg{b}"], writes=[f"KT{j}"] + (stg_keys if t == 4 else []))
            release()
            sl_va = acquire(CH_VA)
            for s in range(4):
                b = tm_matmuls(sl_va, s)
                g = 4 * t + s
                for hd in range(4):
                    P.op("dve", lambda e, g=g, b=b, hd=hd: e.tensor_scalar(out=V[:, g, hd, 0:128], in0=pg[b][:, hd * 128:(hd + 1) * 128],
                                                                          scalar1=eft[:, hd:hd + 1], scalar2=None, op0=ALU.mult),
                         reads=[f"pg{b}", "eft"], writes=["V"])
                P.op("dve", lambda e, g=g: e.tensor_copy(out=V[:, g, :, 128:129], in_=eft[:, :].unsqueeze(2)),
                     reads=["eft"], writes=["V"])
            release()
            sl_za = acquire(CH_ZA)
            for s in range(4):
                b = tm_matmuls(sl_za, s)
                i = s % 2
                P.op("act", lambda e, i=i, b=b: e.activation(out=ztmp[i], in_=pg[b][:, :], func=AF.Tanh, scale=0.5),
                     reads=[f"pg{b}"], writes=[f"PT{i}"])
                P.op("dve", lambda e, i=i, b=b, s=s: e.scalar_tensor_tensor(out=szA[:, s, :], in0=ztmp[i], scalar=1.0,
                                                                            in1=pg[b][:, :], op0=ALU.add, op1=ALU.mult),
                     reads=[f"pg{b}", f"PT{i}"], writes=[f"szA{s}", f"yT{s}"])
            release()
            if debug and t == 0:
                dump("d_QT", QT[:], ["QT0", "QT1", "QT2", "QT3"], [128, 4, 512])

        def gla_at(t, s):
            bA = alloc_pg()
            for c in range(2):
                P.op("pe", lambda e, c=c: e.matmul(
                    out=pg[bA][:, c * 256:(c + 1) * 256], lhsT=ktT[:, c, s * 128:(s + 1) * 128],
                    rhs=qtT[:, c, :, s * 128:(s + 1) * 128], start=True, stop=True),
                    reads=[f"ktT{c}", f"qtT{c}"], writes=[f"pg{bA}"])
            P.op("dve", lambda e: e.tensor_tensor(out=ATs[0][:, :], in0=pg[bA][:, :], in1=tri4[:, :], op=ALU.mult),
                 reads=[f"pg{bA}", "tri4"], writes=["ATs0"])

        def gla_rest(t, s):
            bS = alloc_pg()
            for c in range(2):
                P.op("pe", lambda e, c=c: e.matmul(out=pg[bS][:, c * 256:(c + 1) * 256], lhsT=khat[:, s, c, :],
                                                   rhs=vb[:, s, c * 256:(c + 1) * 256], start=True, stop=True),
                     reads=[f"khat{c}", f"vb{s}"], writes=[f"pg{bS}"])
            bO = alloc_pg()
            for hd in range(4):
                c, hh = hd // 2, hd % 2
                P.op("pe", lambda e, c=c, hh=hh, hd=hd: e.matmul(
                    out=pg[bO][:, hd * 128:(hd + 1) * 128], lhsT=qtT[:, c, hh, s * 128:(s + 1) * 128],
                    rhs=stbf[:, c, :], start=True, stop=False),
                    reads=[f"qtT{c}", f"stbf{c}"], writes=[f"pg{bO}"])
                P.op("pe", lambda e, hd=hd: e.matmul(
                    out=pg[bO][:, hd * 128:(hd + 1) * 128], lhsT=ATs[0][:, hd * 128:(hd + 1) * 128],
                    rhs=vb[:, s, hd * 128:(hd + 1) * 128], start=False, stop=True),
                    reads=["ATs0", f"vb{s}"], writes=[f"pg{bO}"])
            for c in range(2):
                for hh in range(2):
                    P.op("dve", lambda e, c=c, hh=hh: e.scalar_tensor_tensor(
                        out=state[64 * hh:64 * hh + 64, c, :], in0=state[64 * hh:64 * hh + 64, c, :],
                        scalar=eblast[64 * hh:64 * hh + 64, c, s:s + 1],
                        in1=pg[bS][64 * hh:64 * hh + 64, c * 256 + hh * 128:c * 256 + (hh + 1) * 128],
                        op0=ALU.mult, op1=ALU.add),
                        reads=[f"pg{bS}", f"eblast{c}", f"state{c}"], writes=[f"state{c}"])
                P.op("pool", lambda e, c=c: e.tensor_copy(out=stbf[:, c, :], in_=state[:, c, :]),
                     reads=[f"state{c}"], writes=[f"stbf{c}"])
            oi = s % 2
            P.op("dve", lambda e: e.tensor_copy(out=gl[:, oi, :], in_=pg[bO][:, :]), reads=[f"pg{bO}"], writes=[f"gl{oi}"])
            if debug and t == 0:
                P.op("dve", lambda e: e.tensor_copy(out=dbgbuf[:, s, :], in_=gl[:, oi, :]), reads=[f"gl{oi}"], writes=["dbgbuf"])

        def gla_out_ew(t, s):
            oi = s % 2
            for hd in range(4):
                P.op("dve", lambda e, hd=hd: e.scalar_tensor_tensor(out=junkb[:, :], in0=gl[:, oi, hd * 128:(hd + 1) * 128], scalar=1.0,
                                                                    in1=gl[:, oi, hd * 128:(hd + 1) * 128], op0=ALU.mult, op1=ALU.mult,
                                                                    accum_out=ssb[:, s * 4 + hd:s * 4 + hd + 1]),
                     reads=[f"gl{oi}"], writes=["junkb", f"ssb{s}_{hd}"])
            P.op("dve", lambda e: e.tensor_scalar(out=rsb[:, s * 4:(s + 1) * 4], in0=ssb[:, s * 4:(s + 1) * 4],
                                                  scalar1=1.0 / 128, scalar2=EPS, op0=ALU.mult, op1=ALU.add),
                 reads=[f"ssb{s}_{hd}" for hd in range(4)], writes=[f"rsb{s}"])
            pow_cols(rstdb, rsb, s * 4, 4, [f"rsb{s}"], f"rstdb{s}")

        def gla_out_ew2(t, s):
            oi = s % 2
            gi = s % 2
            for hd in range(4):
                P.op("dve", lambda e, hd=hd: e.scalar_tensor_tensor(
                    out=gated[gi][:, hd * 128:(hd + 1) * 128], in0=gl[:, oi, hd * 128:(hd + 1) * 128],
                    scalar=rstdb[:, s * 4 + hd:s * 4 + hd + 1], in1=szB[:, s, hd * 128:(hd + 1) * 128],
                    op0=ALU.mult, op1=ALU.mult),
                    reads=[f"gl{oi}", f"rstdb{s}_{hd}", f"szB{s}"], writes=[f"hbL{gi}"])

        def gla_out_pe(t, s):
            gi = s % 2
            for hd in range(4):
                P.op("pe", lambda e, hd=hd: e.transpose(out=ptr[:, hd, :], in_=gated[gi][:, hd * 128:(hd + 1) * 128],
                                                        identity=ident[:]),
                     reads=[f"hbL{gi}", "ident"], writes=["ptr"])
            P.op("dve", lambda e: e.tensor_copy(out=gbT[:, :, s * 128:(s + 1) * 128], in_=ptr[:, 0:4, :]),
                 reads=["ptr"], writes=[f"gbT{s}"])

        def acc_idx(m, a):
            return 2 * a + m

        def acc_ap(m, a, lo, hi):
            i_ = acc_idx(m, a)
            return poa[i_ // 3][:, (i_ % 3) * 130 + lo:(i_ % 3) * 130 + hi]

        pair_ctr = [0]

        def alloc_pair():
            bp = 2 * (pair_ctr[0] % 2)
            pair_ctr[0] += 1
            return bp

        def attention_jobs(t, h, deferred):
            nkb = 4 * t + 4

            def emit_qk_pair(kb):
                r = kb - 4 * t
                bp = alloc_pair()
                c0 = 0 if r < 0 else 128 * r
                for m in range(2):
                    b = bp + m
                    lhsT = KT[64 * m:64 * m + 64, h, kb * 128:(kb + 1) * 128]
                    if r < 0:
                        P.op("pe", lambda e, b=b, lhsT=lhsT, m=m: e.matmul(out=pg[b][:, :], lhsT=lhsT, rhs=QT[64 * m:64 * m + 64, h, :],
                                                                           start=True, stop=True),
                             reads=[f"KT{h}", f"QT{h}"], writes=[f"pg{b}"])
                    else:
                        P.op("pe", lambda e, b=b, lhsT=lhsT, m=m: e.matmul(out=pg[b][:, c0:c0 + 128], lhsT=lhsT,
                                                                           rhs=QT[64 * m:64 * m + 64, h, c0:c0 + 128],
                                                                           start=True, stop=False),
                             reads=[f"KT{h}", f"QT{h}"], writes=[f"pg{b}"])
                        P.op("pe", lambda e, b=b: e.matmul(out=pg[b][:, c0:c0 + 128], lhsT=ident[:, :], rhs=negm[:, :],
                                                           start=False, stop=True),
                             reads=["ident", "negm"], writes=[f"pg{b}"])
                        if c0 + 128 < 512:
                            P.op("pe", lambda e, b=b, lhsT=lhsT, m=m: e.matmul(out=pg[b][:, c0 + 128:512], lhsT=lhsT,
                                                                               rhs=QT[64 * m:64 * m + 64, h, c0 + 128:512],
                                                                               start=True, stop=True),
                                 reads=[f"KT{h}", f"QT{h}"], writes=[f"pg{b}"])
                pp = kb % 2
                cbias = SLOPES[h] * (128.0 * r - 129.0)
                P.op("act", lambda e: e.activation(out=PT2[pp][:, :, c0:512], in_=pg4[:, bp:bp + 2, c0:512], func=AF.Exp,
                                                   bias=cbias, scale=1.0),
                     reads=[f"pg{bp}", f"pg{bp + 1}"], writes=[f"PT{pp}"])

            def emit_pv(kb, m):
                r = kb - 4 * t
                pp = kb % 2
                for a in range(max(r, 0), 4):
                    i_ = acc_idx(m, a)
                    P.op("pe", lambda e, a=a, i_=i_: e.matmul(out=acc_ap(m, a, 0, 129), lhsT=PT2[pp][:, m, a * 128:(a + 1) * 128],
                                                              rhs=V[:, kb, h, 0:129], start=(kb == 0 and i_ in (0, 4, 6)), stop=False,
                                                              skip_group_check=True),
                         reads=[f"PT{pp}", "V", "Vones"], writes=[f"poa{i_ // 3}"])

            for kb in range(nkb + 1):
                if kb < nkb:
                    emit_qk_pair(kb)
                if kb >= 1:
                    emit_pv(kb - 1, 0)
                    emit_pv(kb - 1, 1)
                    r_done = (kb - 1) - 4 * t
                    if r_done >= 1:
                        attention_evac(t, h, r_done)
                if t == 0:
                    if kb == 1:
                        for kk in sorted(deferred):
                            for f in deferred[kk]:
                                f()
                else:
                    for f in deferred.get(kb, ()):
                        f()

        def attention_evac(t, h, stage):
            bank = stage - 1
            n = 3 if bank < 2 else 2
            i0 = 3 * bank
            zs = poa[bank][:, 0:n * 130].rearrange("p (i c) -> p i c", c=130)[:, :, 128]
            P.op("dve", lambda e: e.reciprocal(out=rz[:, i0:i0 + n], in_=zs),
                 reads=[f"poa{bank}"], writes=[f"rz{i0 + k}" for k in range(n)])
            for i_ in range(i0, i0 + n):
                if i_ % 2 == 1:
                    P.op("dve", lambda e, i_=i_: e.tensor_scalar(out=rz[:, i_:i_ + 1], in0=rz[:, i_:i_ + 1], scalar1=lams[:, 4:5],
                                                                 scalar2=None, op0=ALU.mult),
                         reads=[f"rz{i_}", "nlam"], writes=[f"rz{i_}"])

            def part0(a):
                i_ = acc_idx(0, a)
                oi2 = a % 2
                P.op("dve", lambda e: e.tensor_scalar(out=otmp[oi2][:, :], in0=acc_ap(0, a, 0, 128), scalar1=rz[:, i_:i_ + 1],
                                                      scalar2=None, op0=ALU.mult),
                     reads=[f"poa{i_ // 3}", f"rz{i_}"], writes=[f"otmp{oi2}"])

            def part1(a):
                i_ = acc_idx(1, a)
                oi2 = a % 2
                P.op("dve", lambda e: e.scalar_tensor_tensor(out=oaf[a][:, :], in0=acc_ap(1, a, 0, 128), scalar=rz[:, i_:i_ + 1],
                                                             in1=otmp[oi2][:, :], op0=ALU.mult, op1=ALU.add),
                     reads=[f"poa{i_ // 3}", f"rz{i_}", f"otmp{oi2}"], writes=[f"oaf{a}"])
                if debug and t == 0:
                    P.op("dve", lambda e: e.tensor_copy(out=dbgoa[:, a, h, :], in_=oaf[a][:, :]), reads=[f"oaf{a}"], writes=["dbgoa"])

            if stage == 1:
                part0(0); part1(0); part0(1)
            elif stage == 2:
                part1(1); part0(2); part1(2)
            else:
                part0(3); part1(3)

        def attention_norm(t, h):
            for a in range(4):
                ci = a * 4 + h
                P.op("dve", lambda e, a=a, ci=ci: e.scalar_tensor_tensor(out=junkb[:, :], in0=oaf[a][:, :], scalar=1.0, in1=oaf[a][:, :],
                                                                         op0=ALU.mult, op1=ALU.mult, accum_out=ssa[:, ci:ci + 1]),
                     reads=[f"oaf{a}"], writes=["junkb", f"ssa{ci}"])
            for a in range(4):
                ci = a * 4 + h
                P.op("dve", lambda e, ci=ci: e.tensor_scalar(out=rsa[:, ci:ci + 1], in0=ssa[:, ci:ci + 1], scalar1=1.0 / 128, scalar2=EPS,
                                                            op0=ALU.mult, op1=ALU.add), reads=[f"ssa{ci}"], writes=[f"rsa{ci}"])
                P.op("pool", lambda e, ci=ci: e.tensor_tensor(out=rstda[:, ci:ci + 1], in0=rsa[:, ci:ci + 1], in1=lams[:, 5:6], op=ALU.pow),
                     reads=[f"rsa{ci}", "nhalf"], writes=[f"rstda{ci}"])

        def attention_norm2(t, h):
            for a in range(4):
                ci = a * 4 + h
                P.op("dve", lambda e, a=a, ci=ci: e.scalar_tensor_tensor(
                    out=gatedA[:, a, h * 128:(h + 1) * 128], in0=oaf[a][:, :], scalar=rstda[:, ci:ci + 1],
                    in1=szA[:, a, h * 128:(h + 1) * 128], op0=ALU.mult, op1=ALU.mult),
                    reads=[f"oaf{a}", f"rstda{ci}", f"szA{a}"], writes=[f"gatedA{a}"])

        def attn_post(t):
            for a in range(4):
                for hd in range(4):
                    P.op("pe", lambda e, a=a, hd=hd: e.transpose(out=ptr[:, 4 + hd, :], in_=gatedA[:, a, hd * 128:(hd + 1) * 128],
                                                                 identity=ident[:]),
                         reads=[f"gatedA{a}", "ident"], writes=["ptr"])
                P.op("dve", lambda e, a=a: e.tensor_copy(out=gaT[:, :, a * 128:(a + 1) * 128], in_=ptr[:, 4:8, :]),
                     reads=["ptr"], writes=[f"gaT{a}", "QT0", "QT1", "QT2", "QT3"])
            if debug and t == 0:
                dump("d_gaT", gaT[:], ["gaT0", "gaT1", "gaT2", "gaT3"], [128, 4, 512])
                dump("d_gbT", gbT[:], ["gbT0", "gbT1", "gbT2", "gbT3"], [128, 4, 512])

        def phase_e(t, mid):
            gaT_keys = ["gaT0", "gaT1", "gaT2", "gaT3"]
            gbT_keys = ["gbT0", "gbT1", "gbT2", "gbT3"]
            ytmp = [gl[:, 0, :], gl[:, 1, :]]
            ytk = ["gl0", "gl1"]
            sl_ga0 = acquire(CH_GA0)
            sl_gb0 = acquire(CH_GB0)
            sl_ga1 = sl_gb1 = None
            for j in range(8):
                if j == 4:
                    release()
                    release()
                    sl_ga1 = acquire(CH_GA1)
                    sl_gb1 = acquire(CH_GB1)
                sga = sl_ga0 if j < 4 else sl_ga1
                sgb = sl_gb0 if j < 4 else sl_gb1
                jj = j % 4
                ti = j % 2
                bga = fm_matmuls(sga, jj)
                P.op("act", lambda e, ti=ti, bga=bga: e.activation(out=tga[ti], in_=pg[bga][:, :], func=AF.Tanh, scale=0.5),
                     reads=[f"pg{bga}"], writes=[f"tga{ti}"])
                bgb = fm_matmuls(sgb, jj)
                P.op("act", lambda e, ti=ti, bgb=bgb: e.activation(out=tgb[ti][:, :], in_=pg[bgb][:, :], func=AF.Tanh, scale=0.5),
                     reads=[f"pg{bgb}"], writes=["tgb0"])
                if j == 0:
                    mid()
                bya = alloc_pg()
                for kc in range(4):
                    P.op("pe", lambda e, kc=kc, j=j, bya=bya: e.matmul(out=pg[bya][:, :], lhsT=wupA[:, kc, j * 128:(j + 1) * 128],
                                                                      rhs=gaT[:, kc, :], start=(kc == 0), stop=(kc == 3)),
                         reads=["wupA"] + gaT_keys, writes=[f"pg{bya}"])
                P.op("dve", lambda e, ti=ti, bya=bya: e.scalar_tensor_tensor(out=ytmp[ti], in0=tga[ti], scalar=1.0, in1=pg[bya][:, :],
                                                                             op0=ALU.add, op1=ALU.mult),
                     reads=[f"pg{bya}", f"tga{ti}"], writes=[ytk[ti]])
                byb = alloc_pg()
                for kc in range(4):
                    P.op("pe", lambda e, kc=kc, j=j, byb=byb: e.matmul(out=pg[byb][:, :], lhsT=wupB[:, kc, j * 128:(j + 1) * 128],
                                                                      rhs=gbT[:, kc, :], start=(kc == 0), stop=(kc == 3)),
                         reads=["wupB"] + gbT_keys, writes=[f"pg{byb}"])
                P.op("dve", lambda e, ti=ti, byb=byb: e.scalar_tensor_tensor(out=tgb[ti][:, :], in0=tgb[ti][:, :], scalar=1.0, in1=pg[byb][:, :],
                                                                             op0=ALU.add, op1=ALU.mult),
                     reads=[f"pg{byb}", "tgb0"], writes=["tgb0"])
                P.op("dve", lambda e, ti=ti, j=j: e.tensor_tensor(out=yT[:, j, :], in0=ytmp[ti], in1=tgb[ti][:, :], op=ALU.add),
                     reads=[ytk[ti], "tgb0"], writes=[f"yT{j}", (f"szA{j}" if j < 4 else f"szB{j - 4}")])
            release()
            release()
            if debug and t == 0:
                dump("d_yT", yT[:], [f"yT{j}" for j in range(8)], [128, 8, 512])

        def phase_f(t, hooks):
            sl_wo0 = acquire(CH_WO0)
            sl_wo1 = acquire(CH_WO1)
            zg = [cum, gl]
            zgk = [["cum0", "cum1"], ["gl0", "gl1"]]
            load_xr(4 * t)
            load_xr(4 * t + 1)

            def zbank(s, hh):
                i = 2 * s + hh
                if 4 <= i < 7:
                    return poa[i - 4][:, :], f"poa{i - 4}"
                b = alloc_pg()
                return pg[b], f"pg{b}"

            def f_mm(s):
                g = 4 * t + s
                for hh in range(2):
                    zb_, zk_ = zbank(s, hh)
                    for kc in range(8):
                        slw = sl_wo0 if kc < 4 else sl_wo1
                        kk = kc % 4
                        P.op("pe", lambda e, kc=kc, kk=kk, slw=slw, hh=hh, zb_=zb_: e.matmul(
                            out=zb_, lhsT=yT[:, kc, s * 128:(s + 1) * 128],
                            rhs=wbuf[slw][:, kk * 1024 + hh * 512: kk * 1024 + (hh + 1) * 512], start=(kc == 0), stop=(kc == 7)),
                            reads=[f"yT{kc}", f"w{slw}_{kk * 2 + hh}"], writes=[zk_])
                    P.op("act", lambda e, zb_=zb_, hh=hh: e.activation(out=ttmp[hh], in_=zb_, func=AF.Square,
                                                                      accum_out=ssz[:, 2 * g + hh:2 * g + hh + 1]),
                         reads=[zk_], writes=[f"tga{hh}", f"ssz{g}_{hh}"])
                    P.op("dve", lambda e, zb_=zb_, hh=hh: e.tensor_tensor(out=zg[s % 2][:, hh, :], in0=zb_,
                                                                         in1=gpost[:, hh * 512:(hh + 1) * 512], op=ALU.mult),
                         reads=[zk_, "gpost", f"ssz{g}_{hh}"], writes=[zgk[s % 2][hh]])
                P.op("dve", lambda e: e.tensor_tensor(out=rsz[:, g:g + 1], in0=ssz[:, 2 * g:2 * g + 1], in1=ssz[:, 2 * g + 1:2 * g + 2],
                                                      op=ALU.add), reads=[f"ssz{g}_0", f"ssz{g}_1"], writes=[f"rsz{g}"])
                P.op("dve", lambda e: e.tensor_scalar(out=rsz[:, g:g + 1], in0=rsz[:, g:g + 1], scalar1=1.0 / D, scalar2=EPS,
                                                      op0=ALU.mult, op1=ALU.add), reads=[f"rsz{g}"], writes=[f"rsz{g}"])
                P.op("pool", lambda e: e.tensor_tensor(out=rstdz[:, g:g + 1], in0=rsz[:, g:g + 1], in1=lams[:, 5:6], op=ALU.pow),
                     reads=[f"rsz{g}", "nhalf"], writes=[f"rstdz{g}"])

            def f_fin(s):
                g = 4 * t + s
                ri = g % 2
                for hh in range(2):
                    xk = f"xn{ri}" if hh == 0 else f"xn{ri}b"
                    P.op("dve", lambda e, hh=hh: e.scalar_tensor_tensor(out=xr[ri][:, hh * 512:(hh + 1) * 512], in0=zg[s % 2][:, hh, :],
                                                                        scalar=rstdz[:, g:g + 1], in1=xr[ri][:, hh * 512:(hh + 1) * 512],
                                                                        op0=ALU.mult, op1=ALU.add),
                         reads=[zgk[s % 2][hh], f"rstdz{g}", xk], writes=[xk])
                P.op("pool", lambda e: e.dma_start(out=out[g * 128:(g + 1) * 128, :], in_=xr[ri][:]),
                     reads=[f"xn{ri}", f"xn{ri}b"], dma=f"st{ri}")
                if s + 2 < 4:
                    load_xr(g + 2)

            steps = [("mm", 0), ("mm", 1), ("fin", 0), ("mm", 2), ("fin", 1), ("mm", 3), ("fin", 2), ("fin", 3)]
            for kind, s_ in steps:
                (f_mm if kind == "mm" else f_fin)(s_)
                for f in hooks.get((kind, s_), ()):
                    f()
            release()
            release()

        load_xa(0, 0)
        load_xa(0, 1)
        for t in range(ntiles):
            if stop_after == "prologue":
                break
            if t == 0:
                phase_a(t, [0, 1])
                phase_a(t, [2, 3])
                for _ in range(NSLOT):
                    emit_load()
            if stop_after == "a":
                break
            phase_b1(t)
            if stop_after == "b1":
                break
            gla_at(t, 0)
            gla_rest(t, 0)
            phase_b2(t)
            if stop_after == "b2":
                break
            if t == 0:
                conv_up()
            deferred = {1: [lambda: gla_at(t, 1), lambda: gla_rest(t, 1)], 2: [lambda: gla_out_ew(t, 0)],
                        3: [lambda: gla_out_ew2(t, 0)], 4: [lambda: gla_out_pe(t, 0)]}
            for s in range(4):
                attention_jobs(t, s, deferred)
                deferred = {1: [], 2: [lambda s=s: attention_norm(t, s)], 3: [lambda s=s: attention_norm2(t, s)], 4: []}
                if s < 3:
                    deferred[2].append(lambda s=s: gla_out_ew(t, s + 1))
                    deferred[3].append(lambda s=s: gla_out_ew2(t, s + 1))
                    deferred[4].append(lambda s=s: gla_out_pe(t, s + 1))
                if s < 2:
                    deferred[1].append(lambda s=s: gla_at(t, s + 2))
                    deferred[1].append(lambda s=s: gla_rest(t, s + 2))
            for kk in (1, 2, 3, 4):
                for f in deferred[kk]:
                    f()
            if debug and t == 0:
                P.op("sp", lambda e: e.dma_start(out=dbg["d_ob"], in_=dbgbuf[:, 0:4, :]), reads=["dbgbuf"], dma="dbg")
                dump("d_oacc", dbgoa[:], ["dbgoa"], [128, 4, 4, 128])
                dump("d_state", state[:], ["state0", "state1"], [128, 2, 128])
            if t + 1 < ntiles:
                load_xa(t + 1, 0)
                load_xa(t + 1, 1)
                a_stats(t + 1, 0)
                a_stats(t + 1, 1)
            phase_e(t, lambda: attn_post(t))
            if t + 1 < ntiles:
                nt_ = t + 1
                hooks = {("mm", 2): [lambda: a_norm_tr(nt_, 0)], ("mm", 3): [lambda: a_norm_tr(nt_, 1), lambda: a_stats(nt_, 2)],
                         ("fin", 2): [lambda: a_norm_tr(nt_, 2), lambda: a_stats(nt_, 3)],
                         ("fin", 3): [lambda: a_norm_tr(nt_, 3)]}
            else:
                hooks = {}
            phase_f(t, hooks)
        P.op("sp", lambda e: None, reads=[], writes=["xn0", "xn0b", "xn1", "xn1b"] + (["dbgbuf"] if debug else []))
        P.emit({"pe": block.tensor, "act": block.scalar, "dve": block.vector, "pool": block.gpsimd, "sp": block.sync}, sems)
    return nc


def _host_consts():
    bf = ml_dtypes.bfloat16
    k = np.arange(128)[:, None]
    q = np.arange(128)[None, :]
    tri = (q >= k).astype(np.float32)
    negm = np.where(k > q, NEG, 0.0).astype(np.float32)
    btab = np.zeros((128, 128), np.float32)
    for h in range(4):
        for i in range(32):
            btab[:, h * 32 + i] = SLOPES[h] * (np.arange(128) + 128.0 * (i - 28) - 256.0)
    rmask = np.ones((128, 512), np.float32)
    rmask[:, ::128] = 0.0
    return {
        "ident": np.eye(128, dtype=np.float32).astype(bf),
        "tri4": np.tile(tri, (1, 4)).astype(bf),
        "negmask": negm.astype(bf),
        "biastab": btab,
        "eftab": np.stack([np.exp(SLOPES[h] * (np.arange(128) - 127.0)) for h in range(4)], axis=1).astype(np.float32),
        "resetmask": rmask.astype(bf),
    }


_CACHE = {}


def kernel(x, g_pre, w_in, lam_q1, lam_k1, lam_q2, lam_k2, g_sub_a, w_alpha, b_alpha, g_sub_b, w_up_a, w_up_b, w_out, g_post,
           _ntiles=NT, _debug=False, _cores=8, _stop=None):
    f = np.float32
    x = np.asarray(x, f)
    key = (_ntiles, _debug, _stop)
    if key not in _CACHE:
        _CACHE[key] = _build(_ntiles, _debug, _stop)
    nc = _CACHE[key]
    smalls = np.zeros((128, 12), f)
    smalls[:, 0:8] = np.asarray(g_pre, f)[0].reshape(8, 128).T
    smalls[:, 8] = np.asarray(g_sub_a, f)[0]
    smalls[:, 9] = np.asarray(g_sub_b, f)[0]
    smalls[:, 10:12] = np.asarray(b_alpha, f)[0].reshape(2, 128).T
    lamv = np.concatenate([np.asarray(v, f)[0] for v in (lam_q1, lam_k1, lam_q2, lam_k2)])[None, :].repeat(128, 0)
    shared = {
        "w_in": np.ascontiguousarray(np.asarray(w_in, f)[0]),
        "w_up_a": np.ascontiguousarray(np.asarray(w_up_a, f)[0]),
        "w_up_b": np.ascontiguousarray(np.asarray(w_up_b, f)[0]),
        "w_out": np.ascontiguousarray(np.asarray(w_out, f)[0]),
        "w_alpha": np.ascontiguousarray(np.asarray(w_alpha, f)[0]),
        "smalls": smalls,
        "lamv": np.ascontiguousarray(lamv),
        "gpost": np.ascontiguousarray(np.asarray(g_post, f)[0][None, :].repeat(128, 0)),
    }
    shared.update(_host_consts())
    in_maps = [dict(shared, x=np.ascontiguousarray(x[b])) for b in range(_cores)]
    res = run_bass_kernel_spmd(nc, in_maps, core_ids=list(range(_cores)))
    if _debug:
        return res
    return np.stack([res.results[b]["out"] for b in range(_cores)], axis=0).astype(np.float32)
```
